# Optimizing a Trainium2 kernel written in Bass

```python
import math
import jax, jax.numpy as jnp
from jax import lax
import numpy as np

D_MODEL = 1024
BATCH = 4
SEQ = 8192
DEPTH = 4

N_MIXERS = 2
GDN_HEADS = 8
GDN_HEAD_DIM = 128
GDN_DIM = GDN_HEADS * GDN_HEAD_DIM
GDN_CONV = 4
GDN_CHUNK = 64
S5_DIM = D_MODEL
S5_GROUP = 16
S5_GROUPS = S5_DIM // S5_GROUP
S5_STATE = 64
S5_CHUNK = 128
XA_HEADS = 4
XA_HEAD_DIM = 128
XA_DIM = XA_HEADS * XA_HEAD_DIM
MEM_LEN = 256
D_FF = 4 * D_MODEL
MIX_DIM = GDN_DIM + XA_DIM
GDN_IN = 4 * GDN_DIM + 2 * GDN_HEADS + XA_DIM
S5_IN = S5_DIM + XA_DIM
DN_ALPHA = (2 * DEPTH) ** 0.25
DN_BETA = (8 * DEPTH) ** -0.25
LN_EPS = 1e-5
RMS_EPS = 1e-6
N_GDN_LAYERS = (DEPTH + N_MIXERS - 1) // N_MIXERS
N_S5_LAYERS = DEPTH // N_MIXERS

kernel_name = "hybrid_gdn_s5_memxattn_deepnorm"


def layer_norm(x, g, b):
    xf = x.astype(jnp.float32)
    mu = jnp.mean(xf, axis=-1, keepdims=True)
    var = jnp.mean(jnp.square(xf - mu), axis=-1, keepdims=True)
    return ((xf - mu) * lax.rsqrt(var + LN_EPS) * g.astype(jnp.float32) + b.astype(jnp.float32)).astype(x.dtype)


def l2_normalize(t):
    return t * lax.rsqrt(jnp.sum(jnp.square(t), axis=-1, keepdims=True) + 1e-6)


def causal_depthwise_conv(u, w):
    k_width = w.shape[0]
    length = u.shape[1]
    up = jnp.pad(u, ((0, 0), (k_width - 1, 0), (0, 0)))
    out = up[:, 0:length] * w[0]
    for k in range(1, k_width):
        out = out + up[:, k:k + length] * w[k]
    return out


def gated_delta_rule_chunked(q, k, v, g, beta):
    bsz, length, heads, dk = q.shape
    dv = v.shape[-1]
    c = GDN_CHUNK
    n = length // c

    def to_chunks(t):
        return t.reshape(bsz, n, c, heads, -1).transpose(0, 3, 1, 2, 4)

    q, k, v = to_chunks(q), to_chunks(k), to_chunks(v)
    g = g.reshape(bsz, n, c, heads).transpose(0, 3, 1, 2)
    beta = beta.reshape(bsz, n, c, heads).transpose(0, 3, 1, 2)
    gc = jnp.cumsum(g, axis=-1)

    causal = jnp.tril(jnp.ones((c, c), dtype=bool))
    strict = jnp.tril(jnp.ones((c, c), dtype=bool), k=-1)
    diff = gc[..., :, None] - gc[..., None, :]
    decay = jnp.where(causal, jnp.exp(jnp.where(causal, diff, 0.0)), 0.0)

    kb = k * beta[..., None]
    vb = v * beta[..., None]
    a_strict = jnp.where(strict, jnp.einsum('bhnid,bhnjd->bhnij', kb, k) * decay, 0.0)
    eye = jnp.eye(c, dtype=q.dtype)
    rhs = jnp.concatenate([vb, kb * jnp.exp(gc)[..., None]], axis=-1)
    sol = lax.linalg.triangular_solve(eye + a_strict, rhs, left_side=True, lower=True,
                                      unit_diagonal=True)
    u_blk, w_blk = sol[..., :dv], sol[..., dv:]

    def mv(t):
        return jnp.moveaxis(t, 2, 0)

    def step(state, inp):
        qi, ki, ui, wi, gci, deci = inp
        v_new = ui - jnp.einsum('bhcd,bhde->bhce', wi, state)
        attn = jnp.einsum('bhid,bhjd->bhij', qi, ki) * deci
        o = (jnp.einsum('bhcd,bhde->bhce', qi * jnp.exp(gci)[..., None], state)
             + jnp.einsum('bhij,bhje->bhie', attn, v_new))
        g_last = gci[..., -1]
        k_dec = ki * jnp.exp(g_last[..., None] - gci)[..., None]
        state = state * jnp.exp(g_last)[..., None, None] + jnp.einsum('bhcd,bhce->bhde', k_dec, v_new)
        return state, o

    s0 = jnp.zeros((bsz, heads, dk, dv), dtype=q.dtype)
    _, out = lax.scan(step, s0, (mv(q), mv(k), mv(u_blk), mv(w_blk), mv(gc), mv(decay)))
    return out.transpose(1, 0, 3, 2, 4).reshape(bsz, length, heads, dv)


def gdn_mixer(x, w_in, conv_w, a_log, dt_bias, norm_g):
    bsz, length, _ = x.shape
    f32 = jnp.float32
    proj = x @ w_in
    qkv, z, b_logit, a_logit, xq = jnp.split(
        proj, [3 * GDN_DIM, 4 * GDN_DIM, 4 * GDN_DIM + GDN_HEADS, 4 * GDN_DIM + 2 * GDN_HEADS], axis=-1)
    qkv = jax.nn.silu(causal_depthwise_conv(qkv.astype(f32), conv_w.astype(f32)))
    q, k, v = [t.reshape(bsz, length, GDN_HEADS, GDN_HEAD_DIM) for t in jnp.split(qkv, 3, axis=-1)]
    q = l2_normalize(q) * (GDN_HEAD_DIM ** -0.5)
    k = l2_normalize(k)
    beta = jax.nn.sigmoid(b_logit.astype(f32))
    g = -jnp.exp(a_log.astype(f32)) * jax.nn.softplus(a_logit.astype(f32) + dt_bias.astype(f32))
    o = gated_delta_rule_chunked(q, k, v, g, beta)
    o = o * lax.rsqrt(jnp.mean(jnp.square(o), axis=-1, keepdims=True) + RMS_EPS) * norm_g.astype(f32)
    o = o * jax.nn.silu(z.astype(f32).reshape(bsz, length, GDN_HEADS, GDN_HEAD_DIM))
    return o.reshape(bsz, length, GDN_DIM).astype(x.dtype), xq


def s5_scan(u, abar_re, abar_im, bbar_re, bbar_im, c_re, c_im):
    bsz, length, _ = u.shape
    n = length // S5_CHUNK
    uc = u.reshape(bsz, n, S5_CHUNK, S5_GROUPS, S5_GROUP).transpose(1, 0, 2, 3, 4)
    a_re = jnp.broadcast_to(abar_re, (S5_CHUNK, 1, S5_GROUPS, S5_STATE))
    a_im = jnp.broadcast_to(abar_im, (S5_CHUNK, 1, S5_GROUPS, S5_STATE))

    def combine(e1, e2):
        a1r, a1i, b1r, b1i = e1
        a2r, a2i, b2r, b2i = e2
        return (a2r * a1r - a2i * a1i, a2r * a1i + a2i * a1r,
                a2r * b1r - a2i * b1i + b2r, a2r * b1i + a2i * b1r + b2i)

    def step(h, u_blk):
        h_re, h_im = h
        bu_re = jnp.einsum('bcgi,gpi->cbgp', u_blk, bbar_re)
        bu_im = jnp.einsum('bcgi,gpi->cbgp', u_blk, bbar_im)
        pr, pi, sr, si = lax.associative_scan(combine, (a_re, a_im, bu_re, bu_im), axis=0)
        st_re = sr + pr * h_re - pi * h_im
        st_im = si + pr * h_im + pi * h_re
        y = jnp.einsum('cbgp,gip->bcgi', st_re, c_re) - jnp.einsum('cbgp,gip->bcgi', st_im, c_im)
        return (st_re[-1], st_im[-1]), y

    h0 = jnp.zeros((bsz, S5_GROUPS, S5_STATE), dtype=u.dtype)
    _, ys = lax.scan(step, (h0, h0), uc)
    return ys.transpose(1, 0, 2, 3, 4).reshape(bsz, length, S5_DIM)


def s5_mixer(x, w_in, a_re, a_im, b_re, b_im, c_re, c_im, log_dt, d_skip, w_glu, b_glu):
    f32 = jnp.float32
    proj = x @ w_in
    u, xq = jnp.split(proj, [S5_DIM], axis=-1)
    u = u.astype(f32)
    a_re, a_im = a_re.astype(f32), a_im.astype(f32)
    b_re, b_im = b_re.astype(f32), b_im.astype(f32)
    dt = jnp.exp(log_dt.astype(f32))[:, None]
    mag = jnp.exp(a_re * dt)
    abar_re = mag * jnp.cos(a_im * dt)
    abar_im = mag * jnp.sin(a_im * dt)
    den = jnp.square(a_re) + jnp.square(a_im)
    n_re, n_im = abar_re - 1.0, abar_im
    f_re = (n_re * a_re + n_im * a_im) / den
    f_im = (n_im * a_re - n_re * a_im) / den
    bbar_re = f_re[..., None] * b_re - f_im[..., None] * b_im
    bbar_im = f_re[..., None] * b_im + f_im[..., None] * b_re
    y = s5_scan(u, abar_re, abar_im, bbar_re, bbar_im, c_re.astype(f32), c_im.astype(f32))
    y = y + d_skip.astype(f32) * u
    zg = jax.nn.gelu(y)
    out = zg * jax.nn.sigmoid(zg @ w_glu.astype(f32) + b_glu.astype(f32))
    return out.astype(x.dtype), xq


def memory_attention(xq, mem, w_kv):
    bsz, length, _ = xq.shape
    q = xq.reshape(bsz, length, XA_HEADS, XA_HEAD_DIM)
    k, v = jnp.split(mem @ w_kv, 2, axis=-1)
    k = k.reshape(bsz, -1, XA_HEADS, XA_HEAD_DIM)
    v = v.reshape(bsz, -1, XA_HEADS, XA_HEAD_DIM)
    s = jnp.einsum('blhd,bmhd->bhlm', q, k).astype(jnp.float32) * (XA_HEAD_DIM ** -0.5)
    p = jax.nn.softmax(s, axis=-1).astype(v.dtype)
    o = jnp.einsum('bhlm,bmhd->blhd', p, v)
    return o.reshape(bsz, length, XA_DIM)


def setup_inputs(seed: int = 0) -> dict:
    key = jax.random.key(seed)
    ks = jax.random.split(key, 32)
    nrm = jax.random.normal
    f32 = jnp.float32
    D = D_MODEL
    inp = {}
    inp["x"] = nrm(ks[0], (BATCH, SEQ, D), f32)
    inp["mem"] = nrm(ks[1], (BATCH, MEM_LEN, D), f32)
    inp["w_kv_mem"] = nrm(ks[2], (DEPTH, D, 2 * XA_DIM), f32) * D ** -0.5
    inp["w_o"] = nrm(ks[3], (DEPTH, MIX_DIM, D), f32) * (MIX_DIM ** -0.5) * DN_BETA
    inp["ln1_g"] = 1.0 + 0.02 * nrm(ks[4], (DEPTH, D), f32)
    inp["ln1_b"] = 0.02 * nrm(ks[5], (DEPTH, D), f32)
    inp["ln2_g"] = 1.0 + 0.02 * nrm(ks[6], (DEPTH, D), f32)
    inp["ln2_b"] = 0.02 * nrm(ks[7], (DEPTH, D), f32)
    inp["mlp_w1"] = nrm(ks[8], (DEPTH, D, D_FF), f32) * D ** -0.5
    inp["mlp_w2"] = nrm(ks[9], (DEPTH, D_FF, D), f32) * (D_FF ** -0.5) * DN_BETA
    inp["gdn_w_in"] = nrm(ks[10], (N_GDN_LAYERS, D, GDN_IN), f32) * D ** -0.5
    inp["gdn_conv_w"] = nrm(ks[11], (N_GDN_LAYERS, GDN_CONV, 3 * GDN_DIM), f32) * GDN_CONV ** -0.5
    inp["gdn_a_log"] = jnp.log(jax.random.uniform(ks[12], (N_GDN_LAYERS, GDN_HEADS), f32, 1.0, 16.0))
    dt0 = jnp.exp(jax.random.uniform(ks[13], (N_GDN_LAYERS, GDN_HEADS), f32, math.log(1e-3), math.log(1e-1)))
    inp["gdn_dt_bias"] = dt0 + jnp.log(-jnp.expm1(-dt0))
    inp["gdn_norm_g"] = 1.0 + 0.02 * nrm(ks[14], (N_GDN_LAYERS, GDN_HEAD_DIM), f32)
    inp["s5_w_in"] = nrm(ks[15], (N_S5_LAYERS, D, S5_IN), f32) * D ** -0.5
    sh = (N_S5_LAYERS, S5_GROUPS, S5_STATE)
    inp["s5_a_re"] = -0.5 + 0.01 * nrm(ks[16], sh, f32)
    inp["s5_a_im"] = math.pi * jnp.arange(S5_STATE, dtype=f32) + 0.01 * nrm(ks[17], sh, f32)
    inp["s5_b_re"] = nrm(ks[18], sh + (S5_GROUP,), f32) * (2 * S5_GROUP) ** -0.5
    inp["s5_b_im"] = nrm(ks[19], sh + (S5_GROUP,), f32) * (2 * S5_GROUP) ** -0.5
    shc = (N_S5_LAYERS, S5_GROUPS, S5_GROUP, S5_STATE)
    inp["s5_c_re"] = nrm(ks[20], shc, f32) * (2 * S5_STATE) ** -0.5 * 4.0
    inp["s5_c_im"] = nrm(ks[21], shc, f32) * (2 * S5_STATE) ** -0.5 * 4.0
    inp["s5_log_dt"] = jax.random.uniform(ks[22], (N_S5_LAYERS, S5_GROUPS), f32, math.log(1e-3), math.log(1e-1))
    inp["s5_d"] = nrm(ks[23], (N_S5_LAYERS, S5_DIM), f32)
    inp["s5_w_glu"] = nrm(ks[24], (N_S5_LAYERS, S5_DIM, S5_DIM), f32) * S5_DIM ** -0.5
    inp["s5_b_glu"] = 0.02 * nrm(ks[25], (N_S5_LAYERS, S5_DIM), f32)
    return inp


def reference(x, mem, w_kv_mem, w_o, ln1_g, ln1_b, ln2_g, ln2_b, mlp_w1, mlp_w2,
              gdn_w_in, gdn_conv_w, gdn_a_log, gdn_dt_bias, gdn_norm_g,
              s5_w_in, s5_a_re, s5_a_im, s5_b_re, s5_b_im, s5_c_re, s5_c_im,
              s5_log_dt, s5_d, s5_w_glu, s5_b_glu):
    for i in range(DEPTH):
        j = i // N_MIXERS
        if i % N_MIXERS == 0:
            mix, xq = gdn_mixer(x, gdn_w_in[j], gdn_conv_w[j], gdn_a_log[j], gdn_dt_bias[j], gdn_norm_g[j])
        else:
            mix, xq = s5_mixer(x, s5_w_in[j], s5_a_re[j], s5_a_im[j], s5_b_re[j], s5_b_im[j],
                               s5_c_re[j], s5_c_im[j], s5_log_dt[j], s5_d[j], s5_w_glu[j], s5_b_glu[j])
        cross = memory_attention(xq, mem, w_kv_mem[i])
        h = jnp.concatenate([mix, cross], axis=-1) @ w_o[i]
        x = layer_norm(DN_ALPHA * x + h, ln1_g[i], ln1_b[i])
        f = jnp.square(jax.nn.relu(x @ mlp_w1[i])) @ mlp_w2[i]
        x = layer_norm(DN_ALPHA * x + f, ln2_g[i], ln2_b[i])
    return x
```

```python
import math
import numpy as np
import concourse.bass as bass
import concourse.mybir as mybir
from concourse.bass_utils import run_bass_kernel_spmd
from contextlib import ExitStack

F32 = mybir.dt.float32
BF16 = mybir.dt.bfloat16
AF = mybir.ActivationFunctionType
ALU = mybir.AluOpType

D = 1024
NCH = 8
T = 512
NB = T // 128
DEPTH = 4
MEM = 256
DFF = 4096
GDN_IN = 4624
S5_IN = 1536
DN_ALPHA = (2 * DEPTH) ** 0.25
LN_EPS = 1e-5
RMS_EPS = 1e-6
BIG = 60000.0
TWO_PI = 2.0 * math.pi

ENGS = ("pe", "act", "dve", "pool", "sp")
SAME_SYNC = True
NDS = 8
OPLIMIT = 10 ** 9


class Res:
    __slots__ = ("name", "w", "r", "excl")

    def __init__(self, name, excl=False):
        self.name = name
        self.w = None
        self.r = {}
        self.excl = excl


class _Op:
    __slots__ = ("fn", "waits", "need_inc", "seq", "dma")

    def __init__(self, fn, dma=None):
        self.fn = fn
        self.waits = []
        self.need_inc = False
        self.seq = 0
        self.dma = dma


class Prog:
    def __init__(self, nc, es):
        self.nc = nc
        self.ops = {e: [] for e in ENGS}
        self.waited = {e: {} for e in ENGS}
        self.sems = {e: es.enter_context(nc.semaphore("s_" + e)) for e in ENGS}
        self.dsems = {}
        for q in ("sp", "pool", "act"):
            for i in range(NDS):
                self.dsems[("d", q, i)] = es.enter_context(nc.semaphore(f"d_{q}{i}"))
        self.dma_cnt = {q: 0 for q in ENGS}
        self.dma_val = {}
        self.pending = None
        self.pend_done = {e: None for e in ENGS}
        self.nrec = 0
        self.limit = OPLIMIT
        self.log = []

    def barrier(self):
        if self.nrec > self.limit:
            return
        toks = []
        for e in ENGS:
            idx = len(self.ops[e]) - 1
            while idx >= 0 and self.ops[e][idx].dma is not None:
                idx -= 1
            if idx >= 0:
                toks.append(("o", e, idx, self.ops[e][idx]))
        for key, val in self.dma_val.items():
            toks.append(("d", key, val))
        self.pending = toks

    def _deps(self, R, W, eng=None):
        deps = []
        for r in R:
            if r.w is not None:
                deps.append(r.w)
            if r.excl:
                for k_, t_ in r.r.items():
                    if k_ != eng:
                        deps.append(t_)
        for w in W:
            if w.w is not None:
                deps.append(w.w)
            deps.extend(w.r.values())
        return deps

    def _add(self, eng, op, deps):
        wd = self.waited[eng]
        my_idx = len(self.ops[eng])
        if self.pending is not None and self.pend_done[eng] is not self.pending:
            self.pend_done[eng] = self.pending
            deps = list(deps) + self.pending
        best = {}
        for tok in deps:
            if tok[0] == "o":
                F, idx = tok[1], tok[2]
                if F == eng and (eng == "pe" or not SAME_SYNC) and op.dma is None:
                    continue
                if F not in best or best[F][2] < idx:
                    best[F] = tok
            else:
                key, val = tok[1], tok[2]
                if key not in best or best[key][2] < val:
                    best[key] = tok
        for k_, tok in best.items():
            if tok[0] == "o":
                F, idx = tok[1], tok[2]
                if wd.get(F, -1) >= idx:
                    continue
                wd[F] = idx
                tok[3].need_inc = True
                op.waits.append(tok)
            else:
                key, val = tok[1], tok[2]
                if wd.get(key, 0) >= val:
                    continue
                wd[key] = val
                op.waits.append(tok)
        self.ops[eng].append(op)
        return my_idx

    def op(self, eng, fn, R=(), W=()):
        self.nrec += 1
        if self.nrec > self.limit:
            return None
        import sys as _s
        f = _s._getframe(1) if OPLIMIT < 10 ** 9 else None
        if f is None:
            f = type("F", (), {"f_code": type("C", (), {"co_filename": "", "co_name": ""})(), "f_lineno": 0})()
        while f.f_code.co_filename == __file__ and f.f_code.co_name in ("mm", "tr", "act", "ts", "tt", "stt", "scan", "copy", "memset", "recip", "reduce", "dense", "op"):
            f = f.f_back
        self.log.append((self.nrec, eng, f.f_lineno))
        op = _Op(fn)
        deps = self._deps(R, W, eng)
        idx = self._add(eng, op, deps)
        tok = ("o", eng, idx, op)
        for w in W:
            w.w = tok
            w.r = {}
        for r in R:
            r.r[eng] = tok
        return op

    def dma(self, out, in_, R=(), W=(), q="sp"):
        self.nrec += 1
        if self.nrec > self.limit:
            return None
        i = self.dma_cnt[q] % NDS
        self.dma_cnt[q] += 1
        key = ("d", q, i)
        prev = self.dma_val.get(key, 0)
        val = prev + 16
        self.dma_val[key] = val
        deps = self._deps(R, W)
        if prev > 0:
            deps.append(("d", key, prev))
        op = _Op(lambda e: e.dma_start(out=out, in_=in_), dma=(key, val))
        self._add(q, op, deps)
        tok = ("d", key, val)
        for w in W:
            w.w = tok
            w.r = {}
        for r in R:
            r.r[key] = tok
        return op

    def mm(self, out, lhsT, rhs, start=True, stop=True, R=(), W=()):
        return self.op("pe", lambda e: e.matmul(out, lhsT=lhsT, rhs=rhs, start=start, stop=stop), R, W)

    def tr(self, out, in_, ident, R=(), W=()):
        return self.op("pe", lambda e: e.transpose(out, in_, ident), R, W)

    def act(self, out, in_, func, bias=None, scale=None, accum=None, R=(), W=(), eng="act"):
        kw = {}
        if bias is not None:
            kw["bias"] = bias
        if scale is not None:
            kw["scale"] = scale
        if accum is not None:
            kw["accum_out"] = accum
        if func == AF.Copy and kw:
            func = AF.Identity
        return self.op(eng, lambda e: e.activation(out, in_, func, **kw), R, W)

    def ts(self, out, in0, s1, s2, op0, op1=None, R=(), W=(), eng="dve", accum=None):
        kw = {}
        if accum is not None:
            kw["accum_out"] = accum
        if op1 is None:
            return self.op(eng, lambda e: e.tensor_scalar(out, in0, s1, None, op0, **kw), R, W)
        return self.op(eng, lambda e: e.tensor_scalar(out, in0, s1, s2, op0, op1, **kw), R, W)

    def tt(self, out, in0, in1, op, R=(), W=(), eng="dve"):
        return self.op(eng, lambda e: e.tensor_tensor(out, in0, in1, op), R, W)

    def stt(self, out, in0, scalar, in1, op0, op1, R=(), W=(), eng="dve"):
        return self.op(eng, lambda e: e.scalar_tensor_tensor(out, in0, scalar, in1, op0, op1), R, W)

    def scan(self, out, d0, d1, init, op0, op1, R=(), W=()):
        return self.op("dve", lambda e: e.tensor_tensor_scan(out, d0, d1, init, op0, op1), R, W)

    def copy(self, out, in_, R=(), W=(), eng="dve"):
        if eng == "act":
            return self.op("act", lambda e: e.copy(out, in_), R, W)
        return self.op(eng, lambda e: e.tensor_copy(out, in_), R, W)

    def memset(self, ap, val, W=(), eng="dve"):
        return self.op(eng, lambda e: e.memset(ap, val), (), W)

    def recip(self, out, in_, R=(), W=()):
        return self.op("dve", lambda e: e.reciprocal(out, in_), R, W)

    def reduce(self, out, in_, op, R=(), W=()):
        return self.op("dve", lambda e: e.tensor_reduce(out, in_, mybir.AxisListType.X, op), R, W)

    def emit(self, block):
        nc = self.nc
        for e in ENGS:
            s = 0
            for op in self.ops[e]:
                if op.need_inc:
                    s += 1
                    op.seq = s
        fin = _Op(lambda e: e.nop())
        for key, val in self.dma_val.items():
            if self.waited["sp"].get(key, 0) < val:
                fin.waits.append(("d", key, val))
        self.ops["sp"].append(fin)

        def run(eng_name):
            def body(e):
                for op in self.ops[eng_name]:
                    for tok in op.waits:
                        if tok[0] == "o":
                            e.wait_ge(self.sems[tok[1]], tok[3].seq)
                        else:
                            e.wait_ge(self.dsems[tok[1]], tok[2])
                    ins = op.fn(e)
                    if op.dma is not None:
                        ins.then_inc(self.dsems[op.dma[0]], 16)
                    elif op.need_inc:
                        ins.then_inc(self.sems[eng_name], 1)
            return body

        block.tensor(run("pe"))
        block.scalar(run("act"))
        block.vector(run("dve"))
        block.gpsimd(run("pool"))
        block.sync(run("sp"))


class Arena:
    def __init__(self, nc, es, nbytes):
        self.t = es.enter_context(nc.sbuf_tensor("arena", [128, nbytes // 4], F32))
        self.off = 0
        self.cap = nbytes
        self.peak = 0

    def alloc(self, shape, dt):
        esz = 2 if dt == BF16 else 4
        n = 1
        for s in shape[1:]:
            n *= s
        nb = (n * esz + 31) // 32 * 32
        assert self.off + nb <= self.cap, f"arena overflow {self.off}+{nb}>{self.cap}"
        v = self.t[:, self.off // 4:(self.off + nb) // 4]
        if dt != F32:
            v = v.bitcast(dt)
        v = v[0:shape[0], 0:n]
        if len(shape) == 3:
            v = v.rearrange("p (a b) -> p a b", a=shape[1])
        elif len(shape) == 4:
            v = v.rearrange("p (a b c) -> p a b c", a=shape[1], b=shape[2])
        self.off += nb
        self.peak = max(self.peak, self.off)
        return v

    def mark(self):
        return self.off

    def release(self, m):
        self.off = m


def host_consts():
    r = np.arange(128)[:, None]
    c = np.arange(128)[None, :]
    cs = {}
    cs["c_ident"] = np.eye(128, dtype=np.float32)
    cs["c_ones"] = np.ones((128, 128), np.float32)
    cs["c_triu"] = (r <= c).astype(np.float32)
    cs["c_trils"] = (r > c).astype(np.float32)
    cs["c_mls"] = np.where(r > c, 0.0, -BIG).astype(np.float32)
    cs["c_mus"] = np.where(r < c, 0.0, -BIG).astype(np.float32)
    cs["c_mui"] = np.where(r <= c, 0.0, BIG).astype(np.float32)
    cs["c_nvec"] = np.broadcast_to(np.arange(1, 129, dtype=np.float32)[None, :], (128, 128)).copy()
    return cs


W_NAMES = [
    ("w_kv_mem", [DEPTH, D, 1024]), ("w_o", [DEPTH, 1536, D]),
    ("ln1_g", [DEPTH, D]), ("ln1_b", [DEPTH, D]), ("ln2_g", [DEPTH, D]), ("ln2_b", [DEPTH, D]),
    ("mlp_w1", [DEPTH, D, DFF]), ("mlp_w2", [DEPTH, DFF, D]),
    ("gdn_w_in", [2, D, GDN_IN]), ("gdn_conv_w", [2, 4, 3072]), ("gdn_a_log", [2, 8]),
    ("gdn_dt_bias", [2, 8]), ("gdn_norm_g", [2, 128]),
    ("s5_w_in", [2, D, S5_IN]), ("s5_a_re", [2, 64, 64]), ("s5_a_im", [2, 64, 64]),
    ("s5_b_re", [2, 64, 64, 16]), ("s5_b_im", [2, 64, 64, 16]),
    ("s5_c_re", [2, 64, 16, 64]), ("s5_c_im", [2, 64, 16, 64]),
    ("s5_log_dt", [2, 64]), ("s5_d", [2, D]), ("s5_w_glu", [2, D, D]), ("s5_b_glu", [2, D]),
]


def build_program(L, layers=(0, 1, 2, 3), mixers=True, xattn=True, gdn_slots=2, dbg=False):
    NT = L // T
    nc = bass.Bass("TRN2", target_bir_lowering=False)
    dr = {}
    dr["x"] = nc.dram_tensor("x", [L, D], F32, kind="ExternalInput").ap()
    dr["mem"] = nc.dram_tensor("mem", [MEM, D], F32, kind="ExternalInput").ap()
    for nm, shp in W_NAMES:
        dr[nm] = nc.dram_tensor(nm, shp, F32, kind="ExternalInput").ap()
    for nm, arr in host_consts().items():
        dr[nm] = nc.dram_tensor(nm, list(arr.shape), F32, kind="ExternalInput").ap()
    y = nc.dram_tensor("y", [L, D], F32, kind="ExternalOutput").ap()
    s5c = nc.dram_tensor("s5c", [2, 8, 128, 4 * 4 * 128], BF16).ap()
    s5t = nc.dram_tensor("s5t", [2, 8, 128, 2 * 4 * 128], F32).ap()

    with ExitStack() as es:
        P = Prog(nc, es)
        es.enter_context(nc.allow_non_contiguous_dma("small parameter relayouts"))
        AR = Arena(nc, es, 204 * 1024)
        banks = [es.enter_context(nc.psum_tensor(f"ps{i}", [128, 512], F32)) for i in range(8)]
        PR = [[Res(f"ps{b}", excl=True)] * 4 for b in range(8)]

        def bfview(b):
            return banks[b][:, :].bitcast(BF16)

        def v3(ap, a):
            return ap.rearrange("p (a b) -> p a b", a=a)

        def sin_of(out, in_, shift, tf, ti, shape, RR, WW):
            xs = tf[0]; kf = tf[1]; mm_ = tf[2]
            rw = dict(R=tuple(RR) + tuple(WW), W=tuple(WW))
            P.ts(xs, in_, shift, None, ALU.add, **rw)
            P.ts(kf, xs, 1.0 / TWO_PI, None, ALU.mult, **rw)
            P.copy(ti, kf, **rw)
            P.copy(kf, ti, **rw)
            P.stt(xs, kf, -TWO_PI, xs, ALU.mult, ALU.add, **rw)
            P.ts(mm_, xs, math.pi, TWO_PI, ALU.is_gt, ALU.mult, **rw)
            P.tt(xs, xs, mm_, ALU.subtract, **rw)
            P.ts(mm_, xs, -math.pi, TWO_PI, ALU.is_lt, ALU.mult, **rw)
            P.tt(xs, xs, mm_, ALU.add, **rw)
            P.act(out, xs, AF.Sin, **rw)

        R_c = Res("consts")
        ident_f = AR.alloc([128, 128], F32)
        ident_b = AR.alloc([128, 128], BF16)
        ones_f = AR.alloc([128, 128], F32)
        triu_f = AR.alloc([128, 128], F32)
        trils_f = AR.alloc([128, 128], F32)
        mls_f = AR.alloc([128, 128], F32)
        mus_f = AR.alloc([128, 128], F32)
        mui_f = AR.alloc([128, 128], F32)
        nvec_f = AR.alloc([128, 128], F32)
        ccol = AR.alloc([128, 8], F32)
        lnp = AR.alloc([128, 8, 16], F32)
        convw = AR.alloc([128, 24, 8], F32)
        normg = AR.alloc([128, 2], F32)
        gpar = AR.alloc([128, 32], F32)
        s5dv = AR.alloc([128, 8, 4], F32)
        KT = AR.alloc([128, DEPTH, 4, MEM], BF16)
        VM = AR.alloc([128, DEPTH, 2, 512], BF16)
        R_KT, R_VM = Res("KT"), Res("VM")
        Sst = AR.alloc([128, 2, 8, 128], F32)
        Sbf = AR.alloc([128, 2, 8, 128], BF16)
        R_S = [[Res(f"S{j}_{h}") for h in range(8)] for j in range(2)]
        R_Sb = [[Res(f"Sb{j}_{h}") for h in range(8)] for j in range(2)]
        convst = AR.alloc([128, 2, 24, 4], F32)
        R_cst = [[Res(f"cst{j}_{c}") for c in range(24)] for j in range(2)]
        hst = AR.alloc([128, 2, 32, 2], F32)
        R_hst = [[Res(f"hst{j}_{q}") for q in range(32)] for j in range(2)]
        theta = AR.alloc([128, 2, 32], F32)
        rmag = AR.alloc([128, 2, 32], F32)
        R_s5p = Res("s5p")
        xT = AR.alloc([128, NCH, T], F32)
        xb = AR.alloc([128, NCH, T], BF16)
        R_xT = [Res(f"xT{c}") for c in range(NCH)]
        R_xb = [Res(f"xb{c}") for c in range(NCH)]
        WSLOT = 16 * 1024
        wbufs = [(AR.alloc([128, WSLOT // 2], BF16), Res(f"wbuf{i}")) for i in range(2)]
        wsmall = AR.alloc([128, 8, 16], BF16)
        R_wsmall = Res("wsmall")
        wctr = [0]

        def load_w(src2d, nk, c0, ncols):
            wt, rr = wbufs[wctr[0] % 2]
            wctr[0] += 1
            assert nk * ncols * 2 <= WSLOT
            v = wt[:, 0:nk * ncols].rearrange("p (k n) -> p k n", k=nk)
            src = src2d.rearrange("(k p) n -> p k n", p=128)[:, :, c0:c0 + ncols]
            P.dma(v, src, R=(), W=(rr,), q="pool")
            return v, rr

        for t_, nm in ((ident_f, "c_ident"), (ones_f, "c_ones"), (triu_f, "c_triu"), (trils_f, "c_trils"),
                       (mls_f, "c_mls"), (mus_f, "c_mus"), (mui_f, "c_mui"), (nvec_f, "c_nvec")):
            P.dma(t_, dr[nm], W=(R_c,))
        P.copy(ident_b, ident_f, R=(R_c,), W=(R_c,), eng="dve")
        P.memset(ccol[:, 0:1], -math.pi, W=(R_c,))
        P.memset(ccol[:, 1:2], 1.0, W=(R_c,))
        P.memset(ccol[:, 2:3], LN_EPS, W=(R_c,))
        P.memset(ccol[:, 3:4], 1e-6, W=(R_c,))
        P.memset(ccol[:, 4:5], 128.0 * RMS_EPS, W=(R_c,))
        UMARK = AR.mark()
        rows16 = AR.alloc([16, D], F32); rows8 = AR.alloc([8, 3072], F32); rows4 = AR.alloc([4, D], F32)
        rows2 = AR.alloc([2, 128], F32); rowg = AR.alloc([1, 32], F32)
        R_rows = Res("rows")
        for ki, nm in enumerate(("ln1_g", "ln1_b", "ln2_g", "ln2_b")):
            P.dma(rows16[ki * 4:(ki + 1) * 4, :], dr[nm], W=(R_rows,))
        P.dma(rows8, dr["gdn_conv_w"].rearrange("j k n -> (j k) n"), W=(R_rows,))
        P.dma(rows4[0:2, :], dr["s5_d"], W=(R_rows,))
        P.dma(rows4[2:4, :], dr["s5_b_glu"], W=(R_rows,))
        P.dma(rows2, dr["gdn_norm_g"], W=(R_rows,))
        P.dma(rowg[:, 0:16], dr["gdn_a_log"].rearrange("(o j) h -> o (j h)", o=1), W=(R_rows,))
        P.dma(rowg[:, 16:32], dr["gdn_dt_bias"].rearrange("(o j) h -> o (j h)", o=1), W=(R_rows,))
        for c in range(8):
            P.tr(banks[0][:, c * 16:(c + 1) * 16], rows16[:, c * 128:(c + 1) * 128], ident_f[0:16, 0:16], R=(R_rows, R_c), W=(PR[0][0],))
            P.tr(banks[1][:, c * 4:(c + 1) * 4], rows4[:, c * 128:(c + 1) * 128], ident_f[0:4, 0:4], R=(R_rows, R_c), W=(PR[1][0],))
        for g in range(24):
            P.tr(banks[2][:, g * 8:(g + 1) * 8], rows8[:, g * 128:(g + 1) * 128], ident_f[0:8, 0:8], R=(R_rows, R_c), W=(PR[2][0], PR[2][1]))
        P.tr(banks[3][:, 0:2], rows2, ident_f[0:2, 0:2], R=(R_rows, R_c), W=(PR[3][0],))
        P.mm(banks[3][:, 128:160], ones_f[0:1, :], rowg, R=(R_rows, R_c), W=(PR[3][1],))
        P.copy(lnp.rearrange("p c k -> p (c k)"), banks[0][:, 0:128], R=(PR[0][0],), W=(R_c,), eng="act")
        P.copy(s5dv.rearrange("p c k -> p (c k)"), banks[1][:, 0:32], R=(PR[1][0],), W=(R_c,), eng="act")
        P.copy(convw.rearrange("p g k -> p (g k)"), banks[2][:, 0:192], R=(PR[2][0], PR[2][1]), W=(R_c,), eng="act")
        P.act(normg, banks[3][:, 0:2], AF.Identity, scale=math.sqrt(128.0), R=(PR[3][0],), W=(R_c,))
        P.act(gpar[:, 0:16], banks[3][:, 128:144], AF.Exp, R=(PR[3][1],), W=(R_c,))
        P.ts(gpar[:, 0:16], gpar[:, 0:16], -1.0, None, ALU.mult, R=(R_c,), W=(R_c,))
        P.copy(gpar[:, 16:32], banks[3][:, 144:160], R=(PR[3][1],), W=(R_c,), eng="act")
        P.barrier()
        AR.release(UMARK)
        P.memset(Sst, 0.0, W=[r for rr in R_S for r in rr])
        P.memset(Sbf, 0.0, W=[r for rr in R_Sb for r in rr], eng="pool")
        P.memset(convst, 0.0, W=[r for rr in R_cst for r in rr])
        P.memset(hst, 0.0, W=[r for rr in R_hst for r in rr])

        if xattn:
            memtok = AR.alloc([128, 2, D], F32)
            memT = AR.alloc([128, NCH, MEM], BF16)
            R_mt, R_mT = Res("memtok"), Res("memT")
            P.dma(memtok, dr["mem"].rearrange("(b p) f -> p b f", p=128), W=(R_mt,))
            for c in range(NCH):
                for mb in range(2):
                    P.tr(banks[0][:, mb * 128:(mb + 1) * 128], memtok[:, mb, c * 128:(c + 1) * 128], ident_f,
                         R=(R_mt, R_c), W=(PR[0][mb],))
                P.copy(memT[:, c, :], banks[0][:, 0:256], R=(PR[0][0], PR[0][1]), W=(R_mT,), eng="act")
            for l in layers:
                wv, wr = load_w(dr["w_kv_mem"][l], 8, 0, 1024)
                for h in range(4):
                    for k in range(8):
                        P.mm(banks[1][:, 0:256], wv[:, k, h * 128:(h + 1) * 128], memT[:, k, :],
                             start=(k == 0), stop=(k == 7), R=(wr, R_mT), W=(PR[1][0], PR[1][1]))
                    P.copy(KT[:, l, h, :], banks[1][:, 0:256], R=(PR[1][0], PR[1][1]), W=(R_KT,), eng="act")
                for mb in range(2):
                    for k in range(8):
                        P.mm(banks[2][:, :], memT[:, k, mb * 128:(mb + 1) * 128], wv[:, k, 512:1024],
                             start=(k == 0), stop=(k == 7), R=(wr, R_mT), W=PR[2])
                    P.copy(VM[:, l, mb, :], banks[2][:, :], R=PR[2], W=(R_VM,), eng="act")
            P.barrier()
            AR.release(UMARK)

        s5_layers = [l for l in layers if l % 2 == 1]
        R_s5c = Res("s5c_dram")
        if mixers and s5_layers:
            A2 = AR.alloc([32, 2, 128], F32)
            LD = AR.alloc([32, 2], F32)
            LDb = AR.alloc([32, 128], F32)
            are = AR.alloc([128, 32], F32); aim = AR.alloc([128, 32], F32); dtt = AR.alloc([128, 32], F32)
            t1 = AR.alloc([128, 32], F32); t2 = AR.alloc([128, 32], F32); t3 = AR.alloc([128, 32], F32)
            abr = AR.alloc([128, 32], F32); abi = AR.alloc([128, 32], F32)
            fre = AR.alloc([128, 32], F32); fim = AR.alloc([128, 32], F32)
            bre = AR.alloc([128, 32, 16], F32); bim = AR.alloc([128, 32, 16], F32)
            bbr = AR.alloc([128, 32, 16], F32); bbi = AR.alloc([128, 32, 16], F32); tb16 = AR.alloc([128, 32, 16], F32)
            Ci = AR.alloc([16, 2, 64, 64], F32)
            cre = AR.alloc([128, 32, 16], F32); cim = AR.alloc([128, 32, 16], F32)
            Z = AR.alloc([128, 4, 2, 128], F32)
            CB = AR.alloc([128, 4, 4, 128], BF16)
            R_p, R_Z, R_CB = Res("s5prep"), Res("Z"), Res("CB")
            rr1 = AR.alloc([128, 32], F32); rr2 = AR.alloc([128, 32], F32); rr3 = AR.alloc([128, 32], F32)
            rri = AR.alloc([128, 32], mybir.dt.int32)
            argt = AR.alloc([128, 4, 128], F32); tabt = AR.alloc([128, 2, 4, 128], F32)
            ra1 = AR.alloc([128, 4, 128], F32); ra2 = AR.alloc([128, 4, 128], F32); ra3 = AR.alloc([128, 4, 128], F32)
            rai = AR.alloc([128, 4, 128], mybir.dt.int32)
            RW = dict(R=(R_p,), W=(R_p,))
            for l in s5_layers:
                j = l // 2
                P.dma(A2[:, 0, :], dr["s5_a_re"][j].rearrange("(q g) p -> q (g p)", g=2), W=(R_p,))
                P.dma(A2[:, 1, :], dr["s5_a_im"][j].rearrange("(q g) p -> q (g p)", g=2), W=(R_p,))
                P.dma(LD, dr["s5_log_dt"][j].rearrange("(q g) -> q g", g=2), W=(R_p,))
                for g2 in range(2):
                    ps_ = slice(g2 * 64, (g2 + 1) * 64)
                    for qh in range(2):
                        qs_ = slice(qh * 16, (qh + 1) * 16)
                        P.dma(bre[ps_, qs_, :], dr["s5_b_re"][j].rearrange("(q g) p i -> g p q i", g=2)[g2][:, qs_, :], W=(R_p,))
                        P.dma(bim[ps_, qs_, :], dr["s5_b_im"][j].rearrange("(q g) p i -> g p q i", g=2)[g2][:, qs_, :], W=(R_p,))
                P.dma(Ci[:, 0, :, :], dr["s5_c_re"][j].rearrange("g i p -> i g p"), W=(R_p,))
                P.dma(Ci[:, 1, :, :], dr["s5_c_im"][j].rearrange("g i p -> i g p"), W=(R_p,))
                for ri, dst in ((0, are), (1, aim)):
                    P.tr(banks[4][:, ri * 32:(ri + 1) * 32], A2[:, ri, :], ident_f[0:32, 0:32], R=(R_p, R_c), W=(PR[4][0],))
                    P.copy(dst, banks[4][:, ri * 32:(ri + 1) * 32], R=(PR[4][0],), W=(R_p,), eng="act")
                P.copy(v3(LDb, 2), LD.unsqueeze(2).broadcast_to([32, 2, 64]), **RW)
                P.tr(banks[4][:, 64:96], LDb, ident_f[0:32, 0:32], R=(R_p, R_c), W=(PR[4][0],))
                P.copy(dtt, banks[4][:, 64:96], R=(PR[4][0],), W=(R_p,), eng="act")
                for ri, dst in ((0, cre), (1, cim)):
                    for q in range(32):
                        P.tr(banks[5 + ri][:, q * 16:(q + 1) * 16], Ci[:, ri, 2 * q:2 * q + 2, :].rearrange("i g p -> i (g p)"),
                             ident_f[0:16, 0:16], R=(R_p, R_c), W=(PR[5 + ri][q // 8],))
                    P.copy(dst.rearrange("p q i -> p (q i)"), banks[5 + ri][:, :], R=PR[5 + ri], W=(R_p,), eng="act")
                P.act(dtt, dtt, AF.Exp, **RW)
                P.tt(t1, are, dtt, ALU.mult, **RW)
                P.act(rmag[:, j, :], t1, AF.Exp, R=(R_p,), W=(R_p, R_s5p))
                P.tt(theta[:, j, :], aim, dtt, ALU.mult, R=(R_p,), W=(R_p, R_s5p))
                sin_of(t2, theta[:, j, :], 0.0, (rr1, rr2, rr3), rri, None, (R_s5p,), (R_p,))
                sin_of(t3, theta[:, j, :], 0.5 * math.pi, (rr1, rr2, rr3), rri, None, (R_s5p,), (R_p,))
                for c in range(8):
                    P.tt(argt, theta[:, j, 4 * c:4 * c + 4].unsqueeze(2).broadcast_to([128, 4, 128]),
                         nvec_f.unsqueeze(1).broadcast_to([128, 4, 128]), ALU.mult, R=(R_s5p, R_c, R_p), W=(R_p,))
                    sin_of(tabt[:, 0, :, :], argt, 0.5 * math.pi, (ra1, ra2, ra3), rai, None, (R_s5p,), (R_p,))
                    sin_of(tabt[:, 1, :, :], argt, 0.0, (ra1, ra2, ra3), rai, None, (R_s5p,), (R_p,))
                    P.dma(s5t[j, c].rearrange("p (k q n) -> p k q n", k=2, q=4), tabt, R=(R_p,), W=(R_s5c,))
                P.tt(abr, rmag[:, j, :], t3, ALU.mult, R=(R_p, R_s5p), W=(R_p,))
                P.tt(abi, rmag[:, j, :], t2, ALU.mult, R=(R_p, R_s5p), W=(R_p,))
                P.ts(abr, abr, -1.0, None, ALU.add, **RW)
                P.tt(t1, are, are, ALU.mult, **RW)
                P.tt(t2, aim, aim, ALU.mult, **RW)
                P.tt(t1, t1, t2, ALU.add, **RW)
                P.recip(t1, t1, **RW)
                P.tt(t2, abr, are, ALU.mult, **RW)
                P.tt(t3, abi, aim, ALU.mult, **RW)
                P.tt(t2, t2, t3, ALU.add, **RW)
                P.tt(fre, t2, t1, ALU.mult, **RW)
                P.tt(t2, abi, are, ALU.mult, **RW)
                P.tt(t3, abr, aim, ALU.mult, **RW)
                P.tt(t2, t2, t3, ALU.subtract, **RW)
                P.tt(fim, t2, t1, ALU.mult, **RW)
                fre_b = fre.unsqueeze(2).broadcast_to([128, 32, 16])
                fim_b = fim.unsqueeze(2).broadcast_to([128, 32, 16])
                P.tt(bbr, bre, fre_b, ALU.mult, **RW)
                P.tt(tb16, bim, fim_b, ALU.mult, **RW)
                P.tt(bbr, bbr, tb16, ALU.subtract, **RW)
                P.tt(bbi, bim, fre_b, ALU.mult, **RW)
                P.tt(tb16, bre, fim_b, ALU.mult, **RW)
                P.tt(bbi, bbi, tb16, ALU.add, **RW)
                P.ts(cim, cim, -1.0, None, ALU.mult, **RW)
                for c in range(8):
                    P.memset(Z, 0.0, W=(R_Z,))
                    P.memset(CB, 0.0, W=(R_CB,), eng="pool")
                    for qq in range(4):
                        q = 4 * c + qq
                        for g2 in range(2):
                            ps_ = slice(g2 * 64, (g2 + 1) * 64)
                            gl = 2 * qq + g2
                            cs_ = slice(gl * 16, gl * 16 + 16)
                            P.copy(Z[ps_, qq, 0, cs_], bbr[ps_, q, :], R=(R_p,), W=(R_Z,))
                            P.copy(Z[ps_, qq, 1, cs_], bbi[ps_, q, :], R=(R_p,), W=(R_Z,))
                            P.copy(CB[ps_, qq, 2, cs_], cre[ps_, q, :], R=(R_p,), W=(R_CB,), eng="pool")
                            P.copy(CB[ps_, qq, 3, cs_], cim[ps_, q, :], R=(R_p,), W=(R_CB,), eng="pool")
                    for qq in range(4):
                        for ri in range(2):
                            P.tr(banks[7][:, ri * 128:(ri + 1) * 128], Z[:, qq, ri, :], ident_f, R=(R_Z, R_c), W=(PR[7][ri],))
                        for ri in range(2):
                            P.copy(CB[:, qq, ri, :], banks[7][:, ri * 128:(ri + 1) * 128], R=(PR[7][ri],), W=(R_CB,), eng="act")
                    P.dma(s5c[j, c].rearrange("p (q k n) -> p q k n", q=4, k=4), CB, R=(R_CB,), W=(R_s5c,))
            P.barrier()
            AR.release(UMARK)

        pctr = [0]

        def next_bank(avoid=()):
            while True:
                b = pctr[0] % 8
                pctr[0] += 1
                if b not in avoid:
                    return b

        def dense(wv, wr, col0, nk, rhs_fn, rhs_res, b):
            for k in range(nk):
                P.mm(banks[b][:, :], wv[:, k, col0:col0 + 128], rhs_fn(k), start=(k == 0), stop=(k == nk - 1),
                     R=(wr,) + tuple(rhs_res(k)), W=PR[b])

        def xb_fn(k):
            return xb[:, k, :]

        def xb_res(k):
            return (R_xb[k],)

        def layer_norm(l, which):
            gi, bi = (0, 1) if which == 1 else (2, 3)
            m0 = AR.mark()
            sq = [AR.alloc([128, T], F32) for _ in range(2)]
            R_sq = [Res("lnsq0"), Res("lnsq1")]
            mean = AR.alloc([128, T], F32); rstd = AR.alloc([128, T], F32); tmp = AR.alloc([128, T], F32)
            R_m, R_r, R_t = Res("lnmean"), Res("lnrstd"), Res("lntmp")
            bs = next_bank()
            bq = next_bank()
            for c in range(NCH):
                P.mm(banks[bs][:, :], ones_f, xT[:, c, :], start=(c == 0), stop=(c == NCH - 1), R=(R_c, R_xT[c]), W=PR[bs])
            for c in range(NCH):
                P.act(sq[c % 2], xT[:, c, :], AF.Square, R=(R_xT[c],), W=(R_sq[c % 2],))
                P.mm(banks[bq][:, :], ones_f, sq[c % 2], start=(c == 0), stop=(c == NCH - 1), R=(R_c, R_sq[c % 2]), W=PR[bq])
            P.act(mean, banks[bs][:, :], AF.Copy, scale=1.0 / D, R=PR[bs], W=(R_m,))
            P.tt(tmp, mean, mean, ALU.mult, R=(R_m,), W=(R_t,))
            P.stt(rstd, banks[bq][:, :], 1.0 / D, tmp, ALU.mult, ALU.subtract, R=PR[bq] + [R_t], W=(R_r,))
            P.act(rstd, rstd, AF.Sqrt, bias=ccol[:, 2:3], R=(R_r, R_c), W=(R_r,))
            P.recip(rstd, rstd, R=(R_r,), W=(R_r,))
            for c in range(NCH):
                P.tt(tmp, xT[:, c, :], mean, ALU.subtract, R=(R_xT[c], R_m), W=(R_t,))
                P.tt(tmp, tmp, rstd, ALU.mult, R=(R_t, R_r), W=(R_t,))
                P.act(xT[:, c, :], tmp, AF.Identity, bias=lnp[:, c, bi * 4 + l:bi * 4 + l + 1], scale=lnp[:, c, gi * 4 + l:gi * 4 + l + 1],
                      R=(R_t, R_c), W=(R_xT[c],))
                P.copy(xb[:, c, :], xT[:, c, :], R=(R_xT[c],), W=(R_xb[c],), eng="pool")
            P.barrier()
            AR.release(m0)

        def residual_evac(b, c):
            P.stt(xT[:, c, :], xT[:, c, :], DN_ALPHA, banks[b][:, :], ALU.mult, ALU.add, R=[R_xT[c]] + PR[b], W=(R_xT[c],))

        def cross_attention(l, xq_b, R_xq, mix_b, R_mix):
            sc = 128.0 ** -0.5
            Pf = [AR.alloc([128, MEM], F32) for _ in range(2)]
            Pn = [AR.alloc([128, MEM], BF16) for _ in range(2)]
            PT = [AR.alloc([128, 2, 128], BF16) for _ in range(2)]
            st = [AR.alloc([128, 4], F32) for _ in range(2)]
            R_Pf = [Res("Pf0"), Res("Pf1")]; R_Pn = [Res("Pn0"), Res("Pn1")]; R_PT = [Res("PT0"), Res("PT1")]
            R_st = [Res("st0"), Res("st1")]
            it = 0
            for h in range(4):
                bo = next_bank()
                for tb in range(NB):
                    s = it % 2
                    it += 1
                    bsc = next_bank(avoid=(bo,))
                    tsl = slice(tb * 128, (tb + 1) * 128)
                    P.mm(banks[bsc][:, 0:MEM], xq_b[:, h, tsl], KT[:, l, h, :], R=(R_xq, R_KT), W=(PR[bsc][0], PR[bsc][1]))
                    P.reduce(st[s][:, 0:1], banks[bsc][:, 0:MEM], ALU.max, R=(PR[bsc][0], PR[bsc][1]), W=(R_st[s],))
                    P.ts(st[s][:, 1:2], st[s][:, 0:1], -sc, None, ALU.mult, R=(R_st[s],), W=(R_st[s],))
                    P.act(Pf[s], banks[bsc][:, 0:MEM], AF.Exp, bias=st[s][:, 1:2], scale=sc,
                          R=(PR[bsc][0], PR[bsc][1], R_st[s]), W=(R_Pf[s],))
                    P.reduce(st[s][:, 2:3], Pf[s], ALU.add, R=(R_Pf[s],), W=(R_st[s],))
                    P.recip(st[s][:, 3:4], st[s][:, 2:3], R=(R_st[s],), W=(R_st[s],))
                    P.act(Pn[s], Pf[s], AF.Copy, scale=st[s][:, 3:4], R=(R_Pf[s], R_st[s]), W=(R_Pn[s],))
                    bt = next_bank(avoid=(bo,))
                    btb = bfview(bt)
                    for mb in range(2):
                        P.tr(btb[:, mb * 128:(mb + 1) * 128], Pn[s][:, mb * 128:(mb + 1) * 128], ident_b,
                             R=(R_Pn[s], R_c), W=(PR[bt][0],))
                    P.copy(PT[s].rearrange("p a b -> p (a b)"), btb[:, 0:256], R=(PR[bt][0],), W=(R_PT[s],), eng="act")
                    for mb in range(2):
                        P.mm(banks[bo][:, tsl], VM[:, l, mb, h * 128:(h + 1) * 128], PT[s][:, mb, :],
                             start=(mb == 0), stop=(mb == 1), R=(R_VM, R_PT[s]), W=(PR[bo][tb],))
                P.copy(mix_b[:, 8 + h, :], banks[bo][:, :], R=PR[bo], W=(R_mix[8 + h],), eng="act")

        def out_proj_ln_mlp(l, mix_b, R_mix):
            for grp in range(2):
                wv, wr = load_w(dr["w_o"][l], 12, grp * 512, 512)
                for oc in range(4):
                    b = next_bank()
                    dense(wv, wr, oc * 128, 12, lambda k: mix_b[:, k, :], lambda k: (R_mix[k],), b)
                    residual_evac(b, grp * 4 + oc)
            P.barrier()
            AR.release(TMARK)
            layer_norm(l, 1)
            hb = AR.alloc([128, 32, T], BF16)
            R_h = [Res(f"h{i}") for i in range(32)]
            rl = [AR.alloc([128, T], F32) for _ in range(2)]
            R_rl = [Res("rl0"), Res("rl1")]
            for grp in range(4):
                wv, wr = load_w(dr["mlp_w1"][l], 8, grp * 1024, 1024)
                for oc in range(8):
                    b = next_bank()
                    g = grp * 8 + oc
                    dense(wv, wr, oc * 128, 8, xb_fn, xb_res, b)
                    P.act(rl[g % 2], banks[b][:, :], AF.Relu, R=PR[b], W=(R_rl[g % 2],))
                    P.tt(hb[:, g, :], rl[g % 2], rl[g % 2], ALU.mult, R=(R_rl[g % 2],), W=(R_h[g],), eng="dve")
            for grp in range(4):
                wv, wr = load_w(dr["mlp_w2"][l], 32, grp * 256, 256)
                for oc in range(2):
                    b = next_bank()
                    dense(wv, wr, oc * 128, 32, lambda k: hb[:, k, :], lambda k: (R_h[k],), b)
                    residual_evac(b, grp * 2 + oc)
            P.barrier()
            AR.release(TMARK)
            layer_norm(l, 2)

        def s5_layer(l, j, mix_b, R_mix, xq_b, R_xq):
            W2d = dr["s5_w_in"][j]
            zg_f = AR.alloc([128, NCH, T], F32); zg_b = AR.alloc([128, NCH, T], BF16)
            R_zf = [Res(f"zgf{c}") for c in range(NCH)]; R_zb = [Res(f"zgb{c}") for c in range(NCH)]
            u_f = [AR.alloc([128, T], F32) for _ in range(2)]; u_b = [AR.alloc([128, T], BF16) for _ in range(2)]
            R_u = [Res("u0"), Res("u1")]
            cbuf = [AR.alloc([128, 4, 4, 128], BF16) for _ in range(2)]; R_cb = [Res("cb0"), Res("cb1")]
            tabs = [AR.alloc([128, 2, 4, 128], F32) for _ in range(2)]
            Ctab = [t_[:, 0, :, :] for t_ in tabs]; Stab = [t_[:, 1, :, :] for t_ in tabs]
            nSl = [AR.alloc([128, 4], F32) for _ in range(2)]
            R_tab = [Res("tab0"), Res("tab1")]
            NS = 2
            ta = [AR.alloc([128, T], F32) for _ in range(NS)]; tbb = [AR.alloc([128, T], F32) for _ in range(NS)]
            Wr = [AR.alloc([128, T], F32) for _ in range(NS)]; Wi = [AR.alloc([128, T], F32) for _ in range(NS)]
            gr = [AR.alloc([128, T], F32) for _ in range(NS)]; gi = [AR.alloc([128, T], F32) for _ in range(NS)]
            hrb = [AR.alloc([128, T], BF16) for _ in range(NS)]; hib = [AR.alloc([128, T], BF16) for _ in range(NS)]
            cr = [AR.alloc([128, 4, 4], F32) for _ in range(NS)]
            R_ta = [Res(f"ta{i}") for i in range(NS)]; R_tb = [Res(f"tb{i}") for i in range(NS)]
            R_W = [Res(f"W{i}") for i in range(NS)]; R_g = [Res(f"g{i}") for i in range(NS)]
            R_hb = [Res(f"hb{i}") for i in range(NS)]; R_cr = [Res(f"cr{i}") for i in range(NS)]
            yd = AR.alloc([128, T], F32); x2 = AR.alloc([128, T], F32); sg = AR.alloc([128, T], F32)
            R_yd, R_x2, R_sg = Res("yd"), Res("x2"), Res("sg")
            wv_u, wr_u = load_w(W2d, 8, 0, 1024)
            pc = 0
            for c in range(NCH):
                s = c % 2
                bu = 2 + s
                bY = s
                dense(wv_u, wr_u, c * 128, 8, xb_fn, xb_res, bu)
                P.copy(u_f[s], banks[bu][:, :], R=PR[bu], W=(R_u[s],), eng="act")
                P.copy(u_b[s], u_f[s], R=(R_u[s],), W=(R_u[s],), eng="pool")
                P.dma(cbuf[s], s5c[j, c].rearrange("p (q k n) -> p q k n", q=4, k=4), R=(R_s5c,), W=(R_cb[s],))
                P.dma(tabs[s], s5t[j, c].rearrange("p (k q n) -> p k q n", k=2, q=4), R=(R_s5c,), W=(R_tab[s],))
                P.ts(nSl[s], Stab[s][:, :, 127], -1.0, None, ALU.mult, R=(R_tab[s],), W=(R_tab[s],))
                for qq in range(4):
                    q = 4 * c + qq
                    z = pc % NS
                    pc += 1
                    bA, bB = (4, 5) if z == 0 else (6, 7)
                    P.mm(banks[bA][:, :], cbuf[s][:, qq, 0, :], u_b[s], R=(R_cb[s], R_u[s]), W=PR[bA])
                    P.mm(banks[bB][:, :], cbuf[s][:, qq, 1, :], u_b[s], R=(R_cb[s], R_u[s]), W=PR[bB])
                    Cb = Ctab[s][:, qq, :].unsqueeze(1).broadcast_to([128, 4, 128])
                    Sb_ = Stab[s][:, qq, :].unsqueeze(1).broadcast_to([128, 4, 128])
                    pA, pB = v3(banks[bA][:, :], 4), v3(banks[bB][:, :], 4)
                    P.tt(v3(ta[z], 4), pA, Cb, ALU.mult, R=PR[bA] + [R_tab[s]], W=(R_ta[z],))
                    P.tt(v3(tbb[z], 4), pB, Sb_, ALU.mult, R=PR[bB] + [R_tab[s]], W=(R_tb[z],))
                    P.tt(Wr[z], ta[z], tbb[z], ALU.add, R=(R_ta[z], R_tb[z]), W=(R_W[z],), eng="pool")
                    P.tt(v3(ta[z], 4), pB, Cb, ALU.mult, R=PR[bB] + [R_tab[s]], W=(R_ta[z],))
                    P.tt(v3(tbb[z], 4), pA, Sb_, ALU.mult, R=PR[bA] + [R_tab[s]], W=(R_tb[z],))
                    P.tt(Wi[z], ta[z], tbb[z], ALU.subtract, R=(R_ta[z], R_tb[z]), W=(R_W[z],), eng="pool")
                    rdec = rmag[:, j, q:q + 1].broadcast_to([128, 128])
                    c_l = Ctab[s][:, qq, 127:128]; s_l = Stab[s][:, qq, 127:128]; ns_l = nSl[s][:, qq:qq + 1]
                    for blk in range(NB):
                        bs_ = slice(blk * 128, (blk + 1) * 128)
                        if blk == 0:
                            ir, ii, Rin = hst[:, j, q, 0:1], hst[:, j, q, 1:2], R_hst[j][q]
                        else:
                            ir, ii, Rin = cr[z][:, blk - 1, 0:1], cr[z][:, blk - 1, 1:2], R_cr[z]
                        P.scan(gr[z][:, bs_], rdec, Wr[z][:, bs_], ir, ALU.mult, ALU.add, R=(R_s5p, R_W[z], Rin), W=(R_g[z],))
                        P.scan(gi[z][:, bs_], rdec, Wi[z][:, bs_], ii, ALU.mult, ALU.add, R=(R_s5p, R_W[z], Rin), W=(R_g[z],))
                        gr_l = gr[z][:, blk * 128 + 127:blk * 128 + 128]
                        gi_l = gi[z][:, blk * 128 + 127:blk * 128 + 128]
                        if blk < NB - 1:
                            dr_, di_, Rout = cr[z][:, blk, 0:1], cr[z][:, blk, 1:2], R_cr[z]
                        else:
                            dr_, di_, Rout = hst[:, j, q, 0:1], hst[:, j, q, 1:2], R_hst[j][q]
                        P.ts(cr[z][:, blk, 2:3], gr_l, c_l, None, ALU.mult, R=(R_g[z], R_tab[s]), W=(R_cr[z],))
                        P.ts(cr[z][:, blk, 3:4], gi_l, c_l, None, ALU.mult, R=(R_g[z], R_tab[s]), W=(R_cr[z],))
                        P.stt(dr_, gi_l, ns_l, cr[z][:, blk, 2:3], ALU.mult, ALU.add, R=(R_g[z], R_tab[s], R_cr[z]), W=(Rout,))
                        P.stt(di_, gr_l, s_l, cr[z][:, blk, 3:4], ALU.mult, ALU.add, R=(R_g[z], R_tab[s], R_cr[z]), W=(Rout,))
                    P.tt(v3(ta[z], 4), v3(gr[z], 4), Cb, ALU.mult, R=(R_g[z], R_tab[s]), W=(R_ta[z],), eng="pool")
                    P.tt(v3(tbb[z], 4), v3(gi[z], 4), Sb_, ALU.mult, R=(R_g[z], R_tab[s]), W=(R_tb[z],), eng="pool")
                    P.tt(hrb[z], ta[z], tbb[z], ALU.subtract, R=(R_ta[z], R_tb[z]), W=(R_hb[z],), eng="pool")
                    P.tt(v3(ta[z], 4), v3(gi[z], 4), Cb, ALU.mult, R=(R_g[z], R_tab[s]), W=(R_ta[z],), eng="pool")
                    P.tt(v3(tbb[z], 4), v3(gr[z], 4), Sb_, ALU.mult, R=(R_g[z], R_tab[s]), W=(R_tb[z],), eng="pool")
                    P.tt(hib[z], ta[z], tbb[z], ALU.add, R=(R_ta[z], R_tb[z]), W=(R_hb[z],), eng="pool")
                    P.mm(banks[bY][:, :], cbuf[s][:, qq, 2, :], hrb[z], start=(qq == 0), stop=False, R=(R_cb[s], R_hb[z]), W=PR[bY])
                    P.mm(banks[bY][:, :], cbuf[s][:, qq, 3, :], hib[z], start=False, stop=(qq == 3), R=(R_cb[s], R_hb[z]), W=PR[bY])
                P.stt(yd, u_f[s], s5dv[:, c, j:j + 1], banks[bY][:, :], ALU.mult, ALU.add, R=[R_u[s], R_c] + PR[bY], W=(R_yd,))
                P.act(x2, yd, AF.Square, R=(R_yd,), W=(R_x2,))
                P.ts(x2, x2, 0.044715, 1.0, ALU.mult, ALU.add, R=(R_x2,), W=(R_x2,))
                P.tt(x2, x2, yd, ALU.mult, R=(R_x2, R_yd), W=(R_x2,))
                P.act(sg, x2, AF.Sigmoid, scale=2.0 * math.sqrt(2.0 / math.pi), R=(R_x2,), W=(R_sg,))
                P.tt(zg_f[:, c, :], yd, sg, ALU.mult, R=(R_yd, R_sg), W=(R_zf[c],))
                P.copy(zg_b[:, c, :], zg_f[:, c, :], R=(R_zf[c],), W=(R_zb[c],), eng="pool")
            wv, wr = load_w(W2d, 8, 1024, 512)
            for h in range(4):
                b = 2 + (h % 2)
                dense(wv, wr, h * 128, 8, xb_fn, xb_res, b)
                P.copy(xq_b[:, h, :], banks[b][:, :], R=PR[b], W=(R_xq,), eng="act")
            wv, wr = load_w(dr["s5_w_glu"][j], 8, 0, 1024)
            for oc in range(NCH):
                b = 4 + (oc % 4)
                dense(wv, wr, oc * 128, 8, lambda k: zg_b[:, k, :], lambda k: (R_zb[k],), b)
                P.act(sg, banks[b][:, :], AF.Sigmoid, bias=s5dv[:, oc, 2 + j:3 + j], R=PR[b] + [R_c], W=(R_sg,))
                P.tt(mix_b[:, oc, :], zg_f[:, oc, :], sg, ALU.mult, R=(R_zf[oc], R_sg), W=(R_mix[oc],))

        def gdn_layer(l, j, mix_b, R_mix, xq_b, R_xq):
            W2d = dr["gdn_w_in"][j]
            qkv_b = AR.alloc([128, 24, T], BF16); R_qkv = [Res(f"qkv{i}") for i in range(24)]
            zs = AR.alloc([128, 8, T], BF16); R_zs = [Res(f"zs{i}") for i in range(8)]
            oT = AR.alloc([128, 8, T], F32); R_oT = [Res(f"oT{i}") for i in range(8)]
            stage = [AR.alloc([128, T + 4], F32) for _ in range(2)]; R_stg = [Res("stg0"), Res("stg1")]
            acc = [AR.alloc([128, T], F32) for _ in range(2)]; R_acc = [Res("acc0"), Res("acc1")]
            sl = [AR.alloc([128, T], F32) for _ in range(2)]; R_sl = [Res("sl0"), Res("sl1")]
            sq = AR.alloc([128, T], F32); R_sq = Res("gsq")
            rs = AR.alloc([128, T], F32); R_rs = Res("grs")
            for grp in range(3):
                wv, wr = load_w(W2d, 8, grp * 1024, 1024)
                for oc in range(8):
                    g = grp * 8 + oc
                    s = g % 2
                    b = 4 + (g % 4)
                    dense(wv, wr, oc * 128, 8, xb_fn, xb_res, b)
                    P.copy(stage[s][:, 0:3], convst[:, j, g, 0:3], R=(R_cst[j][g],), W=(R_stg[s],), eng="pool")
                    P.copy(stage[s][:, 3:3 + T], banks[b][:, :], R=PR[b], W=(R_stg[s],), eng="act")
                    P.ts(acc[s], stage[s][:, 0:T], convw[:, g, j * 4:j * 4 + 1], None, ALU.mult, R=(R_stg[s], R_c), W=(R_acc[s],))
                    for k in range(1, 4):
                        P.stt(acc[s], stage[s][:, k:k + T], convw[:, g, j * 4 + k:j * 4 + k + 1], acc[s], ALU.mult, ALU.add,
                              R=(R_stg[s], R_c, R_acc[s]), W=(R_acc[s],))
                    P.copy(convst[:, j, g, 0:3], stage[s][:, T:T + 3], R=(R_stg[s],), W=(R_cst[j][g],), eng="pool")
                    if g >= 16:
                        P.act(qkv_b[:, g, :], acc[s], AF.Silu, R=(R_acc[s],), W=(R_qkv[g],))
                    else:
                        P.act(sl[s], acc[s], AF.Silu, R=(R_acc[s],), W=(R_sl[s],))
                        P.act(sq, sl[s], AF.Square, R=(R_sl[s],), W=(R_sq,))
                        bq = g % 2
                        P.mm(banks[bq][:, :], ones_f, sq, R=(R_c, R_sq), W=PR[bq])
                        P.act(rs, banks[bq][:, :], AF.Sqrt, bias=ccol[:, 3:4], R=PR[bq] + [R_c], W=(R_rs,))
                        P.recip(rs, rs, R=(R_rs,), W=(R_rs,))
                        P.stt(qkv_b[:, g, :], sl[s], (128.0 ** -0.5 if g < 8 else 1.0), rs, ALU.mult, ALU.mult,
                              R=(R_sl[s], R_rs), W=(R_qkv[g],))
            wv, wr = load_w(W2d, 8, 3072, 1024)
            for oc in range(8):
                b = 4 + (oc % 4)
                dense(wv, wr, oc * 128, 8, xb_fn, xb_res, b)
                P.act(zs[:, oc, :], banks[b][:, :], AF.Silu, R=PR[b], W=(R_zs[oc],))
            wv, wr = load_w(W2d, 8, 4112, 512)
            for h in range(4):
                b = 4 + (h % 4)
                dense(wv, wr, h * 128, 8, xb_fn, xb_res, b)
                P.copy(xq_b[:, h, :], banks[b][:, :], R=PR[b], W=(R_xq,), eng="act")
            P.dma(wsmall, W2d.rearrange("(k p) n -> p k n", p=128)[:, :, 4096:4112], W=(R_wsmall,), q="pool")
            for tb in range(NB):
                for k in range(8):
                    P.mm(banks[2][:, tb * 16:(tb + 1) * 16], xb[:, k, tb * 128:(tb + 1) * 128], wsmall[:, k, :],
                         start=(k == 0), stop=(k == 7), R=(R_xb[k], R_wsmall), W=(PR[2][0],))
            sm = AR.alloc([128, 12, NB, 8], F32)
            R_sm = Res("gsm")
            bet, lnb, apre, spv, gg, gc, ngc, gcb, be, kd, egl = [sm[:, i, :, :] for i in range(11)]
            ba = banks[2][:, 0:NB * 16].rearrange("p (t k h) -> p t k h", t=NB, k=2)
            RWs = dict(R=(R_sm, R_c), W=(R_sm,))
            P.act(bet, ba[:, :, 0, :], AF.Sigmoid, R=(PR[2][0],), W=(R_sm,))
            P.act(lnb, bet, AF.Ln, **RWs)
            P.tt(apre, ba[:, :, 1, :], gpar[:, 16 + j * 8:24 + j * 8].unsqueeze(1).broadcast_to([128, NB, 8]), ALU.add, R=(PR[2][0], R_c), W=(R_sm,))
            P.act(spv, apre, AF.Exp, **RWs)
            P.act(spv, spv, AF.Ln, bias=ccol[:, 1:2], **RWs)
            P.tt(gg, spv, gpar[:, j * 8:j * 8 + 8].unsqueeze(1).broadcast_to([128, NB, 8]), ALU.mult, **RWs)
            for tb in range(NB):
                P.mm(banks[3][:, tb * 8:(tb + 1) * 8], triu_f, gg[:, tb, :], R=(R_c, R_sm), W=(PR[3][0],))
                P.mm(banks[3][:, 32 + tb * 8:32 + (tb + 1) * 8], trils_f, gg[:, tb, :], R=(R_c, R_sm), W=(PR[3][0],))
                P.mm(banks[3][:, 64 + tb * 8:64 + (tb + 1) * 8], ones_f, gg[:, tb, :], R=(R_c, R_sm), W=(PR[3][0],))
            p3 = lambda o: banks[3][:, o:o + NB * 8].rearrange("p (t h) -> p t h", t=NB)
            P.copy(gc, p3(0), R=(PR[3][0],), W=(R_sm,), eng="act")
            P.ts(ngc, p3(0), -1.0, None, ALU.mult, R=(PR[3][0],), W=(R_sm,))
            P.tt(gcb, gc, lnb, ALU.add, **RWs)
            P.act(be, gcb, AF.Exp, **RWs)
            P.act(kd, p3(32), AF.Exp, R=(PR[3][0],), W=(R_sm,))
            P.act(egl, p3(64), AF.Exp, R=(PR[3][0],), W=(R_sm,))
            NSL = gdn_slots
            def mk(dt):
                return [AR.alloc([128, 128], dt) for _ in range(NSL)]
            Kbe, Kd_, Vb, attnT, qs, TT, nWmT, vn = [mk(BF16) for _ in range(8)]
            ndg, bdg, DA, DB, EG, A0, A1, B0, B1, Q0, Q1 = [mk(F32) for _ in range(11)]
            names = ["Kbe", "Kd", "Vb", "attnT", "qs", "TT", "nWmT", "vn", "ndg", "bdg", "DA", "DB", "EG", "A0", "A1", "B0", "B1", "Q0", "Q1"]
            RT = [{n: Res(f"{n}{z}") for n in names} for z in range(NSL)]
            it = 0
            for tb in range(NB):
                tsl = slice(tb * 128, (tb + 1) * 128)
                for h in range(8):
                    z = it % NSL
                    it += 1
                    r = RT[z]
                    Y0, Y1 = 2 * z, 2 * z + 1
                    qT, kT, vT = qkv_b[:, h, tsl], qkv_b[:, 8 + h, tsl], qkv_b[:, 16 + h, tsl]
                    Rq, Rk, Rv = R_qkv[h], R_qkv[8 + h], R_qkv[16 + h]
                    col = lambda t_: t_[:, tb, h:h + 1]
                    y0b = bfview(Y0)
                    reg = lambda b_, q_: banks[b_][:, q_ * 128:(q_ + 1) * 128]
                    P.tr(y0b[:, 0:128], kT, ident_b, R=(Rk, R_c), W=(PR[Y0][0],))
                    P.tr(y0b[:, 128:256], vT, ident_b, R=(Rv, R_c), W=(PR[Y0][0],))
                    P.act(Kbe[z], y0b[:, 0:128], AF.Copy, scale=col(be), R=(PR[Y0][0], R_sm), W=(r["Kbe"],))
                    P.act(Kd_[z], y0b[:, 0:128], AF.Copy, scale=col(kd), R=(PR[Y0][0], R_sm), W=(r["Kd"],))
                    P.act(Vb[z], y0b[:, 128:256], AF.Copy, scale=col(bet), R=(PR[Y0][0], R_sm), W=(r["Vb"],))
                    P.mm(reg(Y0, 1), kT, kT, R=(Rk,), W=(PR[Y0][1],))
                    P.mm(reg(Y0, 2), kT, qT, R=(Rk, Rq), W=(PR[Y0][2],))
                    P.ts(ndg[z], ident_f, col(ngc), None, ALU.mult, R=(R_c, R_sm), W=(r["ndg"],))
                    P.ts(bdg[z], ident_f, col(gcb), None, ALU.mult, R=(R_c, R_sm), W=(r["bdg"],))
                    P.mm(reg(Y1, 0), ones_f, ndg[z], start=True, stop=False, R=(R_c, r["ndg"]), W=(PR[Y1][0],))
                    P.mm(reg(Y1, 0), ident_f, mls_f, start=False, stop=True, R=(R_c,), W=(PR[Y1][0],))
                    P.mm(reg(Y1, 1), ones_f, bdg[z], start=True, stop=False, R=(R_c, r["bdg"]), W=(PR[Y1][1],))
                    P.mm(reg(Y1, 1), ident_f, mus_f, start=False, stop=True, R=(R_c,), W=(PR[Y1][1],))
                    P.mm(reg(Y1, 2), ones_f, ndg[z], start=True, stop=False, R=(R_c, r["ndg"]), W=(PR[Y1][2],))
                    P.mm(reg(Y1, 2), ident_f, mui_f, start=False, stop=True, R=(R_c,), W=(PR[Y1][2],))
                    P.mm(reg(Y0, 3), ones_f, ndg[z], R=(R_c, r["ndg"]), W=(PR[Y0][3],))
                    P.act(DA[z], reg(Y1, 0), AF.Exp, bias=col(gcb), R=(PR[Y1][0], R_sm), W=(r["DA"],))
                    P.act(DB[z], reg(Y1, 1), AF.Exp, bias=col(ngc), R=(PR[Y1][1], R_sm), W=(r["DB"],))
                    P.act(EG[z], reg(Y0, 3), AF.Exp, scale=-1.0, R=(PR[Y0][3],), W=(r["EG"],))
                    P.tt(A0[z], reg(Y0, 1), DA[z], ALU.mult, R=(PR[Y0][1], r["DA"]), W=(r["A0"],))
                    P.tt(B0[z], reg(Y0, 1), DB[z], ALU.mult, R=(PR[Y0][1], r["DB"]), W=(r["B0"],))
                    P.tt(Q0[z], ident_f, B0[z], ALU.subtract, R=(R_c, r["B0"]), W=(r["Q0"],))
                    P.act(DA[z], reg(Y1, 2), AF.Exp, bias=col(ngc), scale=-1.0, R=(PR[Y1][2], R_sm, r["A0"]), W=(r["DA"],))
                    P.tt(DB[z], reg(Y0, 2), DA[z], ALU.mult, R=(PR[Y0][2], r["DA"], r["B0"]), W=(r["DB"],))
                    P.copy(attnT[z], DB[z], R=(r["DB"],), W=(r["attnT"],), eng="pool")
                    P.tt(qs[z], qT, EG[z], ALU.mult, R=(Rq, r["EG"]), W=(r["qs"],), eng="pool")
                    Ac, Bc, Qc = (A0, A1), (B0, B1), (Q0, Q1)
                    An, Bn, Qn = ("A0", "A1"), ("B0", "B1"), ("Q0", "Q1")
                    for lev in range(6):
                        ci, ni = lev % 2, (lev + 1) % 2
                        P.mm(reg(Y1, 0), Bc[ci][z], Ac[ci][z], R=(r[Bn[ci]], r[An[ci]]), W=(PR[Y1][0],))
                        if lev < 5:
                            P.mm(reg(Y1, 1), Ac[ci][z], Bc[ci][z], R=(r[Bn[ci]], r[An[ci]]), W=(PR[Y1][1],))
                        P.copy(Ac[ni][z], reg(Y1, 0), R=(PR[Y1][0],), W=(r[An[ni]],), eng="act")
                        if lev < 5:
                            P.copy(Bc[ni][z], reg(Y1, 1), R=(PR[Y1][1],), W=(r[Bn[ni]],), eng="dve")
                        P.mm(reg(Y1, 2), Ac[ni][z], Qc[ci][z], R=(r[An[ni]], r[Qn[ci]]), W=(PR[Y1][2],))
                        P.tt(Qc[ni][z], Qc[ci][z], reg(Y1, 2), ALU.add, R=(r[Qn[ci]], PR[Y1][2]), W=(r[Qn[ni]],))
                        if lev == 5:
                            P.copy(TT[z], Qc[ni][z], R=(r[Qn[ni]],), W=(r["TT"],), eng="pool")
                    P.mm(reg(Y0, 0), Kbe[z], TT[z], R=(r["Kbe"], r["TT"]), W=(PR[Y0][0],))
                    P.act(nWmT[z], reg(Y0, 0), AF.Copy, scale=-1.0, R=(PR[Y0][0],), W=(r["nWmT"],))
                    P.mm(reg(Y0, 1), TT[z], Vb[z], start=True, stop=False, R=(r["TT"], r["Vb"]), W=(PR[Y0][1],))
                    P.mm(reg(Y0, 1), nWmT[z], Sbf[:, j, h, :], start=False, stop=True, R=(r["nWmT"], R_Sb[j][h]), W=(PR[Y0][1],))
                    P.copy(vn[z], reg(Y0, 1), R=(PR[Y0][1],), W=(r["vn"],), eng="act")
                    P.mm(reg(Y0, 3), Sbf[:, j, h, :], qs[z], start=True, stop=False, R=(R_Sb[j][h], r["qs"]), W=(PR[Y0][3],))
                    P.mm(reg(Y0, 3), vn[z], attnT[z], start=False, stop=True, R=(r["vn"], r["attnT"]), W=(PR[Y0][3],))
                    P.copy(oT[:, h, tsl], reg(Y0, 3), R=(PR[Y0][3],), W=(R_oT[h],), eng="act")
                    P.mm(reg(Y0, 2), Kd_[z], vn[z], R=(r["Kd"], r["vn"]), W=(PR[Y0][2],))
                    P.stt(Sst[:, j, h, :], Sst[:, j, h, :], col(egl), reg(Y0, 2), ALU.mult, ALU.add,
                          R=(R_S[j][h], R_sm, PR[Y0][2]), W=(R_S[j][h],))
                    P.copy(Sbf[:, j, h, :], Sst[:, j, h, :], R=(R_S[j][h],), W=(R_Sb[j][h],), eng="pool")
            for h in range(8):
                b = 4 + (h % 4)
                P.act(sq, oT[:, h, :], AF.Square, R=(R_oT[h],), W=(R_sq,))
                P.mm(banks[b][:, :], ones_f, sq, R=(R_c, R_sq), W=PR[b])
                P.act(rs, banks[b][:, :], AF.Sqrt, bias=ccol[:, 4:5], R=PR[b] + [R_c], W=(R_rs,))
                P.recip(rs, rs, R=(R_rs,), W=(R_rs,))
                P.tt(rs, rs, oT[:, h, :], ALU.mult, R=(R_rs, R_oT[h]), W=(R_rs,))
                P.stt(mix_b[:, h, :], rs, normg[:, j:j + 1], zs[:, h, :], ALU.mult, ALU.mult, R=(R_rs, R_c, R_zs[h]), W=(R_mix[h],))

        TMARK = UMARK
        for ti in range(NT):
            t0 = ti * T
            xin = AR.alloc([128, NB, D], F32)
            R_xin = Res("xin")
            P.dma(xin, dr["x"][t0:t0 + T, :].rearrange("(b p) f -> p b f", p=128), W=(R_xin,))
            for c in range(NCH):
                b = next_bank()
                for tb in range(NB):
                    P.tr(banks[b][:, tb * 128:(tb + 1) * 128], xin[:, tb, c * 128:(c + 1) * 128], ident_f,
                         R=(R_xin, R_c), W=(PR[b][tb],))
                P.copy(xT[:, c, :], banks[b][:, :], R=PR[b], W=(R_xT[c],), eng="act")
                P.copy(xb[:, c, :], banks[b][:, :], R=PR[b], W=(R_xb[c],), eng="act")
            P.barrier()
            AR.release(TMARK)

            for l in layers:
                j = l // 2
                mix_b = AR.alloc([128, 12, T], BF16)
                R_mix = [Res(f"mix{i}") for i in range(12)]
                xq_b = AR.alloc([128, 4, T], BF16)
                R_xq = Res("xq")
                if not mixers:
                    for h in range(8):
                        P.memset(mix_b[:, h, :], 0.0, W=(R_mix[h],), eng="pool")
                    if xattn:
                        W2d = dr["gdn_w_in"][j] if l % 2 == 0 else dr["s5_w_in"][j]
                        wv, wr = load_w(W2d, 8, 4112 if l % 2 == 0 else 1024, 512)
                        for h in range(4):
                            b = next_bank()
                            dense(wv, wr, h * 128, 8, xb_fn, xb_res, b)
                            P.copy(xq_b[:, h, :], banks[b][:, :], R=PR[b], W=(R_xq,), eng="act")
                elif l % 2 == 0:
                    gdn_layer(l, j, mix_b, R_mix, xq_b, R_xq)
                else:
                    s5_layer(l, j, mix_b, R_mix, xq_b, R_xq)
                if xattn:
                    cross_attention(l, xq_b, R_xq, mix_b, R_mix)
                else:
                    for h in range(4):
                        P.memset(mix_b[:, 8 + h, :], 0.0, W=(R_mix[8 + h],), eng="pool")
                out_proj_ln_mlp(l, mix_b, R_mix)

            xo = AR.alloc([128, NB, D], F32)
            R_xo = Res("xo")
            for tb in range(NB):
                for half in range(2):
                    b = next_bank()
                    for cc in range(4):
                        c = half * 4 + cc
                        P.tr(banks[b][:, cc * 128:(cc + 1) * 128], xT[:, c, tb * 128:(tb + 1) * 128], ident_f,
                             R=(R_xT[c], R_c), W=(PR[b][cc],))
                    P.copy(xo[:, tb, half * 512:(half + 1) * 512], banks[b][:, :], R=PR[b], W=(R_xo,),
                           eng=("act" if half == 0 else "dve"))
            P.dma(y[t0:t0 + T, :].rearrange("(b p) f -> p b f", p=128), xo, R=(R_xo,), W=())
            P.barrier()
            AR.release(TMARK)

        block = es.enter_context(nc.Block())
        P.emit(block)
        print("arena peak KB", AR.peak / 1024, "ops", {e: len(P.ops[e]) for e in ENGS}, flush=True)
    return nc


_CACHE = {}


def run_cores(inputs, L, nb, ncores=8, **kw):
    key = (L, tuple(sorted(kw.items())))
    nc = build_program(L, **kw)
    cs = host_consts()
    shared = {nm: np.ascontiguousarray(np.asarray(inputs[nm], dtype=np.float32)) for nm, _ in W_NAMES}
    shared.update(cs)
    in_maps = []
    for c in range(ncores):
        b = c % nb
        m = dict(shared)
        m["x"] = np.ascontiguousarray(np.asarray(inputs["x"][b, :L], dtype=np.float32))
        m["mem"] = np.ascontiguousarray(np.asarray(inputs["mem"][b], dtype=np.float32))
        in_maps.append(m)
    res = run_bass_kernel_spmd(nc, in_maps, core_ids=list(range(ncores)))
    return np.stack([res.results[b]["y"] for b in range(nb)], axis=0)


def kernel(**inputs):
    out = run_cores(inputs, 8192, 4, ncores=4)
    return out.astype(np.float32)
```

```python
import math
import numpy as np
import concourse.bass as bass
import concourse.mybir as mybir
from concourse.bass_utils import run_bass_kernel_spmd
from contextlib import ExitStack

F32 = mybir.dt.float32
BF16 = mybir.dt.bfloat16
AF = mybir.ActivationFunctionType
ALU = mybir.AluOpType

D = 1024
NCH = 8
T = 512
NB = T // 128
DEPTH = 4
MEM = 256
DFF = 4096
GDN_IN = 4624
S5_IN = 1536
DN_ALPHA = (2 * DEPTH) ** 0.25
LN_EPS = 1e-5
RMS_EPS = 1e-6
BIG = 60000.0
TWO_PI = 2.0 * math.pi

ENGS = ("pe", "act", "dve", "pool", "sp")
SAME_SYNC = True
NDS = 8
OPLIMIT = 10 ** 9


class Res:
    __slots__ = ("name", "w", "r", "excl")

    def __init__(self, name, excl=False):
        self.name = name
        self.w = None
        self.r = {}
        self.excl = excl


class _Op:
    __slots__ = ("fn", "waits", "need_inc", "seq", "dma")

    def __init__(self, fn, dma=None):
        self.fn = fn
        self.waits = []
        self.need_inc = False
        self.seq = 0
        self.dma = dma


class Prog:
    def __init__(self, nc, es):
        self.nc = nc
        self.ops = {e: [] for e in ENGS}
        self.waited = {e: {} for e in ENGS}
        self.sems = {e: es.enter_context(nc.semaphore("s_" + e)) for e in ENGS}
        self.dsems = {}
        for q in ("sp", "pool", "act"):
            for i in range(NDS):
                self.dsems[("d", q, i)] = es.enter_context(nc.semaphore(f"d_{q}{i}"))
        self.dma_cnt = {q: 0 for q in ENGS}
        self.dma_val = {}
        self.pending = None
        self.pend_done = {e: None for e in ENGS}
        self.nrec = 0
        self.limit = OPLIMIT
        self.log = []

    def barrier(self):
        if self.nrec > self.limit:
            return
        toks = []
        for e in ENGS:
            idx = len(self.ops[e]) - 1
            while idx >= 0 and self.ops[e][idx].dma is not None:
                idx -= 1
            if idx >= 0:
                toks.append(("o", e, idx, self.ops[e][idx]))
        for key, val in self.dma_val.items():
            toks.append(("d", key, val))
        self.pending = toks

    def _deps(self, R, W, eng=None):
        deps = []
        for r in R:
            if r.w is not None:
                deps.append(r.w)
            if r.excl:
                for k_, t_ in r.r.items():
                    if k_ != eng:
                        deps.append(t_)
        for w in W:
            if w.w is not None:
                deps.append(w.w)
            deps.extend(w.r.values())
        return deps

    def _add(self, eng, op, deps):
        wd = self.waited[eng]
        my_idx = len(self.ops[eng])
        if self.pending is not None and self.pend_done[eng] is not self.pending:
            self.pend_done[eng] = self.pending
            deps = list(deps) + self.pending
        best = {}
        for tok in deps:
            if tok[0] == "o":
                F, idx = tok[1], tok[2]
                if F == eng and (eng == "pe" or not SAME_SYNC) and op.dma is None:
                    continue
                if F not in best or best[F][2] < idx:
                    best[F] = tok
            else:
                key, val = tok[1], tok[2]
                if key not in best or best[key][2] < val:
                    best[key] = tok
        for k_, tok in best.items():
            if tok[0] == "o":
                F, idx = tok[1], tok[2]
                if wd.get(F, -1) >= idx:
                    continue
                wd[F] = idx
                tok[3].need_inc = True
                op.waits.append(tok)
            else:
                key, val = tok[1], tok[2]
                if wd.get(key, 0) >= val:
                    continue
                wd[key] = val
                op.waits.append(tok)
        self.ops[eng].append(op)
        return my_idx

    def op(self, eng, fn, R=(), W=()):
        self.nrec += 1
        if self.nrec > self.limit:
            return None
        import sys as _s
        f = _s._getframe(1) if OPLIMIT < 10 ** 9 else None
        if f is None:
            f = type("F", (), {"f_code": type("C", (), {"co_filename": "", "co_name": ""})(), "f_lineno": 0})()
        while f.f_code.co_filename == __file__ and f.f_code.co_name in ("mm", "tr", "act", "ts", "tt", "stt", "scan", "copy", "memset", "recip", "reduce", "dense", "op"):
            f = f.f_back
        self.log.append((self.nrec, eng, f.f_lineno))
        op = _Op(fn)
        deps = self._deps(R, W, eng)
        idx = self._add(eng, op, deps)
        tok = ("o", eng, idx, op)
        for w in W:
            w.w = tok
            w.r = {}
        for r in R:
            r.r[eng] = tok
        return op

    def dma(self, out, in_, R=(), W=(), q="sp"):
        self.nrec += 1
        if self.nrec > self.limit:
            return None
        i = self.dma_cnt[q] % NDS
        self.dma_cnt[q] += 1
        key = ("d", q, i)
        prev = self.dma_val.get(key, 0)
        val = prev + 16
        self.dma_val[key] = val
        deps = self._deps(R, W)
        if prev > 0:
            deps.append(("d", key, prev))
        op = _Op(lambda e: e.dma_start(out=out, in_=in_), dma=(key, val))
        self._add(q, op, deps)
        tok = ("d", key, val)
        for w in W:
            w.w = tok
            w.r = {}
        for r in R:
            r.r[key] = tok
        return op

    def mm(self, out, lhsT, rhs, start=True, stop=True, R=(), W=()):
        return self.op("pe", lambda e: e.matmul(out, lhsT=lhsT, rhs=rhs, start=start, stop=stop), R, W)

    def tr(self, out, in_, ident, R=(), W=()):
        return self.op("pe", lambda e: e.transpose(out, in_, ident), R, W)

    def act(self, out, in_, func, bias=None, scale=None, accum=None, R=(), W=(), eng="act"):
        kw = {}
        if bias is not None:
            kw["bias"] = bias
        if scale is not None:
            kw["scale"] = scale
        if accum is not None:
            kw["accum_out"] = accum
        if func == AF.Copy and kw:
            func = AF.Identity
        return self.op(eng, lambda e: e.activation(out, in_, func, **kw), R, W)

    def ts(self, out, in0, s1, s2, op0, op1=None, R=(), W=(), eng="dve", accum=None):
        kw = {}
        if accum is not None:
            kw["accum_out"] = accum
        if op1 is None:
            return self.op(eng, lambda e: e.tensor_scalar(out, in0, s1, None, op0, **kw), R, W)
        return self.op(eng, lambda e: e.tensor_scalar(out, in0, s1, s2, op0, op1, **kw), R, W)

    def tt(self, out, in0, in1, op, R=(), W=(), eng="dve"):
        return self.op(eng, lambda e: e.tensor_tensor(out, in0, in1, op), R, W)

    def stt(self, out, in0, scalar, in1, op0, op1, R=(), W=(), eng="dve"):
        return self.op(eng, lambda e: e.scalar_tensor_tensor(out, in0, scalar, in1, op0, op1), R, W)

    def scan(self, out, d0, d1, init, op0, op1, R=(), W=()):
        return self.op("dve", lambda e: e.tensor_tensor_scan(out, d0, d1, init, op0, op1), R, W)

    def copy(self, out, in_, R=(), W=(), eng="dve"):
        if eng == "act":
            return self.op("act", lambda e: e.copy(out, in_), R, W)
        return self.op(eng, lambda e: e.tensor_copy(out, in_), R, W)

    def memset(self, ap, val, W=(), eng="dve"):
        return self.op(eng, lambda e: e.memset(ap, val), (), W)

    def recip(self, out, in_, R=(), W=()):
        return self.op("dve", lambda e: e.reciprocal(out, in_), R, W)

    def reduce(self, out, in_, op, R=(), W=()):
        return self.op("dve", lambda e: e.tensor_reduce(out, in_, mybir.AxisListType.X, op), R, W)

    def emit(self, block):
        nc = self.nc
        for e in ENGS:
            s = 0
            for op in self.ops[e]:
                if op.need_inc:
                    s += 1
                    op.seq = s
        fin = _Op(lambda e: e.nop())
        for key, val in self.dma_val.items():
            if self.waited["sp"].get(key, 0) < val:
                fin.waits.append(("d", key, val))
        self.ops["sp"].append(fin)

        def run(eng_name):
            def body(e):
                for op in self.ops[eng_name]:
                    for tok in op.waits:
                        if tok[0] == "o":
                            e.wait_ge(self.sems[tok[1]], tok[3].seq)
                        else:
                            e.wait_ge(self.dsems[tok[1]], tok[2])
                    ins = op.fn(e)
                    if op.dma is not None:
                        ins.then_inc(self.dsems[op.dma[0]], 16)
                    elif op.need_inc:
                        ins.then_inc(self.sems[eng_name], 1)
            return body

        block.tensor(run("pe"))
        block.scalar(run("act"))
        block.vector(run("dve"))
        block.gpsimd(run("pool"))
        block.sync(run("sp"))


class Arena:
    def __init__(self, nc, es, nbytes):
        self.t = es.enter_context(nc.sbuf_tensor("arena", [128, nbytes // 4], F32))
        self.off = 0
        self.cap = nbytes
        self.peak = 0

    def alloc(self, shape, dt):
        esz = 2 if dt == BF16 else 4
        n = 1
        for s in shape[1:]:
            n *= s
        nb = (n * esz + 31) // 32 * 32
        assert self.off + nb <= self.cap, f"arena overflow {self.off}+{nb}>{self.cap}"
        v = self.t[:, self.off // 4:(self.off + nb) // 4]
        if dt != F32:
            v = v.bitcast(dt)
        v = v[0:shape[0], 0:n]
        if len(shape) == 3:
            v = v.rearrange("p (a b) -> p a b", a=shape[1])
        elif len(shape) == 4:
            v = v.rearrange("p (a b c) -> p a b c", a=shape[1], b=shape[2])
        self.off += nb
        self.peak = max(self.peak, self.off)
        return v

    def mark(self):
        return self.off

    def release(self, m):
        self.off = m


def host_consts():
    r = np.arange(128)[:, None]
    c = np.arange(128)[None, :]
    cs = {}
    cs["c_ident"] = np.eye(128, dtype=np.float32)
    cs["c_ones"] = np.ones((128, 128), np.float32)
    cs["c_triu"] = (r <= c).astype(np.float32)
    cs["c_trils"] = (r > c).astype(np.float32)
    cs["c_mls"] = np.where(r > c, 0.0, -BIG).astype(np.float32)
    cs["c_mus"] = np.where(r < c, 0.0, -BIG).astype(np.float32)
    cs["c_mui"] = np.where(r <= c, 0.0, BIG).astype(np.float32)
    cs["c_nvec"] = np.broadcast_to(np.arange(1, 129, dtype=np.float32)[None, :], (128, 128)).copy()
    return cs


W_NAMES = [
    ("w_kv_mem", [DEPTH, D, 1024]), ("w_o", [DEPTH, 1536, D]),
    ("ln1_g", [DEPTH, D]), ("ln1_b", [DEPTH, D]), ("ln2_g", [DEPTH, D]), ("ln2_b", [DEPTH, D]),
    ("mlp_w1", [DEPTH, D, DFF]), ("mlp_w2", [DEPTH, DFF, D]),
    ("gdn_w_in", [2, D, GDN_IN]), ("gdn_conv_w", [2, 4, 3072]), ("gdn_a_log", [2, 8]),
    ("gdn_dt_bias", [2, 8]), ("gdn_norm_g", [2, 128]),
    ("s5_w_in", [2, D, S5_IN]), ("s5_a_re", [2, 64, 64]), ("s5_a_im", [2, 64, 64]),
    ("s5_b_re", [2, 64, 64, 16]), ("s5_b_im", [2, 64, 64, 16]),
    ("s5_c_re", [2, 64, 16, 64]), ("s5_c_im", [2, 64, 16, 64]),
    ("s5_log_dt", [2, 64]), ("s5_d", [2, D]), ("s5_w_glu", [2, D, D]), ("s5_b_glu", [2, D]),
]


def build_program(L, layers=(0, 1, 2, 3), mixers=True, xattn=True, gdn_slots=2, dbg=False):
    NT = L // T
    nc = bass.Bass("TRN2", target_bir_lowering=False)
    dr = {}
    dr["x"] = nc.dram_tensor("x", [L, D], F32, kind="ExternalInput").ap()
    dr["mem"] = nc.dram_tensor("mem", [MEM, D], F32, kind="ExternalInput").ap()
    for nm, shp in W_NAMES:
        dr[nm] = nc.dram_tensor(nm, shp, F32, kind="ExternalInput").ap()
    for nm, arr in host_consts().items():
        dr[nm] = nc.dram_tensor(nm, list(arr.shape), F32, kind="ExternalInput").ap()
    y = nc.dram_tensor("y", [L, D], F32, kind="ExternalOutput").ap()
    s5c = nc.dram_tensor("s5c", [2, 8, 128, 4 * 4 * 128], BF16).ap()
    s5t = nc.dram_tensor("s5t", [2, 8, 128, 2 * 4 * 128], F32).ap()

    with ExitStack() as es:
        P = Prog(nc, es)
        es.enter_context(nc.allow_non_contiguous_dma("small parameter relayouts"))
        AR = Arena(nc, es, 204 * 1024)
        banks = [es.enter_context(nc.psum_tensor(f"ps{i}", [128, 512], F32)) for i in range(8)]
        PR = [[Res(f"ps{b}", excl=True)] * 4 for b in range(8)]

        def bfview(b):
            return banks[b][:, :].bitcast(BF16)

        def v3(ap, a):
            return ap.rearrange("p (a b) -> p a b", a=a)

        def sin_of(out, in_, shift, tf, ti, shape, RR, WW):
            xs = tf[0]; kf = tf[1]; mm_ = tf[2]
            rw = dict(R=tuple(RR) + tuple(WW), W=tuple(WW))
            P.ts(xs, in_, shift, None, ALU.add, **rw)
            P.ts(kf, xs, 1.0 / TWO_PI, None, ALU.mult, **rw)
            P.copy(ti, kf, **rw)
            P.copy(kf, ti, **rw)
            P.stt(xs, kf, -TWO_PI, xs, ALU.mult, ALU.add, **rw)
            P.ts(mm_, xs, math.pi, TWO_PI, ALU.is_gt, ALU.mult, **rw)
            P.tt(xs, xs, mm_, ALU.subtract, **rw)
            P.ts(mm_, xs, -math.pi, TWO_PI, ALU.is_lt, ALU.mult, **rw)
            P.tt(xs, xs, mm_, ALU.add, **rw)
            P.act(out, xs, AF.Sin, **rw)

        R_c = Res("consts")
        ident_f = AR.alloc([128, 128], F32)
        ident_b = AR.alloc([128, 128], BF16)
        ones_f = AR.alloc([128, 128], F32)
        triu_f = AR.alloc([128, 128], F32)
        trils_f = AR.alloc([128, 128], F32)
        mls_f = AR.alloc([128, 128], F32)
        mus_f = AR.alloc([128, 128], F32)
        mui_f = AR.alloc([128, 128], F32)
        nvec_f = AR.alloc([128, 128], F32)
        ccol = AR.alloc([128, 8], F32)
        lnp = AR.alloc([128, 8, 16], F32)
        convw = AR.alloc([128, 24, 8], F32)
        normg = AR.alloc([128, 2], F32)
        gpar = AR.alloc([128, 32], F32)
        s5dv = AR.alloc([128, 8, 4], F32)
        KT = AR.alloc([128, DEPTH, 4, MEM], BF16)
        VM = AR.alloc([128, DEPTH, 2, 512], BF16)
        R_KT, R_VM = Res("KT"), Res("VM")
        Sst = AR.alloc([128, 2, 8, 128], F32)
        Sbf = AR.alloc([128, 2, 8, 128], BF16)
        R_S = [[Res(f"S{j}_{h}") for h in range(8)] for j in range(2)]
        R_Sb = [[Res(f"Sb{j}_{h}") for h in range(8)] for j in range(2)]
        convst = AR.alloc([128, 2, 24, 4], F32)
        R_cst = [[Res(f"cst{j}_{c}") for c in range(24)] for j in range(2)]
        hst = AR.alloc([128, 2, 32, 2], F32)
        R_hst = [[Res(f"hst{j}_{q}") for q in range(32)] for j in range(2)]
        theta = AR.alloc([128, 2, 32], F32)
        rmag = AR.alloc([128, 2, 32], F32)
        R_s5p = Res("s5p")
        xT = AR.alloc([128, NCH, T], F32)
        xb = AR.alloc([128, NCH, T], BF16)
        R_xT = [Res(f"xT{c}") for c in range(NCH)]
        R_xb = [Res(f"xb{c}") for c in range(NCH)]
        WSLOT = 16 * 1024
        wbufs = [(AR.alloc([128, WSLOT // 2], BF16), Res(f"wbuf{i}")) for i in range(2)]
        wsmall = AR.alloc([128, 8, 16], BF16)
        R_wsmall = Res("wsmall")
        wctr = [0]

        def load_w(src2d, nk, c0, ncols):
            wt, rr = wbufs[wctr[0] % 2]
            wctr[0] += 1
            assert nk * ncols * 2 <= WSLOT
            v = wt[:, 0:nk * ncols].rearrange("p (k n) -> p k n", k=nk)
            src = src2d.rearrange("(k p) n -> p k n", p=128)[:, :, c0:c0 + ncols]
            P.dma(v, src, R=(), W=(rr,), q="pool")
            return v, rr

        for t_, nm in ((ident_f, "c_ident"), (ones_f, "c_ones"), (triu_f, "c_triu"), (trils_f, "c_trils"),
                       (mls_f, "c_mls"), (mus_f, "c_mus"), (mui_f, "c_mui"), (nvec_f, "c_nvec")):
            P.dma(t_, dr[nm], W=(R_c,))
        P.copy(ident_b, ident_f, R=(R_c,), W=(R_c,), eng="dve")
        P.memset(ccol[:, 0:1], -math.pi, W=(R_c,))
        P.memset(ccol[:, 1:2], 1.0, W=(R_c,))
        P.memset(ccol[:, 2:3], LN_EPS, W=(R_c,))
        P.memset(ccol[:, 3:4], 1e-6, W=(R_c,))
        P.memset(ccol[:, 4:5], 128.0 * RMS_EPS, W=(R_c,))
        UMARK = AR.mark()
        rows16 = AR.alloc([16, D], F32); rows8 = AR.alloc([8, 3072], F32); rows4 = AR.alloc([4, D], F32)
        rows2 = AR.alloc([2, 128], F32); rowg = AR.alloc([1, 32], F32)
        R_rows = Res("rows")
        for ki, nm in enumerate(("ln1_g", "ln1_b", "ln2_g", "ln2_b")):
            P.dma(rows16[ki * 4:(ki + 1) * 4, :], dr[nm], W=(R_rows,))
        P.dma(rows8, dr["gdn_conv_w"].rearrange("j k n -> (j k) n"), W=(R_rows,))
        P.dma(rows4[0:2, :], dr["s5_d"], W=(R_rows,))
        P.dma(rows4[2:4, :], dr["s5_b_glu"], W=(R_rows,))
        P.dma(rows2, dr["gdn_norm_g"], W=(R_rows,))
        P.dma(rowg[:, 0:16], dr["gdn_a_log"].rearrange("(o j) h -> o (j h)", o=1), W=(R_rows,))
        P.dma(rowg[:, 16:32], dr["gdn_dt_bias"].rearrange("(o j) h -> o (j h)", o=1), W=(R_rows,))
        for c in range(8):
            P.tr(banks[0][:, c * 16:(c + 1) * 16], rows16[:, c * 128:(c + 1) * 128], ident_f[0:16, 0:16], R=(R_rows, R_c), W=(PR[0][0],))
            P.tr(banks[1][:, c * 4:(c + 1) * 4], rows4[:, c * 128:(c + 1) * 128], ident_f[0:4, 0:4], R=(R_rows, R_c), W=(PR[1][0],))
        for g in range(24):
            P.tr(banks[2][:, g * 8:(g + 1) * 8], rows8[:, g * 128:(g + 1) * 128], ident_f[0:8, 0:8], R=(R_rows, R_c), W=(PR[2][0], PR[2][1]))
        P.tr(banks[3][:, 0:2], rows2, ident_f[0:2, 0:2], R=(R_rows, R_c), W=(PR[3][0],))
        P.mm(banks[3][:, 128:160], ones_f[0:1, :], rowg, R=(R_rows, R_c), W=(PR[3][1],))
        P.copy(lnp.rearrange("p c k -> p (c k)"), banks[0][:, 0:128], R=(PR[0][0],), W=(R_c,), eng="act")
        P.copy(s5dv.rearrange("p c k -> p (c k)"), banks[1][:, 0:32], R=(PR[1][0],), W=(R_c,), eng="act")
        P.copy(convw.rearrange("p g k -> p (g k)"), banks[2][:, 0:192], R=(PR[2][0], PR[2][1]), W=(R_c,), eng="act")
        P.act(normg, banks[3][:, 0:2], AF.Identity, scale=math.sqrt(128.0), R=(PR[3][0],), W=(R_c,))
        P.act(gpar[:, 0:16], banks[3][:, 128:144], AF.Exp, R=(PR[3][1],), W=(R_c,))
        P.ts(gpar[:, 0:16], gpar[:, 0:16], -1.0, None, ALU.mult, R=(R_c,), W=(R_c,))
        P.copy(gpar[:, 16:32], banks[3][:, 144:160], R=(PR[3][1],), W=(R_c,), eng="act")
        P.barrier()
        AR.release(UMARK)
        P.memset(Sst, 0.0, W=[r for rr in R_S for r in rr])
        P.memset(Sbf, 0.0, W=[r for rr in R_Sb for r in rr], eng="pool")
        P.memset(convst, 0.0, W=[r for rr in R_cst for r in rr])
        P.memset(hst, 0.0, W=[r for rr in R_hst for r in rr])

        if xattn:
            memtok = AR.alloc([128, 2, D], F32)
            memT = AR.alloc([128, NCH, MEM], BF16)
            R_mt, R_mT = Res("memtok"), Res("memT")
            P.dma(memtok, dr["mem"].rearrange("(b p) f -> p b f", p=128), W=(R_mt,))
            for c in range(NCH):
                for mb in range(2):
                    P.tr(banks[0][:, mb * 128:(mb + 1) * 128], memtok[:, mb, c * 128:(c + 1) * 128], ident_f,
                         R=(R_mt, R_c), W=(PR[0][mb],))
                P.copy(memT[:, c, :], banks[0][:, 0:256], R=(PR[0][0], PR[0][1]), W=(R_mT,), eng="act")
            for l in layers:
                wv, wr = load_w(dr["w_kv_mem"][l], 8, 0, 1024)
                for h in range(4):
                    for k in range(8):
                        P.mm(banks[1][:, 0:256], wv[:, k, h * 128:(h + 1) * 128], memT[:, k, :],
                             start=(k == 0), stop=(k == 7), R=(wr, R_mT), W=(PR[1][0], PR[1][1]))
                    P.copy(KT[:, l, h, :], banks[1][:, 0:256], R=(PR[1][0], PR[1][1]), W=(R_KT,), eng="act")
                for mb in range(2):
                    for k in range(8):
                        P.mm(banks[2][:, :], memT[:, k, mb * 128:(mb + 1) * 128], wv[:, k, 512:1024],
                             start=(k == 0), stop=(k == 7), R=(wr, R_mT), W=PR[2])
                    P.copy(VM[:, l, mb, :], banks[2][:, :], R=PR[2], W=(R_VM,), eng="act")
            P.barrier()
            AR.release(UMARK)

        s5_layers = [l for l in layers if l % 2 == 1]
        R_s5c = Res("s5c_dram")
        if mixers and s5_layers:
            A2 = AR.alloc([32, 2, 128], F32)
            LD = AR.alloc([32, 2], F32)
            LDb = AR.alloc([32, 128], F32)
            are = AR.alloc([128, 32], F32); aim = AR.alloc([128, 32], F32); dtt = AR.alloc([128, 32], F32)
            t1 = AR.alloc([128, 32], F32); t2 = AR.alloc([128, 32], F32); t3 = AR.alloc([128, 32], F32)
            abr = AR.alloc([128, 32], F32); abi = AR.alloc([128, 32], F32)
            fre = AR.alloc([128, 32], F32); fim = AR.alloc([128, 32], F32)
            bre = AR.alloc([128, 32, 16], F32); bim = AR.alloc([128, 32, 16], F32)
            bbr = AR.alloc([128, 32, 16], F32); bbi = AR.alloc([128, 32, 16], F32); tb16 = AR.alloc([128, 32, 16], F32)
            Ci = AR.alloc([16, 2, 64, 64], F32)
            cre = AR.alloc([128, 32, 16], F32); cim = AR.alloc([128, 32, 16], F32)
            Z = AR.alloc([128, 4, 2, 128], F32)
            CB = AR.alloc([128, 4, 4, 128], BF16)
            R_p, R_Z, R_CB = Res("s5prep"), Res("Z"), Res("CB")
            rr1 = AR.alloc([128, 32], F32); rr2 = AR.alloc([128, 32], F32); rr3 = AR.alloc([128, 32], F32)
            rri = AR.alloc([128, 32], mybir.dt.int32)
            argt = AR.alloc([128, 4, 128], F32); tabt = AR.alloc([128, 2, 4, 128], F32)
            ra1 = AR.alloc([128, 4, 128], F32); ra2 = AR.alloc([128, 4, 128], F32); ra3 = AR.alloc([128, 4, 128], F32)
            rai = AR.alloc([128, 4, 128], mybir.dt.int32)
            RW = dict(R=(R_p,), W=(R_p,))
            for l in s5_layers:
                j = l // 2
                P.dma(A2[:, 0, :], dr["s5_a_re"][j].rearrange("(q g) p -> q (g p)", g=2), W=(R_p,))
                P.dma(A2[:, 1, :], dr["s5_a_im"][j].rearrange("(q g) p -> q (g p)", g=2), W=(R_p,))
                P.dma(LD, dr["s5_log_dt"][j].rearrange("(q g) -> q g", g=2), W=(R_p,))
                for g2 in range(2):
                    ps_ = slice(g2 * 64, (g2 + 1) * 64)
                    for qh in range(2):
                        qs_ = slice(qh * 16, (qh + 1) * 16)
                        P.dma(bre[ps_, qs_, :], dr["s5_b_re"][j].rearrange("(q g) p i -> g p q i", g=2)[g2][:, qs_, :], W=(R_p,))
                        P.dma(bim[ps_, qs_, :], dr["s5_b_im"][j].rearrange("(q g) p i -> g p q i", g=2)[g2][:, qs_, :], W=(R_p,))
                P.dma(Ci[:, 0, :, :], dr["s5_c_re"][j].rearrange("g i p -> i g p"), W=(R_p,))
                P.dma(Ci[:, 1, :, :], dr["s5_c_im"][j].rearrange("g i p -> i g p"), W=(R_p,))
                for ri, dst in ((0, are), (1, aim)):
                    P.tr(banks[4][:, ri * 32:(ri + 1) * 32], A2[:, ri, :], ident_f[0:32, 0:32], R=(R_p, R_c), W=(PR[4][0],))
                    P.copy(dst, banks[4][:, ri * 32:(ri + 1) * 32], R=(PR[4][0],), W=(R_p,), eng="act")
                P.copy(v3(LDb, 2), LD.unsqueeze(2).broadcast_to([32, 2, 64]), **RW)
                P.tr(banks[4][:, 64:96], LDb, ident_f[0:32, 0:32], R=(R_p, R_c), W=(PR[4][0],))
                P.copy(dtt, banks[4][:, 64:96], R=(PR[4][0],), W=(R_p,), eng="act")
                for ri, dst in ((0, cre), (1, cim)):
                    for q in range(32):
                        P.tr(banks[5 + ri][:, q * 16:(q + 1) * 16], Ci[:, ri, 2 * q:2 * q + 2, :].rearrange("i g p -> i (g p)"),
                             ident_f[0:16, 0:16], R=(R_p, R_c), W=(PR[5 + ri][q // 8],))
                    P.copy(dst.rearrange("p q i -> p (q i)"), banks[5 + ri][:, :], R=PR[5 + ri], W=(R_p,), eng="act")
                P.act(dtt, dtt, AF.Exp, **RW)
                P.tt(t1, are, dtt, ALU.mult, **RW)
                P.act(rmag[:, j, :], t1, AF.Exp, R=(R_p,), W=(R_p, R_s5p))
                P.tt(theta[:, j, :], aim, dtt, ALU.mult, R=(R_p,), W=(R_p, R_s5p))
                sin_of(t2, theta[:, j, :], 0.0, (rr1, rr2, rr3), rri, None, (R_s5p,), (R_p,))
                sin_of(t3, theta[:, j, :], 0.5 * math.pi, (rr1, rr2, rr3), rri, None, (R_s5p,), (R_p,))
                for c in range(8):
                    P.tt(argt, theta[:, j, 4 * c:4 * c + 4].unsqueeze(2).broadcast_to([128, 4, 128]),
                         nvec_f.unsqueeze(1).broadcast_to([128, 4, 128]), ALU.mult, R=(R_s5p, R_c, R_p), W=(R_p,))
                    sin_of(tabt[:, 0, :, :], argt, 0.5 * math.pi, (ra1, ra2, ra3), rai, None, (R_s5p,), (R_p,))
                    sin_of(tabt[:, 1, :, :], argt, 0.0, (ra1, ra2, ra3), rai, None, (R_s5p,), (R_p,))
                    P.dma(s5t[j, c].rearrange("p (k q n) -> p k q n", k=2, q=4), tabt, R=(R_p,), W=(R_s5c,))
                P.tt(abr, rmag[:, j, :], t3, ALU.mult, R=(R_p, R_s5p), W=(R_p,))
                P.tt(abi, rmag[:, j, :], t2, ALU.mult, R=(R_p, R_s5p), W=(R_p,))
                P.ts(abr, abr, -1.0, None, ALU.add, **RW)
                P.tt(t1, are, are, ALU.mult, **RW)
                P.tt(t2, aim, aim, ALU.mult, **RW)
                P.tt(t1, t1, t2, ALU.add, **RW)
                P.recip(t1, t1, **RW)
                P.tt(t2, abr, are, ALU.mult, **RW)
                P.tt(t3, abi, aim, ALU.mult, **RW)
                P.tt(t2, t2, t3, ALU.add, **RW)
                P.tt(fre, t2, t1, ALU.mult, **RW)
                P.tt(t2, abi, are, ALU.mult, **RW)
                P.tt(t3, abr, aim, ALU.mult, **RW)
                P.tt(t2, t2, t3, ALU.subtract, **RW)
                P.tt(fim, t2, t1, ALU.mult, **RW)
                fre_b = fre.unsqueeze(2).broadcast_to([128, 32, 16])
                fim_b = fim.unsqueeze(2).broadcast_to([128, 32, 16])
                P.tt(bbr, bre, fre_b, ALU.mult, **RW)
                P.tt(tb16, bim, fim_b, ALU.mult, **RW)
                P.tt(bbr, bbr, tb16, ALU.subtract, **RW)
                P.tt(bbi, bim, fre_b, ALU.mult, **RW)
                P.tt(tb16, bre, fim_b, ALU.mult, **RW)
                P.tt(bbi, bbi, tb16, ALU.add, **RW)
                P.ts(cim, cim, -1.0, None, ALU.mult, **RW)
                for c in range(8):
                    P.memset(Z, 0.0, W=(R_Z,))
                    P.memset(CB, 0.0, W=(R_CB,), eng="pool")
                    for qq in range(4):
                        q = 4 * c + qq
                        for g2 in range(2):
                            ps_ = slice(g2 * 64, (g2 + 1) * 64)
                            gl = 2 * qq + g2
                            cs_ = slice(gl * 16, gl * 16 + 16)
                            P.copy(Z[ps_, qq, 0, cs_], bbr[ps_, q, :], R=(R_p,), W=(R_Z,))
                            P.copy(Z[ps_, qq, 1, cs_], bbi[ps_, q, :], R=(R_p,), W=(R_Z,))
                            P.copy(CB[ps_, qq, 2, cs_], cre[ps_, q, :], R=(R_p,), W=(R_CB,), eng="pool")
                            P.copy(CB[ps_, qq, 3, cs_], cim[ps_, q, :], R=(R_p,), W=(R_CB,), eng="pool")
                    for qq in range(4):
                        for ri in range(2):
                            P.tr(banks[7][:, ri * 128:(ri + 1) * 128], Z[:, qq, ri, :], ident_f, R=(R_Z, R_c), W=(PR[7][ri],))
                        for ri in range(2):
                            P.copy(CB[:, qq, ri, :], banks[7][:, ri * 128:(ri + 1) * 128], R=(PR[7][ri],), W=(R_CB,), eng="act")
                    P.dma(s5c[j, c].rearrange("p (q k n) -> p q k n", q=4, k=4), CB, R=(R_CB,), W=(R_s5c,))
            P.barrier()
            AR.release(UMARK)

        pctr = [0]

        def next_bank(avoid=()):
            while True:
                b = pctr[0] % 8
                pctr[0] += 1
                if b not in avoid:
                    return b

        def dense(wv, wr, col0, nk, rhs_fn, rhs_res, b):
            for k in range(nk):
                P.mm(banks[b][:, :], wv[:, k, col0:col0 + 128], rhs_fn(k), start=(k == 0), stop=(k == nk - 1),
                     R=(wr,) + tuple(rhs_res(k)), W=PR[b])

        def xb_fn(k):
            return xb[:, k, :]

        def xb_res(k):
            return (R_xb[k],)

        def layer_norm(l, which):
            gi, bi = (0, 1) if which == 1 else (2, 3)
            m0 = AR.mark()
            sq = [AR.alloc([128, T], F32) for _ in range(2)]
            R_sq = [Res("lnsq0"), Res("lnsq1")]
            mean = AR.alloc([128, T], F32); rstd = AR.alloc([128, T], F32); tmp = AR.alloc([128, T], F32)
            R_m, R_r, R_t = Res("lnmean"), Res("lnrstd"), Res("lntmp")
            bs = next_bank()
            bq = next_bank()
            for c in range(NCH):
                P.mm(banks[bs][:, :], ones_f, xT[:, c, :], start=(c == 0), stop=(c == NCH - 1), R=(R_c, R_xT[c]), W=PR[bs])
            for c in range(NCH):
                P.act(sq[c % 2], xT[:, c, :], AF.Square, R=(R_xT[c],), W=(R_sq[c % 2],))
                P.mm(banks[bq][:, :], ones_f, sq[c % 2], start=(c == 0), stop=(c == NCH - 1), R=(R_c, R_sq[c % 2]), W=PR[bq])
            P.act(mean, banks[bs][:, :], AF.Copy, scale=1.0 / D, R=PR[bs], W=(R_m,))
            P.tt(tmp, mean, mean, ALU.mult, R=(R_m,), W=(R_t,))
            P.stt(rstd, banks[bq][:, :], 1.0 / D, tmp, ALU.mult, ALU.subtract, R=PR[bq] + [R_t], W=(R_r,))
            P.act(rstd, rstd, AF.Sqrt, bias=ccol[:, 2:3], R=(R_r, R_c), W=(R_r,))
            P.recip(rstd, rstd, R=(R_r,), W=(R_r,))
            for c in range(NCH):
                P.tt(tmp, xT[:, c, :], mean, ALU.subtract, R=(R_xT[c], R_m), W=(R_t,))
                P.tt(tmp, tmp, rstd, ALU.mult, R=(R_t, R_r), W=(R_t,))
                P.act(xT[:, c, :], tmp, AF.Identity, bias=lnp[:, c, bi * 4 + l:bi * 4 + l + 1], scale=lnp[:, c, gi * 4 + l:gi * 4 + l + 1],
                      R=(R_t, R_c), W=(R_xT[c],))
                P.copy(xb[:, c, :], xT[:, c, :], R=(R_xT[c],), W=(R_xb[c],), eng="pool")
            P.barrier()
            AR.release(m0)

        def residual_evac(b, c):
            P.stt(xT[:, c, :], xT[:, c, :], DN_ALPHA, banks[b][:, :], ALU.mult, ALU.add, R=[R_xT[c]] + PR[b], W=(R_xT[c],))

        def cross_attention(l, xq_b, R_xq, mix_b, R_mix):
            sc = 128.0 ** -0.5
            Pf = [AR.alloc([128, MEM], F32) for _ in range(2)]
            Pn = [AR.alloc([128, MEM], BF16) for _ in range(2)]
            PT = [AR.alloc([128, 2, 128], BF16) for _ in range(2)]
            st = [AR.alloc([128, 4], F32) for _ in range(2)]
            R_Pf = [Res("Pf0"), Res("Pf1")]; R_Pn = [Res("Pn0"), Res("Pn1")]; R_PT = [Res("PT0"), Res("PT1")]
            R_st = [Res("st0"), Res("st1")]
            it = 0
            for h in range(4):
                bo = next_bank()
                for tb in range(NB):
                    s = it % 2
                    it += 1
                    bsc = next_bank(avoid=(bo,))
                    tsl = slice(tb * 128, (tb + 1) * 128)
                    P.mm(banks[bsc][:, 0:MEM], xq_b[:, h, tsl], KT[:, l, h, :], R=(R_xq, R_KT), W=(PR[bsc][0], PR[bsc][1]))
                    P.reduce(st[s][:, 0:1], banks[bsc][:, 0:MEM], ALU.max, R=(PR[bsc][0], PR[bsc][1]), W=(R_st[s],))
                    P.ts(st[s][:, 1:2], st[s][:, 0:1], -sc, None, ALU.mult, R=(R_st[s],), W=(R_st[s],))
                    P.act(Pf[s], banks[bsc][:, 0:MEM], AF.Exp, bias=st[s][:, 1:2], scale=sc,
                          R=(PR[bsc][0], PR[bsc][1], R_st[s]), W=(R_Pf[s],))
                    P.reduce(st[s][:, 2:3], Pf[s], ALU.add, R=(R_Pf[s],), W=(R_st[s],))
                    P.recip(st[s][:, 3:4], st[s][:, 2:3], R=(R_st[s],), W=(R_st[s],))
                    P.act(Pn[s], Pf[s], AF.Copy, scale=st[s][:, 3:4], R=(R_Pf[s], R_st[s]), W=(R_Pn[s],))
                    bt = next_bank(avoid=(bo,))
                    btb = bfview(bt)
                    for mb in range(2):
                        P.tr(btb[:, mb * 128:(mb + 1) * 128], Pn[s][:, mb * 128:(mb + 1) * 128], ident_b,
                             R=(R_Pn[s], R_c), W=(PR[bt][0],))
                    P.copy(PT[s].rearrange("p a b -> p (a b)"), btb[:, 0:256], R=(PR[bt][0],), W=(R_PT[s],), eng="act")
                    for mb in range(2):
                        P.mm(banks[bo][:, tsl], VM[:, l, mb, h * 128:(h + 1) * 128], PT[s][:, mb, :],
                             start=(mb == 0), stop=(mb == 1), R=(R_VM, R_PT[s]), W=(PR[bo][tb],))
                P.copy(mix_b[:, 8 + h, :], banks[bo][:, :], R=PR[bo], W=(R_mix[8 + h],), eng="act")

        def out_proj_ln_mlp(l, mix_b, R_mix):
            for grp in range(2):
                wv, wr = load_w(dr["w_o"][l], 12, grp * 512, 512)
                for oc in range(4):
                    b = next_bank()
                    dense(wv, wr, oc * 128, 12, lambda k: mix_b[:, k, :], lambda k: (R_mix[k],), b)
                    residual_evac(b, grp * 4 + oc)
            P.barrier()
            AR.release(TMARK)
            layer_norm(l, 1)
            hb = AR.alloc([128, 32, T], BF16)
            R_h = [Res(f"h{i}") for i in range(32)]
            rl = [AR.alloc([128, T], F32) for _ in range(2)]
            R_rl = [Res("rl0"), Res("rl1")]
            for grp in range(4):
                wv, wr = load_w(dr["mlp_w1"][l], 8, grp * 1024, 1024)
                for oc in range(8):
                    b = next_bank()
                    g = grp * 8 + oc
                    dense(wv, wr, oc * 128, 8, xb_fn, xb_res, b)
                    P.act(rl[g % 2], banks[b][:, :], AF.Relu, R=PR[b], W=(R_rl[g % 2],))
                    P.tt(hb[:, g, :], rl[g % 2], rl[g % 2], ALU.mult, R=(R_rl[g % 2],), W=(R_h[g],), eng="dve")
            for grp in range(4):
                wv, wr = load_w(dr["mlp_w2"][l], 32, grp * 256, 256)
                for oc in range(2):
                    b = next_bank()
                    dense(wv, wr, oc * 128, 32, lambda k: hb[:, k, :], lambda k: (R_h[k],), b)
                    residual_evac(b, grp * 2 + oc)
            P.barrier()
            AR.release(TMARK)
            layer_norm(l, 2)

        def s5_layer(l, j, mix_b, R_mix, xq_b, R_xq):
            W2d = dr["s5_w_in"][j]
            zg_f = AR.alloc([128, NCH, T], F32); zg_b = AR.alloc([128, NCH, T], BF16)
            R_zf = [Res(f"zgf{c}") for c in range(NCH)]; R_zb = [Res(f"zgb{c}") for c in range(NCH)]
            u_f = [AR.alloc([128, T], F32) for _ in range(2)]; u_b = [AR.alloc([128, T], BF16) for _ in range(2)]
            R_u = [Res("u0"), Res("u1")]
            cbuf = [AR.alloc([128, 4, 4, 128], BF16) for _ in range(2)]; R_cb = [Res("cb0"), Res("cb1")]
            tabs = [AR.alloc([128, 2, 4, 128], F32) for _ in range(2)]
            Ctab = [t_[:, 0, :, :] for t_ in tabs]; Stab = [t_[:, 1, :, :] for t_ in tabs]
            nSl = [AR.alloc([128, 4], F32) for _ in range(2)]
            R_tab = [Res("tab0"), Res("tab1")]
            NS = 2
            ta = [AR.alloc([128, T], F32) for _ in range(NS)]; tbb = [AR.alloc([128, T], F32) for _ in range(NS)]
            Wr = [AR.alloc([128, T], F32) for _ in range(NS)]; Wi = [AR.alloc([128, T], F32) for _ in range(NS)]
            gr = [AR.alloc([128, T], F32) for _ in range(NS)]; gi = [AR.alloc([128, T], F32) for _ in range(NS)]
            hrb = [AR.alloc([128, T], BF16) for _ in range(NS)]; hib = [AR.alloc([128, T], BF16) for _ in range(NS)]
            cr = [AR.alloc([128, 4, 4], F32) for _ in range(NS)]
            R_ta = [Res(f"ta{i}") for i in range(NS)]; R_tb = [Res(f"tb{i}") for i in range(NS)]
            R_W = [Res(f"W{i}") for i in range(NS)]; R_g = [Res(f"g{i}") for i in range(NS)]
            R_hb = [Res(f"hb{i}") for i in range(NS)]; R_cr = [Res(f"cr{i}") for i in range(NS)]
            yd = AR.alloc([128, T], F32); x2 = AR.alloc([128, T], F32); sg = AR.alloc([128, T], F32)
            R_yd, R_x2, R_sg = Res("yd"), Res("x2"), Res("sg")
            wv_u, wr_u = load_w(W2d, 8, 0, 1024)
            pc = 0
            for c in range(NCH):
                s = c % 2
                bu = 2 + s
                bY = s
                dense(wv_u, wr_u, c * 128, 8, xb_fn, xb_res, bu)
                P.copy(u_f[s], banks[bu][:, :], R=PR[bu], W=(R_u[s],), eng="act")
                P.copy(u_b[s], u_f[s], R=(R_u[s],), W=(R_u[s],), eng="pool")
                P.dma(cbuf[s], s5c[j, c].rearrange("p (q k n) -> p q k n", q=4, k=4), R=(R_s5c,), W=(R_cb[s],))
                P.dma(tabs[s], s5t[j, c].rearrange("p (k q n) -> p k q n", k=2, q=4), R=(R_s5c,), W=(R_tab[s],))
                P.ts(nSl[s], Stab[s][:, :, 127], -1.0, None, ALU.mult, R=(R_tab[s],), W=(R_tab[s],))
                for qq in range(4):
                    q = 4 * c + qq
                    z = pc % NS
                    pc += 1
                    bA, bB = (4, 5) if z == 0 else (6, 7)
                    P.mm(banks[bA][:, :], cbuf[s][:, qq, 0, :], u_b[s], R=(R_cb[s], R_u[s]), W=PR[bA])
                    P.mm(banks[bB][:, :], cbuf[s][:, qq, 1, :], u_b[s], R=(R_cb[s], R_u[s]), W=PR[bB])
                    Cb = Ctab[s][:, qq, :].unsqueeze(1).broadcast_to([128, 4, 128])
                    Sb_ = Stab[s][:, qq, :].unsqueeze(1).broadcast_to([128, 4, 128])
                    pA, pB = v3(banks[bA][:, :], 4), v3(banks[bB][:, :], 4)
                    P.tt(v3(ta[z], 4), pA, Cb, ALU.mult, R=PR[bA] + [R_tab[s]], W=(R_ta[z],))
                    P.tt(v3(tbb[z], 4), pB, Sb_, ALU.mult, R=PR[bB] + [R_tab[s]], W=(R_tb[z],))
                    P.tt(Wr[z], ta[z], tbb[z], ALU.add, R=(R_ta[z], R_tb[z]), W=(R_W[z],), eng="pool")
                    P.tt(v3(ta[z], 4), pB, Cb, ALU.mult, R=PR[bB] + [R_tab[s]], W=(R_ta[z],))
                    P.tt(v3(tbb[z], 4), pA, Sb_, ALU.mult, R=PR[bA] + [R_tab[s]], W=(R_tb[z],))
                    P.tt(Wi[z], ta[z], tbb[z], ALU.subtract, R=(R_ta[z], R_tb[z]), W=(R_W[z],), eng="pool")
                    rdec = rmag[:, j, q:q + 1].broadcast_to([128, 128])
                    c_l = Ctab[s][:, qq, 127:128]; s_l = Stab[s][:, qq, 127:128]; ns_l = nSl[s][:, qq:qq + 1]
                    for blk in range(NB):
                        bs_ = slice(blk * 128, (blk + 1) * 128)
                        if blk == 0:
                            ir, ii, Rin = hst[:, j, q, 0:1], hst[:, j, q, 1:2], R_hst[j][q]
                        else:
                            ir, ii, Rin = cr[z][:, blk - 1, 0:1], cr[z][:, blk - 1, 1:2], R_cr[z]
                        P.scan(gr[z][:, bs_], rdec, Wr[z][:, bs_], ir, ALU.mult, ALU.add, R=(R_s5p, R_W[z], Rin), W=(R_g[z],))
                        P.scan(gi[z][:, bs_], rdec, Wi[z][:, bs_], ii, ALU.mult, ALU.add, R=(R_s5p, R_W[z], Rin), W=(R_g[z],))
                        gr_l = gr[z][:, blk * 128 + 127:blk * 128 + 128]
                        gi_l = gi[z][:, blk * 128 + 127:blk * 128 + 128]
                        if blk < NB - 1:
                            dr_, di_, Rout = cr[z][:, blk, 0:1], cr[z][:, blk, 1:2], R_cr[z]
                        else:
                            dr_, di_, Rout = hst[:, j, q, 0:1], hst[:, j, q, 1:2], R_hst[j][q]
                        P.ts(cr[z][:, blk, 2:3], gr_l, c_l, None, ALU.mult, R=(R_g[z], R_tab[s]), W=(R_cr[z],))
                        P.ts(cr[z][:, blk, 3:4], gi_l, c_l, None, ALU.mult, R=(R_g[z], R_tab[s]), W=(R_cr[z],))
                        P.stt(dr_, gi_l, ns_l, cr[z][:, blk, 2:3], ALU.mult, ALU.add, R=(R_g[z], R_tab[s], R_cr[z]), W=(Rout,))
                        P.stt(di_, gr_l, s_l, cr[z][:, blk, 3:4], ALU.mult, ALU.add, R=(R_g[z], R_tab[s], R_cr[z]), W=(Rout,))
                    P.tt(v3(ta[z], 4), v3(gr[z], 4), Cb, ALU.mult, R=(R_g[z], R_tab[s]), W=(R_ta[z],), eng="pool")
                    P.tt(v3(tbb[z], 4), v3(gi[z], 4), Sb_, ALU.mult, R=(R_g[z], R_tab[s]), W=(R_tb[z],), eng="pool")
                    P.tt(hrb[z], ta[z], tbb[z], ALU.subtract, R=(R_ta[z], R_tb[z]), W=(R_hb[z],), eng="pool")
                    P.tt(v3(ta[z], 4), v3(gi[z], 4), Cb, ALU.mult, R=(R_g[z], R_tab[s]), W=(R_ta[z],), eng="pool")
                    P.tt(v3(tbb[z], 4), v3(gr[z], 4), Sb_, ALU.mult, R=(R_g[z], R_tab[s]), W=(R_tb[z],), eng="pool")
                    P.tt(hib[z], ta[z], tbb[z], ALU.add, R=(R_ta[z], R_tb[z]), W=(R_hb[z],), eng="pool")
                    P.mm(banks[bY][:, :], cbuf[s][:, qq, 2, :], hrb[z], start=(qq == 0), stop=False, R=(R_cb[s], R_hb[z]), W=PR[bY])
                    P.mm(banks[bY][:, :], cbuf[s][:, qq, 3, :], hib[z], start=False, stop=(qq == 3), R=(R_cb[s], R_hb[z]), W=PR[bY])
                P.stt(yd, u_f[s], s5dv[:, c, j:j + 1], banks[bY][:, :], ALU.mult, ALU.add, R=[R_u[s], R_c] + PR[bY], W=(R_yd,))
                P.act(x2, yd, AF.Square, R=(R_yd,), W=(R_x2,))
                P.ts(x2, x2, 0.044715, 1.0, ALU.mult, ALU.add, R=(R_x2,), W=(R_x2,))
                P.tt(x2, x2, yd, ALU.mult, R=(R_x2, R_yd), W=(R_x2,))
                P.act(sg, x2, AF.Sigmoid, scale=2.0 * math.sqrt(2.0 / math.pi), R=(R_x2,), W=(R_sg,))
                P.tt(zg_f[:, c, :], yd, sg, ALU.mult, R=(R_yd, R_sg), W=(R_zf[c],))
                P.copy(zg_b[:, c, :], zg_f[:, c, :], R=(R_zf[c],), W=(R_zb[c],), eng="pool")
            wv, wr = load_w(W2d, 8, 1024, 512)
            for h in range(4):
                b = 2 + (h % 2)
                dense(wv, wr, h * 128, 8, xb_fn, xb_res, b)
                P.copy(xq_b[:, h, :], banks[b][:, :], R=PR[b], W=(R_xq,), eng="act")
            wv, wr = load_w(dr["s5_w_glu"][j], 8, 0, 1024)
            for oc in range(NCH):
                b = 4 + (oc % 4)
                dense(wv, wr, oc * 128, 8, lambda k: zg_b[:, k, :], lambda k: (R_zb[k],), b)
                P.act(sg, banks[b][:, :], AF.Sigmoid, bias=s5dv[:, oc, 2 + j:3 + j], R=PR[b] + [R_c], W=(R_sg,))
                P.tt(mix_b[:, oc, :], zg_f[:, oc, :], sg, ALU.mult, R=(R_zf[oc], R_sg), W=(R_mix[oc],))

        def gdn_layer(l, j, mix_b, R_mix, xq_b, R_xq):
            W2d = dr["gdn_w_in"][j]
            qkv_b = AR.alloc([128, 24, T], BF16); R_qkv = [Res(f"qkv{i}") for i in range(24)]
            zs = AR.alloc([128, 8, T], BF16); R_zs = [Res(f"zs{i}") for i in range(8)]
            oT = AR.alloc([128, 8, T], F32); R_oT = [Res(f"oT{i}") for i in range(8)]
            stage = [AR.alloc([128, T + 4], F32) for _ in range(2)]; R_stg = [Res("stg0"), Res("stg1")]
            acc = [AR.alloc([128, T], F32) for _ in range(2)]; R_acc = [Res("acc0"), Res("acc1")]
            sl = [AR.alloc([128, T], F32) for _ in range(2)]; R_sl = [Res("sl0"), Res("sl1")]
            sq = AR.alloc([128, T], F32); R_sq = Res("gsq")
            rs = AR.alloc([128, T], F32); R_rs = Res("grs")
            for grp in range(3):
                wv, wr = load_w(W2d, 8, grp * 1024, 1024)
                for oc in range(8):
                    g = grp * 8 + oc
                    s = g % 2
                    b = 4 + (g % 4)
                    dense(wv, wr, oc * 128, 8, xb_fn, xb_res, b)
                    P.copy(stage[s][:, 0:3], convst[:, j, g, 0:3], R=(R_cst[j][g],), W=(R_stg[s],), eng="pool")
                    P.copy(stage[s][:, 3:3 + T], banks[b][:, :], R=PR[b], W=(R_stg[s],), eng="act")
                    P.ts(acc[s], stage[s][:, 0:T], convw[:, g, j * 4:j * 4 + 1], None, ALU.mult, R=(R_stg[s], R_c), W=(R_acc[s],))
                    for k in range(1, 4):
                        P.stt(acc[s], stage[s][:, k:k + T], convw[:, g, j * 4 + k:j * 4 + k + 1], acc[s], ALU.mult, ALU.add,
                              R=(R_stg[s], R_c, R_acc[s]), W=(R_acc[s],))
                    P.copy(convst[:, j, g, 0:3], stage[s][:, T:T + 3], R=(R_stg[s],), W=(R_cst[j][g],), eng="pool")
                    if g >= 16:
                        P.act(qkv_b[:, g, :], acc[s], AF.Silu, R=(R_acc[s],), W=(R_qkv[g],))
                    else:
                        P.act(sl[s], acc[s], AF.Silu, R=(R_acc[s],), W=(R_sl[s],))
                        P.act(sq, sl[s], AF.Square, R=(R_sl[s],), W=(R_sq,))
                        bq = g % 2
                        P.mm(banks[bq][:, :], ones_f, sq, R=(R_c, R_sq), W=PR[bq])
                        P.act(rs, banks[bq][:, :], AF.Sqrt, bias=ccol[:, 3:4], R=PR[bq] + [R_c], W=(R_rs,))
                        P.recip(rs, rs, R=(R_rs,), W=(R_rs,))
                        P.stt(qkv_b[:, g, :], sl[s], (128.0 ** -0.5 if g < 8 else 1.0), rs, ALU.mult, ALU.mult,
                              R=(R_sl[s], R_rs), W=(R_qkv[g],))
            wv, wr = load_w(W2d, 8, 3072, 1024)
            for oc in range(8):
                b = 4 + (oc % 4)
                dense(wv, wr, oc * 128, 8, xb_fn, xb_res, b)
                P.act(zs[:, oc, :], banks[b][:, :], AF.Silu, R=PR[b], W=(R_zs[oc],))
            wv, wr = load_w(W2d, 8, 4112, 512)
            for h in range(4):
                b = 4 + (h % 4)
                dense(wv, wr, h * 128, 8, xb_fn, xb_res, b)
                P.copy(xq_b[:, h, :], banks[b][:, :], R=PR[b], W=(R_xq,), eng="act")
            P.dma(wsmall, W2d.rearrange("(k p) n -> p k n", p=128)[:, :, 4096:4112], W=(R_wsmall,), q="pool")
            for tb in range(NB):
                for k in range(8):
                    P.mm(banks[2][:, tb * 16:(tb + 1) * 16], xb[:, k, tb * 128:(tb + 1) * 128], wsmall[:, k, :],
                         start=(k == 0), stop=(k == 7), R=(R_xb[k], R_wsmall), W=(PR[2][0],))
            sm = AR.alloc([128, 12, NB, 8], F32)
            R_sm = Res("gsm")
            bet, lnb, apre, spv, gg, gc, ngc, gcb, be, kd, egl = [sm[:, i, :, :] for i in range(11)]
            ba = banks[2][:, 0:NB * 16].rearrange("p (t k h) -> p t k h", t=NB, k=2)
            RWs = dict(R=(R_sm, R_c), W=(R_sm,))
            P.act(bet, ba[:, :, 0, :], AF.Sigmoid, R=(PR[2][0],), W=(R_sm,))
            P.act(lnb, bet, AF.Ln, **RWs)
            P.tt(apre, ba[:, :, 1, :], gpar[:, 16 + j * 8:24 + j * 8].unsqueeze(1).broadcast_to([128, NB, 8]), ALU.add, R=(PR[2][0], R_c), W=(R_sm,))
            P.act(spv, apre, AF.Exp, **RWs)
            P.act(spv, spv, AF.Ln, bias=ccol[:, 1:2], **RWs)
            P.tt(gg, spv, gpar[:, j * 8:j * 8 + 8].unsqueeze(1).broadcast_to([128, NB, 8]), ALU.mult, **RWs)
            for tb in range(NB):
                P.mm(banks[3][:, tb * 8:(tb + 1) * 8], triu_f, gg[:, tb, :], R=(R_c, R_sm), W=(PR[3][0],))
                P.mm(banks[3][:, 32 + tb * 8:32 + (tb + 1) * 8], trils_f, gg[:, tb, :], R=(R_c, R_sm), W=(PR[3][0],))
                P.mm(banks[3][:, 64 + tb * 8:64 + (tb + 1) * 8], ones_f, gg[:, tb, :], R=(R_c, R_sm), W=(PR[3][0],))
            p3 = lambda o: banks[3][:, o:o + NB * 8].rearrange("p (t h) -> p t h", t=NB)
            P.copy(gc, p3(0), R=(PR[3][0],), W=(R_sm,), eng="act")
            P.ts(ngc, p3(0), -1.0, None, ALU.mult, R=(PR[3][0],), W=(R_sm,))
            P.tt(gcb, gc, lnb, ALU.add, **RWs)
            P.act(be, gcb, AF.Exp, **RWs)
            P.act(kd, p3(32), AF.Exp, R=(PR[3][0],), W=(R_sm,))
            P.act(egl, p3(64), AF.Exp, R=(PR[3][0],), W=(R_sm,))
            NSL = gdn_slots
            def mk(dt):
                return [AR.alloc([128, 128], dt) for _ in range(NSL)]
            Kbe, Kd_, Vb, attnT, qs, TT, nWmT, vn = [mk(BF16) for _ in range(8)]
            ndg, bdg, DA, DB, EG, A0, A1, B0, B1, Q0, Q1 = [mk(F32) for _ in range(11)]
            names = ["Kbe", "Kd", "Vb", "attnT", "qs", "TT", "nWmT", "vn", "ndg", "bdg", "DA", "DB", "EG", "A0", "A1", "B0", "B1", "Q0", "Q1"]
            RT = [{n: Res(f"{n}{z}") for n in names} for z in range(NSL)]
            it = 0
            def g5(tb, h, z):
                tsl = slice(tb * 128, (tb + 1) * 128)
                r = RT[z]
                Y0, Y1 = 2 * z, 2 * z + 1
                qT, kT, vT = qkv_b[:, h, tsl], qkv_b[:, 8 + h, tsl], qkv_b[:, 16 + h, tsl]
                Rq, Rk, Rv = R_qkv[h], R_qkv[8 + h], R_qkv[16 + h]
                col = lambda t_: t_[:, tb, h:h + 1]
                y0b = bfview(Y0)
                reg = lambda b_, q_: banks[b_][:, q_ * 128:(q_ + 1) * 128]
                P.tr(y0b[:, 0:128], kT, ident_b, R=(Rk, R_c), W=(PR[Y0][0],))
                P.tr(y0b[:, 128:256], vT, ident_b, R=(Rv, R_c), W=(PR[Y0][0],))
                P.act(Kbe[z], y0b[:, 0:128], AF.Copy, scale=col(be), R=(PR[Y0][0], R_sm), W=(r["Kbe"],))
                P.act(Kd_[z], y0b[:, 0:128], AF.Copy, scale=col(kd), R=(PR[Y0][0], R_sm), W=(r["Kd"],))
                P.act(Vb[z], y0b[:, 128:256], AF.Copy, scale=col(bet), R=(PR[Y0][0], R_sm), W=(r["Vb"],))
                yield
                P.mm(reg(Y0, 1), kT, kT, R=(Rk,), W=(PR[Y0][1],))
                P.mm(reg(Y0, 2), kT, qT, R=(Rk, Rq), W=(PR[Y0][2],))
                yield
                P.ts(ndg[z], ident_f, col(ngc), None, ALU.mult, R=(R_c, R_sm), W=(r["ndg"],))
                P.ts(bdg[z], ident_f, col(gcb), None, ALU.mult, R=(R_c, R_sm), W=(r["bdg"],))
                P.mm(reg(Y1, 0), ones_f, ndg[z], start=True, stop=False, R=(R_c, r["ndg"]), W=(PR[Y1][0],))
                P.mm(reg(Y1, 0), ident_f, mls_f, start=False, stop=True, R=(R_c,), W=(PR[Y1][0],))
                P.mm(reg(Y1, 1), ones_f, bdg[z], start=True, stop=False, R=(R_c, r["bdg"]), W=(PR[Y1][1],))
                P.mm(reg(Y1, 1), ident_f, mus_f, start=False, stop=True, R=(R_c,), W=(PR[Y1][1],))
                P.mm(reg(Y1, 2), ones_f, ndg[z], start=True, stop=False, R=(R_c, r["ndg"]), W=(PR[Y1][2],))
                P.mm(reg(Y1, 2), ident_f, mui_f, start=False, stop=True, R=(R_c,), W=(PR[Y1][2],))
                P.mm(reg(Y0, 3), ones_f, ndg[z], R=(R_c, r["ndg"]), W=(PR[Y0][3],))
                yield
                P.act(DA[z], reg(Y1, 0), AF.Exp, bias=col(gcb), R=(PR[Y1][0], R_sm), W=(r["DA"],))
                P.act(DB[z], reg(Y1, 1), AF.Exp, bias=col(ngc), R=(PR[Y1][1], R_sm), W=(r["DB"],))
                P.act(EG[z], reg(Y0, 3), AF.Exp, scale=-1.0, R=(PR[Y0][3],), W=(r["EG"],))
                yield
                P.tt(A0[z], reg(Y0, 1), DA[z], ALU.mult, R=(PR[Y0][1], r["DA"]), W=(r["A0"],))
                P.tt(B0[z], reg(Y0, 1), DB[z], ALU.mult, R=(PR[Y0][1], r["DB"]), W=(r["B0"],))
                P.tt(Q0[z], ident_f, B0[z], ALU.subtract, R=(R_c, r["B0"]), W=(r["Q0"],))
                yield
                P.act(DA[z], reg(Y1, 2), AF.Exp, bias=col(ngc), scale=-1.0, R=(PR[Y1][2], R_sm, r["A0"]), W=(r["DA"],))
                P.tt(DB[z], reg(Y0, 2), DA[z], ALU.mult, R=(PR[Y0][2], r["DA"], r["B0"]), W=(r["DB"],))
                P.copy(attnT[z], DB[z], R=(r["DB"],), W=(r["attnT"],), eng="pool")
                P.tt(qs[z], qT, EG[z], ALU.mult, R=(Rq, r["EG"]), W=(r["qs"],), eng="pool")
                yield
                Ac, Bc, Qc = (A0, A1), (B0, B1), (Q0, Q1)
                An, Bn, Qn = ("A0", "A1"), ("B0", "B1"), ("Q0", "Q1")
                for lev in range(6):
                    ci, ni = lev % 2, (lev + 1) % 2
                    P.mm(reg(Y1, 0), Bc[ci][z], Ac[ci][z], R=(r[Bn[ci]], r[An[ci]]), W=(PR[Y1][0],))
                    if lev < 5:
                        P.mm(reg(Y1, 1), Ac[ci][z], Bc[ci][z], R=(r[Bn[ci]], r[An[ci]]), W=(PR[Y1][1],))
                    P.copy(Ac[ni][z], reg(Y1, 0), R=(PR[Y1][0],), W=(r[An[ni]],), eng="act")
                    if lev < 5:
                        P.copy(Bc[ni][z], reg(Y1, 1), R=(PR[Y1][1],), W=(r[Bn[ni]],), eng="dve")
                    P.mm(reg(Y1, 2), Ac[ni][z], Qc[ci][z], R=(r[An[ni]], r[Qn[ci]]), W=(PR[Y1][2],))
                    P.tt(Qc[ni][z], Qc[ci][z], reg(Y1, 2), ALU.add, R=(r[Qn[ci]], PR[Y1][2]), W=(r[Qn[ni]],))
                    yield
                    if lev == 5:
                        P.copy(TT[z], Qc[ni][z], R=(r[Qn[ni]],), W=(r["TT"],), eng="pool")
                P.mm(reg(Y0, 0), Kbe[z], TT[z], R=(r["Kbe"], r["TT"]), W=(PR[Y0][0],))
                P.act(nWmT[z], reg(Y0, 0), AF.Copy, scale=-1.0, R=(PR[Y0][0],), W=(r["nWmT"],))
                yield
                P.mm(reg(Y0, 1), TT[z], Vb[z], start=True, stop=False, R=(r["TT"], r["Vb"]), W=(PR[Y0][1],))
                P.mm(reg(Y0, 1), nWmT[z], Sbf[:, j, h, :], start=False, stop=True, R=(r["nWmT"], R_Sb[j][h]), W=(PR[Y0][1],))
                P.copy(vn[z], reg(Y0, 1), R=(PR[Y0][1],), W=(r["vn"],), eng="act")
                yield
                P.mm(reg(Y0, 3), Sbf[:, j, h, :], qs[z], start=True, stop=False, R=(R_Sb[j][h], r["qs"]), W=(PR[Y0][3],))
                P.mm(reg(Y0, 3), vn[z], attnT[z], start=False, stop=True, R=(r["vn"], r["attnT"]), W=(PR[Y0][3],))
                P.copy(oT[:, h, tsl], reg(Y0, 3), R=(PR[Y0][3],), W=(R_oT[h],), eng="act")
                yield
                P.mm(reg(Y0, 2), Kd_[z], vn[z], R=(r["Kd"], r["vn"]), W=(PR[Y0][2],))
                P.stt(Sst[:, j, h, :], Sst[:, j, h, :], col(egl), reg(Y0, 2), ALU.mult, ALU.add,
                      R=(R_S[j][h], R_sm, PR[Y0][2]), W=(R_S[j][h],))
                P.copy(Sbf[:, j, h, :], Sst[:, j, h, :], R=(R_S[j][h],), W=(R_Sb[j][h],), eng="pool")
            for tb in range(NB):
                for hp in range(0, 8, NSL):
                    alive = [g5(tb, hp + i_, i_) for i_ in range(NSL)]
                    while alive:
                        for g_ in list(alive):
                            try:
                                next(g_)
                            except StopIteration:
                                alive.remove(g_)
            for h in range(8):
                b = 4 + (h % 4)
                P.act(sq, oT[:, h, :], AF.Square, R=(R_oT[h],), W=(R_sq,))
                P.mm(banks[b][:, :], ones_f, sq, R=(R_c, R_sq), W=PR[b])
                P.act(rs, banks[b][:, :], AF.Sqrt, bias=ccol[:, 4:5], R=PR[b] + [R_c], W=(R_rs,))
                P.recip(rs, rs, R=(R_rs,), W=(R_rs,))
                P.tt(rs, rs, oT[:, h, :], ALU.mult, R=(R_rs, R_oT[h]), W=(R_rs,))
                P.stt(mix_b[:, h, :], rs, normg[:, j:j + 1], zs[:, h, :], ALU.mult, ALU.mult, R=(R_rs, R_c, R_zs[h]), W=(R_mix[h],))

        TMARK = UMARK
        for ti in range(NT):
            t0 = ti * T
            xin = AR.alloc([128, NB, D], F32)
            R_xin = Res("xin")
            P.dma(xin, dr["x"][t0:t0 + T, :].rearrange("(b p) f -> p b f", p=128), W=(R_xin,))
            for c in range(NCH):
                b = next_bank()
                for tb in range(NB):
                    P.tr(banks[b][:, tb * 128:(tb + 1) * 128], xin[:, tb, c * 128:(c + 1) * 128], ident_f,
                         R=(R_xin, R_c), W=(PR[b][tb],))
                P.copy(xT[:, c, :], banks[b][:, :], R=PR[b], W=(R_xT[c],), eng="act")
                P.copy(xb[:, c, :], banks[b][:, :], R=PR[b], W=(R_xb[c],), eng="act")
            P.barrier()
            AR.release(TMARK)

            for l in layers:
                j = l // 2
                mix_b = AR.alloc([128, 12, T], BF16)
                R_mix = [Res(f"mix{i}") for i in range(12)]
                xq_b = AR.alloc([128, 4, T], BF16)
                R_xq = Res("xq")
                if not mixers:
                    for h in range(8):
                        P.memset(mix_b[:, h, :], 0.0, W=(R_mix[h],), eng="pool")
                    if xattn:
                        W2d = dr["gdn_w_in"][j] if l % 2 == 0 else dr["s5_w_in"][j]
                        wv, wr = load_w(W2d, 8, 4112 if l % 2 == 0 else 1024, 512)
                        for h in range(4):
                            b = next_bank()
                            dense(wv, wr, h * 128, 8, xb_fn, xb_res, b)
                            P.copy(xq_b[:, h, :], banks[b][:, :], R=PR[b], W=(R_xq,), eng="act")
                elif l % 2 == 0:
                    gdn_layer(l, j, mix_b, R_mix, xq_b, R_xq)
                else:
                    s5_layer(l, j, mix_b, R_mix, xq_b, R_xq)
                if xattn:
                    cross_attention(l, xq_b, R_xq, mix_b, R_mix)
                else:
                    for h in range(4):
                        P.memset(mix_b[:, 8 + h, :], 0.0, W=(R_mix[8 + h],), eng="pool")
                out_proj_ln_mlp(l, mix_b, R_mix)

            xo = AR.alloc([128, NB, D], F32)
            R_xo = Res("xo")
            for tb in range(NB):
                for half in range(2):
                    b = next_bank()
                    for cc in range(4):
                        c = half * 4 + cc
                        P.tr(banks[b][:, cc * 128:(cc + 1) * 128], xT[:, c, tb * 128:(tb + 1) * 128], ident_f,
                             R=(R_xT[c], R_c), W=(PR[b][cc],))
                    P.copy(xo[:, tb, half * 512:(half + 1) * 512], banks[b][:, :], R=PR[b], W=(R_xo,),
                           eng=("act" if half == 0 else "dve"))
            P.dma(y[t0:t0 + T, :].rearrange("(b p) f -> p b f", p=128), xo, R=(R_xo,), W=())
            P.barrier()
            AR.release(TMARK)

        block = es.enter_context(nc.Block())
        P.emit(block)
        print("arena peak KB", AR.peak / 1024, "ops", {e: len(P.ops[e]) for e in ENGS}, flush=True)
    return nc


_CACHE = {}


def run_cores(inputs, L, nb, ncores=8, **kw):
    key = (L, tuple(sorted(kw.items())))
    nc = build_program(L, **kw)
    cs = host_consts()
    shared = {nm: np.ascontiguousarray(np.asarray(inputs[nm], dtype=np.float32)) for nm, _ in W_NAMES}
    shared.update(cs)
    in_maps = []
    for c in range(ncores):
        b = c % nb
        m = dict(shared)
        m["x"] = np.ascontiguousarray(np.asarray(inputs["x"][b, :L], dtype=np.float32))
        m["mem"] = np.ascontiguousarray(np.asarray(inputs["mem"][b], dtype=np.float32))
        in_maps.append(m)
    res = run_bass_kernel_spmd(nc, in_maps, core_ids=list(range(ncores)))
    return np.stack([res.results[b]["y"] for b in range(nb)], axis=0)


def kernel(**inputs):
    out = run_cores(inputs, 8192, 4, ncores=4)
    return out.astype(np.float32)
```

```python
import math
import numpy as np
import concourse.bass as bass
import concourse.mybir as mybir
from concourse.bass_utils import run_bass_kernel_spmd
from contextlib import ExitStack

F32 = mybir.dt.float32
BF16 = mybir.dt.bfloat16
AF = mybir.ActivationFunctionType
ALU = mybir.AluOpType

D = 1024
NCH = 8
T = 512
NB = T // 128
DEPTH = 4
MEM = 256
DFF = 4096
GDN_IN = 4624
S5_IN = 1536
DN_ALPHA = (2 * DEPTH) ** 0.25
LN_EPS = 1e-5
RMS_EPS = 1e-6
BIG = 60000.0
TWO_PI = 2.0 * math.pi

ENGS = ("pe", "act", "dve", "pool", "sp")
SAME_SYNC = True
NDS = 8
OPLIMIT = 10 ** 9


class Res:
    __slots__ = ("name", "w", "r", "excl")

    def __init__(self, name, excl=False):
        self.name = name
        self.w = None
        self.r = {}
        self.excl = excl


class _Op:
    __slots__ = ("fn", "waits", "need_inc", "seq", "dma")

    def __init__(self, fn, dma=None):
        self.fn = fn
        self.waits = []
        self.need_inc = False
        self.seq = 0
        self.dma = dma


class Prog:
    def __init__(self, nc, es):
        self.nc = nc
        self.ops = {e: [] for e in ENGS}
        self.waited = {e: {} for e in ENGS}
        self.sems = {e: es.enter_context(nc.semaphore("s_" + e)) for e in ENGS}
        self.dsems = {}
        for q in ("sp", "pool", "act"):
            for i in range(NDS):
                self.dsems[("d", q, i)] = es.enter_context(nc.semaphore(f"d_{q}{i}"))
        self.dma_cnt = {q: 0 for q in ENGS}
        self.dma_val = {}
        self.pending = None
        self.pend_done = {e: None for e in ENGS}
        self.nrec = 0
        self.limit = OPLIMIT
        self.log = []

    def barrier(self):
        if self.nrec > self.limit:
            return
        toks = []
        for e in ENGS:
            idx = len(self.ops[e]) - 1
            while idx >= 0 and self.ops[e][idx].dma is not None:
                idx -= 1
            if idx >= 0:
                toks.append(("o", e, idx, self.ops[e][idx]))
        for key, val in self.dma_val.items():
            toks.append(("d", key, val))
        self.pending = toks

    def _deps(self, R, W, eng=None):
        deps = []
        for r in R:
            if r.w is not None:
                deps.append(r.w)
            if r.excl:
                for k_, t_ in r.r.items():
                    if k_ != eng:
                        deps.append(t_)
        for w in W:
            if w.w is not None:
                deps.append(w.w)
            deps.extend(w.r.values())
        return deps

    def _add(self, eng, op, deps):
        wd = self.waited[eng]
        my_idx = len(self.ops[eng])
        if self.pending is not None and self.pend_done[eng] is not self.pending:
            self.pend_done[eng] = self.pending
            deps = list(deps) + self.pending
        best = {}
        for tok in deps:
            if tok[0] == "o":
                F, idx = tok[1], tok[2]
                if F == eng and (eng == "pe" or not SAME_SYNC) and op.dma is None:
                    continue
                if F not in best or best[F][2] < idx:
                    best[F] = tok
            else:
                key, val = tok[1], tok[2]
                if key not in best or best[key][2] < val:
                    best[key] = tok
        for k_, tok in best.items():
            if tok[0] == "o":
                F, idx = tok[1], tok[2]
                if wd.get(F, -1) >= idx:
                    continue
                wd[F] = idx
                tok[3].need_inc = True
                op.waits.append(tok)
            else:
                key, val = tok[1], tok[2]
                if wd.get(key, 0) >= val:
                    continue
                wd[key] = val
                op.waits.append(tok)
        self.ops[eng].append(op)
        return my_idx

    def op(self, eng, fn, R=(), W=()):
        self.nrec += 1
        if self.nrec > self.limit:
            return None
        import sys as _s
        f = _s._getframe(1) if OPLIMIT < 10 ** 9 else None
        if f is None:
            f = type("F", (), {"f_code": type("C", (), {"co_filename": "", "co_name": ""})(), "f_lineno": 0})()
        while f.f_code.co_filename == __file__ and f.f_code.co_name in ("mm", "tr", "act", "ts", "tt", "stt", "scan", "copy", "memset", "recip", "reduce", "dense", "op"):
            f = f.f_back
        self.log.append((self.nrec, eng, f.f_lineno))
        op = _Op(fn)
        deps = self._deps(R, W, eng)
        idx = self._add(eng, op, deps)
        tok = ("o", eng, idx, op)
        for w in W:
            w.w = tok
            w.r = {}
        for r in R:
            r.r[eng] = tok
        return op

    def dma(self, out, in_, R=(), W=(), q="sp"):
        self.nrec += 1
        if self.nrec > self.limit:
            return None
        i = self.dma_cnt[q] % NDS
        self.dma_cnt[q] += 1
        key = ("d", q, i)
        prev = self.dma_val.get(key, 0)
        val = prev + 16
        self.dma_val[key] = val
        deps = self._deps(R, W)
        if prev > 0:
            deps.append(("d", key, prev))
        op = _Op(lambda e: e.dma_start(out=out, in_=in_), dma=(key, val))
        self._add(q, op, deps)
        tok = ("d", key, val)
        for w in W:
            w.w = tok
            w.r = {}
        for r in R:
            r.r[key] = tok
        return op

    def mm(self, out, lhsT, rhs, start=True, stop=True, R=(), W=()):
        return self.op("pe", lambda e: e.matmul(out, lhsT=lhsT, rhs=rhs, start=start, stop=stop), R, W)

    def tr(self, out, in_, ident, R=(), W=()):
        return self.op("pe", lambda e: e.transpose(out, in_, ident), R, W)

    def act(self, out, in_, func, bias=None, scale=None, accum=None, R=(), W=(), eng="act"):
        kw = {}
        if bias is not None:
            kw["bias"] = bias
        if scale is not None:
            kw["scale"] = scale
        if accum is not None:
            kw["accum_out"] = accum
        if func == AF.Copy and kw:
            func = AF.Identity
        return self.op(eng, lambda e: e.activation(out, in_, func, **kw), R, W)

    def ts(self, out, in0, s1, s2, op0, op1=None, R=(), W=(), eng="dve", accum=None):
        kw = {}
        if accum is not None:
            kw["accum_out"] = accum
        if op1 is None:
            return self.op(eng, lambda e: e.tensor_scalar(out, in0, s1, None, op0, **kw), R, W)
        return self.op(eng, lambda e: e.tensor_scalar(out, in0, s1, s2, op0, op1, **kw), R, W)

    def tt(self, out, in0, in1, op, R=(), W=(), eng="dve"):
        return self.op(eng, lambda e: e.tensor_tensor(out, in0, in1, op), R, W)

    def stt(self, out, in0, scalar, in1, op0, op1, R=(), W=(), eng="dve"):
        return self.op(eng, lambda e: e.scalar_tensor_tensor(out, in0, scalar, in1, op0, op1), R, W)

    def scan(self, out, d0, d1, init, op0, op1, R=(), W=()):
        return self.op("dve", lambda e: e.tensor_tensor_scan(out, d0, d1, init, op0, op1), R, W)

    def copy(self, out, in_, R=(), W=(), eng="dve"):
        if eng == "act":
            return self.op("act", lambda e: e.copy(out, in_), R, W)
        return self.op(eng, lambda e: e.tensor_copy(out, in_), R, W)

    def memset(self, ap, val, W=(), eng="dve"):
        return self.op(eng, lambda e: e.memset(ap, val), (), W)

    def recip(self, out, in_, R=(), W=()):
        return self.op("dve", lambda e: e.reciprocal(out, in_), R, W)

    def reduce(self, out, in_, op, R=(), W=()):
        return self.op("dve", lambda e: e.tensor_reduce(out, in_, mybir.AxisListType.X, op), R, W)

    def emit(self, block):
        nc = self.nc
        for e in ENGS:
            s = 0
            for op in self.ops[e]:
                if op.need_inc:
                    s += 1
                    op.seq = s
        fin = _Op(lambda e: e.nop())
        for key, val in self.dma_val.items():
            if self.waited["sp"].get(key, 0) < val:
                fin.waits.append(("d", key, val))
        self.ops["sp"].append(fin)

        def run(eng_name):
            def body(e):
                for op in self.ops[eng_name]:
                    for tok in op.waits:
                        if tok[0] == "o":
                            e.wait_ge(self.sems[tok[1]], tok[3].seq)
                        else:
                            e.wait_ge(self.dsems[tok[1]], tok[2])
                    ins = op.fn(e)
                    if op.dma is not None:
                        ins.then_inc(self.dsems[op.dma[0]], 16)
                    elif op.need_inc:
                        ins.then_inc(self.sems[eng_name], 1)
            return body

        block.tensor(run("pe"))
        block.scalar(run("act"))
        block.vector(run("dve"))
        block.gpsimd(run("pool"))
        block.sync(run("sp"))


class Arena:
    def __init__(self, nc, es, nbytes):
        self.t = es.enter_context(nc.sbuf_tensor("arena", [128, nbytes // 4], F32))
        self.off = 0
        self.cap = nbytes
        self.peak = 0

    def alloc(self, shape, dt):
        esz = 2 if dt == BF16 else 4
        n = 1
        for s in shape[1:]:
            n *= s
        nb = (n * esz + 31) // 32 * 32
        assert self.off + nb <= self.cap, f"arena overflow {self.off}+{nb}>{self.cap}"
        v = self.t[:, self.off // 4:(self.off + nb) // 4]
        if dt != F32:
            v = v.bitcast(dt)
        v = v[0:shape[0], 0:n]
        if len(shape) == 3:
            v = v.rearrange("p (a b) -> p a b", a=shape[1])
        elif len(shape) == 4:
            v = v.rearrange("p (a b c) -> p a b c", a=shape[1], b=shape[2])
        self.off += nb
        self.peak = max(self.peak, self.off)
        return v

    def mark(self):
        return self.off

    def release(self, m):
        self.off = m


def host_consts():
    r = np.arange(128)[:, None]
    c = np.arange(128)[None, :]
    cs = {}
    cs["c_ident"] = np.eye(128, dtype=np.float32)
    cs["c_ones"] = np.ones((128, 128), np.float32)
    cs["c_triu"] = (r <= c).astype(np.float32)
    cs["c_trils"] = (r > c).astype(np.float32)
    cs["c_mls"] = np.where(r > c, 0.0, -BIG).astype(np.float32)
    cs["c_mus"] = np.where(r < c, 0.0, -BIG).astype(np.float32)
    cs["c_mui"] = np.where(r <= c, 0.0, BIG).astype(np.float32)
    cs["c_nvec"] = np.broadcast_to(np.arange(1, 129, dtype=np.float32)[None, :], (128, 128)).copy()
    return cs


W_NAMES = [
    ("w_kv_mem", [DEPTH, D, 1024]), ("w_o", [DEPTH, 1536, D]),
    ("ln1_g", [DEPTH, D]), ("ln1_b", [DEPTH, D]), ("ln2_g", [DEPTH, D]), ("ln2_b", [DEPTH, D]),
    ("mlp_w1", [DEPTH, D, DFF]), ("mlp_w2", [DEPTH, DFF, D]),
    ("gdn_w_in", [2, D, GDN_IN]), ("gdn_conv_w", [2, 4, 3072]), ("gdn_a_log", [2, 8]),
    ("gdn_dt_bias", [2, 8]), ("gdn_norm_g", [2, 128]),
    ("s5_w_in", [2, D, S5_IN]), ("s5_a_re", [2, 64, 64]), ("s5_a_im", [2, 64, 64]),
    ("s5_b_re", [2, 64, 64, 16]), ("s5_b_im", [2, 64, 64, 16]),
    ("s5_c_re", [2, 64, 16, 64]), ("s5_c_im", [2, 64, 16, 64]),
    ("s5_log_dt", [2, 64]), ("s5_d", [2, D]), ("s5_w_glu", [2, D, D]), ("s5_b_glu", [2, D]),
]


def build_program(L, layers=(0, 1, 2, 3), mixers=True, xattn=True, gdn_slots=2, dbg=False):
    NT = L // T
    nc = bass.Bass("TRN2", target_bir_lowering=False)
    dr = {}
    dr["x"] = nc.dram_tensor("x", [L, D], F32, kind="ExternalInput").ap()
    dr["mem"] = nc.dram_tensor("mem", [MEM, D], F32, kind="ExternalInput").ap()
    for nm, shp in W_NAMES:
        dr[nm] = nc.dram_tensor(nm, shp, F32, kind="ExternalInput").ap()
    for nm, arr in host_consts().items():
        dr[nm] = nc.dram_tensor(nm, list(arr.shape), F32, kind="ExternalInput").ap()
    y = nc.dram_tensor("y", [L, D], F32, kind="ExternalOutput").ap()
    s5c = nc.dram_tensor("s5c", [2, 8, 128, 4 * 4 * 128], BF16).ap()
    s5t = nc.dram_tensor("s5t", [2, 8, 128, 2 * 4 * 128], F32).ap()

    with ExitStack() as es:
        P = Prog(nc, es)
        es.enter_context(nc.allow_non_contiguous_dma("small parameter relayouts"))
        AR = Arena(nc, es, 204 * 1024)
        banks = [es.enter_context(nc.psum_tensor(f"ps{i}", [128, 512], F32)) for i in range(8)]
        PR = [[Res(f"ps{b}", excl=True)] * 4 for b in range(8)]

        def bfview(b):
            return banks[b][:, :].bitcast(BF16)

        def v3(ap, a):
            return ap.rearrange("p (a b) -> p a b", a=a)

        def sin_of(out, in_, shift, tf, ti, shape, RR, WW):
            xs = tf[0]; kf = tf[1]; mm_ = tf[2]
            rw = dict(R=tuple(RR) + tuple(WW), W=tuple(WW))
            P.ts(xs, in_, shift, None, ALU.add, **rw)
            P.ts(kf, xs, 1.0 / TWO_PI, None, ALU.mult, **rw)
            P.copy(ti, kf, **rw)
            P.copy(kf, ti, **rw)
            P.stt(xs, kf, -TWO_PI, xs, ALU.mult, ALU.add, **rw)
            P.ts(mm_, xs, math.pi, TWO_PI, ALU.is_gt, ALU.mult, **rw)
            P.tt(xs, xs, mm_, ALU.subtract, **rw)
            P.ts(mm_, xs, -math.pi, TWO_PI, ALU.is_lt, ALU.mult, **rw)
            P.tt(xs, xs, mm_, ALU.add, **rw)
            P.act(out, xs, AF.Sin, **rw)

        R_c = Res("consts")
        ident_f = AR.alloc([128, 128], F32)
        ident_b = AR.alloc([128, 128], BF16)
        ones_f = AR.alloc([128, 128], F32)
        triu_f = AR.alloc([128, 128], F32)
        trils_f = AR.alloc([128, 128], F32)
        mls_f = AR.alloc([128, 128], F32)
        mus_f = AR.alloc([128, 128], F32)
        mui_f = AR.alloc([128, 128], F32)
        nvec_f = AR.alloc([128, 128], F32)
        ccol = AR.alloc([128, 8], F32)
        lnp = AR.alloc([128, 8, 16], F32)
        convw = AR.alloc([128, 24, 8], F32)
        normg = AR.alloc([128, 2], F32)
        gpar = AR.alloc([128, 32], F32)
        s5dv = AR.alloc([128, 8, 4], F32)
        KT = AR.alloc([128, DEPTH, 4, MEM], BF16)
        VM = AR.alloc([128, DEPTH, 2, 512], BF16)
        R_KT, R_VM = Res("KT"), Res("VM")
        Sst = AR.alloc([128, 2, 8, 128], F32)
        Sbf = AR.alloc([128, 2, 8, 128], BF16)
        R_S = [[Res(f"S{j}_{h}") for h in range(8)] for j in range(2)]
        R_Sb = [[Res(f"Sb{j}_{h}") for h in range(8)] for j in range(2)]
        convst = AR.alloc([128, 2, 24, 4], F32)
        R_cst = [[Res(f"cst{j}_{c}") for c in range(24)] for j in range(2)]
        hst = AR.alloc([128, 2, 32, 2], F32)
        R_hst = [[Res(f"hst{j}_{q}") for q in range(32)] for j in range(2)]
        theta = AR.alloc([128, 2, 32], F32)
        rmag = AR.alloc([128, 2, 32], F32)
        R_s5p = Res("s5p")
        xT = AR.alloc([128, NCH, T], F32)
        xb = AR.alloc([128, NCH, T], BF16)
        R_xT = [Res(f"xT{c}") for c in range(NCH)]
        R_xb = [Res(f"xb{c}") for c in range(NCH)]
        WSLOT = 16 * 1024
        wbufs = [(AR.alloc([128, WSLOT // 2], BF16), Res(f"wbuf{i}")) for i in range(2)]
        wsmall = AR.alloc([128, 8, 16], BF16)
        R_wsmall = Res("wsmall")
        wctr = [0]

        def load_w(src2d, nk, c0, ncols):
            wt, rr = wbufs[wctr[0] % 2]
            wctr[0] += 1
            assert nk * ncols * 2 <= WSLOT
            v = wt[:, 0:nk * ncols].rearrange("p (k n) -> p k n", k=nk)
            src = src2d.rearrange("(k p) n -> p k n", p=128)[:, :, c0:c0 + ncols]
            P.dma(v, src, R=(), W=(rr,), q="pool")
            return v, rr

        for t_, nm in ((ident_f, "c_ident"), (ones_f, "c_ones"), (triu_f, "c_triu"), (trils_f, "c_trils"),
                       (mls_f, "c_mls"), (mus_f, "c_mus"), (mui_f, "c_mui"), (nvec_f, "c_nvec")):
            P.dma(t_, dr[nm], W=(R_c,))
        P.copy(ident_b, ident_f, R=(R_c,), W=(R_c,), eng="dve")
        P.memset(ccol[:, 0:1], -math.pi, W=(R_c,))
        P.memset(ccol[:, 1:2], 1.0, W=(R_c,))
        P.memset(ccol[:, 2:3], LN_EPS, W=(R_c,))
        P.memset(ccol[:, 3:4], 1e-6, W=(R_c,))
        P.memset(ccol[:, 4:5], 128.0 * RMS_EPS, W=(R_c,))
        UMARK = AR.mark()
        rows16 = AR.alloc([16, D], F32); rows8 = AR.alloc([8, 3072], F32); rows4 = AR.alloc([4, D], F32)
        rows2 = AR.alloc([2, 128], F32); rowg = AR.alloc([1, 32], F32)
        R_rows = Res("rows")
        for ki, nm in enumerate(("ln1_g", "ln1_b", "ln2_g", "ln2_b")):
            P.dma(rows16[ki * 4:(ki + 1) * 4, :], dr[nm], W=(R_rows,))
        P.dma(rows8, dr["gdn_conv_w"].rearrange("j k n -> (j k) n"), W=(R_rows,))
        P.dma(rows4[0:2, :], dr["s5_d"], W=(R_rows,))
        P.dma(rows4[2:4, :], dr["s5_b_glu"], W=(R_rows,))
        P.dma(rows2, dr["gdn_norm_g"], W=(R_rows,))
        P.dma(rowg[:, 0:16], dr["gdn_a_log"].rearrange("(o j) h -> o (j h)", o=1), W=(R_rows,))
        P.dma(rowg[:, 16:32], dr["gdn_dt_bias"].rearrange("(o j) h -> o (j h)", o=1), W=(R_rows,))
        for c in range(8):
            P.tr(banks[0][:, c * 16:(c + 1) * 16], rows16[:, c * 128:(c + 1) * 128], ident_f[0:16, 0:16], R=(R_rows, R_c), W=(PR[0][0],))
            P.tr(banks[1][:, c * 4:(c + 1) * 4], rows4[:, c * 128:(c + 1) * 128], ident_f[0:4, 0:4], R=(R_rows, R_c), W=(PR[1][0],))
        for g in range(24):
            P.tr(banks[2][:, g * 8:(g + 1) * 8], rows8[:, g * 128:(g + 1) * 128], ident_f[0:8, 0:8], R=(R_rows, R_c), W=(PR[2][0], PR[2][1]))
        P.tr(banks[3][:, 0:2], rows2, ident_f[0:2, 0:2], R=(R_rows, R_c), W=(PR[3][0],))
        P.mm(banks[3][:, 128:160], ones_f[0:1, :], rowg, R=(R_rows, R_c), W=(PR[3][1],))
        P.copy(lnp.rearrange("p c k -> p (c k)"), banks[0][:, 0:128], R=(PR[0][0],), W=(R_c,), eng="act")
        P.copy(s5dv.rearrange("p c k -> p (c k)"), banks[1][:, 0:32], R=(PR[1][0],), W=(R_c,), eng="act")
        P.copy(convw.rearrange("p g k -> p (g k)"), banks[2][:, 0:192], R=(PR[2][0], PR[2][1]), W=(R_c,), eng="act")
        P.act(normg, banks[3][:, 0:2], AF.Identity, scale=math.sqrt(128.0), R=(PR[3][0],), W=(R_c,))
        P.act(gpar[:, 0:16], banks[3][:, 128:144], AF.Exp, R=(PR[3][1],), W=(R_c,))
        P.ts(gpar[:, 0:16], gpar[:, 0:16], -1.0, None, ALU.mult, R=(R_c,), W=(R_c,))
        P.copy(gpar[:, 16:32], banks[3][:, 144:160], R=(PR[3][1],), W=(R_c,), eng="act")
        P.barrier()
        AR.release(UMARK)
        P.memset(Sst, 0.0, W=[r for rr in R_S for r in rr])
        P.memset(Sbf, 0.0, W=[r for rr in R_Sb for r in rr], eng="pool")
        P.memset(convst, 0.0, W=[r for rr in R_cst for r in rr])
        P.memset(hst, 0.0, W=[r for rr in R_hst for r in rr])

        if xattn:
            memtok = AR.alloc([128, 2, D], F32)
            memT = AR.alloc([128, NCH, MEM], BF16)
            R_mt, R_mT = Res("memtok"), Res("memT")
            P.dma(memtok, dr["mem"].rearrange("(b p) f -> p b f", p=128), W=(R_mt,))
            for c in range(NCH):
                for mb in range(2):
                    P.tr(banks[0][:, mb * 128:(mb + 1) * 128], memtok[:, mb, c * 128:(c + 1) * 128], ident_f,
                         R=(R_mt, R_c), W=(PR[0][mb],))
                P.copy(memT[:, c, :], banks[0][:, 0:256], R=(PR[0][0], PR[0][1]), W=(R_mT,), eng="act")
            for l in layers:
                wv, wr = load_w(dr["w_kv_mem"][l], 8, 0, 1024)
                for h in range(4):
                    for k in range(8):
                        P.mm(banks[1][:, 0:256], wv[:, k, h * 128:(h + 1) * 128], memT[:, k, :],
                             start=(k == 0), stop=(k == 7), R=(wr, R_mT), W=(PR[1][0], PR[1][1]))
                    P.copy(KT[:, l, h, :], banks[1][:, 0:256], R=(PR[1][0], PR[1][1]), W=(R_KT,), eng="act")
                for mb in range(2):
                    for k in range(8):
                        P.mm(banks[2][:, :], memT[:, k, mb * 128:(mb + 1) * 128], wv[:, k, 512:1024],
                             start=(k == 0), stop=(k == 7), R=(wr, R_mT), W=PR[2])
                    P.copy(VM[:, l, mb, :], banks[2][:, :], R=PR[2], W=(R_VM,), eng="act")
            P.barrier()
            AR.release(UMARK)

        s5_layers = [l for l in layers if l % 2 == 1]
        R_s5c = Res("s5c_dram")
        if mixers and s5_layers:
            A2 = AR.alloc([32, 2, 128], F32)
            LD = AR.alloc([32, 2], F32)
            LDb = AR.alloc([32, 128], F32)
            are = AR.alloc([128, 32], F32); aim = AR.alloc([128, 32], F32); dtt = AR.alloc([128, 32], F32)
            t1 = AR.alloc([128, 32], F32); t2 = AR.alloc([128, 32], F32); t3 = AR.alloc([128, 32], F32)
            abr = AR.alloc([128, 32], F32); abi = AR.alloc([128, 32], F32)
            fre = AR.alloc([128, 32], F32); fim = AR.alloc([128, 32], F32)
            bre = AR.alloc([128, 32, 16], F32); bim = AR.alloc([128, 32, 16], F32)
            bbr = AR.alloc([128, 32, 16], F32); bbi = AR.alloc([128, 32, 16], F32); tb16 = AR.alloc([128, 32, 16], F32)
            Ci = AR.alloc([16, 2, 64, 64], F32)
            cre = AR.alloc([128, 32, 16], F32); cim = AR.alloc([128, 32, 16], F32)
            Z = AR.alloc([128, 4, 2, 128], F32)
            CB = AR.alloc([128, 4, 4, 128], BF16)
            R_p, R_Z, R_CB = Res("s5prep"), Res("Z"), Res("CB")
            rr1 = AR.alloc([128, 32], F32); rr2 = AR.alloc([128, 32], F32); rr3 = AR.alloc([128, 32], F32)
            rri = AR.alloc([128, 32], mybir.dt.int32)
            argt = AR.alloc([128, 4, 128], F32); tabt = AR.alloc([128, 2, 4, 128], F32)
            ra1 = AR.alloc([128, 4, 128], F32); ra2 = AR.alloc([128, 4, 128], F32); ra3 = AR.alloc([128, 4, 128], F32)
            rai = AR.alloc([128, 4, 128], mybir.dt.int32)
            RW = dict(R=(R_p,), W=(R_p,))
            for l in s5_layers:
                j = l // 2
                P.dma(A2[:, 0, :], dr["s5_a_re"][j].rearrange("(q g) p -> q (g p)", g=2), W=(R_p,))
                P.dma(A2[:, 1, :], dr["s5_a_im"][j].rearrange("(q g) p -> q (g p)", g=2), W=(R_p,))
                P.dma(LD, dr["s5_log_dt"][j].rearrange("(q g) -> q g", g=2), W=(R_p,))
                for g2 in range(2):
                    ps_ = slice(g2 * 64, (g2 + 1) * 64)
                    for qh in range(2):
                        qs_ = slice(qh * 16, (qh + 1) * 16)
                        P.dma(bre[ps_, qs_, :], dr["s5_b_re"][j].rearrange("(q g) p i -> g p q i", g=2)[g2][:, qs_, :], W=(R_p,))
                        P.dma(bim[ps_, qs_, :], dr["s5_b_im"][j].rearrange("(q g) p i -> g p q i", g=2)[g2][:, qs_, :], W=(R_p,))
                P.dma(Ci[:, 0, :, :], dr["s5_c_re"][j].rearrange("g i p -> i g p"), W=(R_p,))
                P.dma(Ci[:, 1, :, :], dr["s5_c_im"][j].rearrange("g i p -> i g p"), W=(R_p,))
                for ri, dst in ((0, are), (1, aim)):
                    P.tr(banks[4][:, ri * 32:(ri + 1) * 32], A2[:, ri, :], ident_f[0:32, 0:32], R=(R_p, R_c), W=(PR[4][0],))
                    P.copy(dst, banks[4][:, ri * 32:(ri + 1) * 32], R=(PR[4][0],), W=(R_p,), eng="act")
                P.copy(v3(LDb, 2), LD.unsqueeze(2).broadcast_to([32, 2, 64]), **RW)
                P.tr(banks[4][:, 64:96], LDb, ident_f[0:32, 0:32], R=(R_p, R_c), W=(PR[4][0],))
                P.copy(dtt, banks[4][:, 64:96], R=(PR[4][0],), W=(R_p,), eng="act")
                for ri, dst in ((0, cre), (1, cim)):
                    for q in range(32):
                        P.tr(banks[5 + ri][:, q * 16:(q + 1) * 16], Ci[:, ri, 2 * q:2 * q + 2, :].rearrange("i g p -> i (g p)"),
                             ident_f[0:16, 0:16], R=(R_p, R_c), W=(PR[5 + ri][q // 8],))
                    P.copy(dst.rearrange("p q i -> p (q i)"), banks[5 + ri][:, :], R=PR[5 + ri], W=(R_p,), eng="act")
                P.act(dtt, dtt, AF.Exp, **RW)
                P.tt(t1, are, dtt, ALU.mult, **RW)
                P.act(rmag[:, j, :], t1, AF.Exp, R=(R_p,), W=(R_p, R_s5p))
                P.tt(theta[:, j, :], aim, dtt, ALU.mult, R=(R_p,), W=(R_p, R_s5p))
                sin_of(t2, theta[:, j, :], 0.0, (rr1, rr2, rr3), rri, None, (R_s5p,), (R_p,))
                sin_of(t3, theta[:, j, :], 0.5 * math.pi, (rr1, rr2, rr3), rri, None, (R_s5p,), (R_p,))
                for c in range(8):
                    P.tt(argt, theta[:, j, 4 * c:4 * c + 4].unsqueeze(2).broadcast_to([128, 4, 128]),
                         nvec_f.unsqueeze(1).broadcast_to([128, 4, 128]), ALU.mult, R=(R_s5p, R_c, R_p), W=(R_p,))
                    sin_of(tabt[:, 0, :, :], argt, 0.5 * math.pi, (ra1, ra2, ra3), rai, None, (R_s5p,), (R_p,))
                    sin_of(tabt[:, 1, :, :], argt, 0.0, (ra1, ra2, ra3), rai, None, (R_s5p,), (R_p,))
                    P.dma(s5t[j, c].rearrange("p (k q n) -> p k q n", k=2, q=4), tabt, R=(R_p,), W=(R_s5c,))
                P.tt(abr, rmag[:, j, :], t3, ALU.mult, R=(R_p, R_s5p), W=(R_p,))
                P.tt(abi, rmag[:, j, :], t2, ALU.mult, R=(R_p, R_s5p), W=(R_p,))
                P.ts(abr, abr, -1.0, None, ALU.add, **RW)
                P.tt(t1, are, are, ALU.mult, **RW)
                P.tt(t2, aim, aim, ALU.mult, **RW)
                P.tt(t1, t1, t2, ALU.add, **RW)
                P.recip(t1, t1, **RW)
                P.tt(t2, abr, are, ALU.mult, **RW)
                P.tt(t3, abi, aim, ALU.mult, **RW)
                P.tt(t2, t2, t3, ALU.add, **RW)
                P.tt(fre, t2, t1, ALU.mult, **RW)
                P.tt(t2, abi, are, ALU.mult, **RW)
                P.tt(t3, abr, aim, ALU.mult, **RW)
                P.tt(t2, t2, t3, ALU.subtract, **RW)
                P.tt(fim, t2, t1, ALU.mult, **RW)
                fre_b = fre.unsqueeze(2).broadcast_to([128, 32, 16])
                fim_b = fim.unsqueeze(2).broadcast_to([128, 32, 16])
                P.tt(bbr, bre, fre_b, ALU.mult, **RW)
                P.tt(tb16, bim, fim_b, ALU.mult, **RW)
                P.tt(bbr, bbr, tb16, ALU.subtract, **RW)
                P.tt(bbi, bim, fre_b, ALU.mult, **RW)
                P.tt(tb16, bre, fim_b, ALU.mult, **RW)
                P.tt(bbi, bbi, tb16, ALU.add, **RW)
                P.ts(cim, cim, -1.0, None, ALU.mult, **RW)
                for c in range(8):
                    P.memset(Z, 0.0, W=(R_Z,))
                    P.memset(CB, 0.0, W=(R_CB,), eng="pool")
                    for qq in range(4):
                        q = 4 * c + qq
                        for g2 in range(2):
                            ps_ = slice(g2 * 64, (g2 + 1) * 64)
                            gl = 2 * qq + g2
                            cs_ = slice(gl * 16, gl * 16 + 16)
                            P.copy(Z[ps_, qq, 0, cs_], bbr[ps_, q, :], R=(R_p,), W=(R_Z,))
                            P.copy(Z[ps_, qq, 1, cs_], bbi[ps_, q, :], R=(R_p,), W=(R_Z,))
                            P.copy(CB[ps_, qq, 2, cs_], cre[ps_, q, :], R=(R_p,), W=(R_CB,), eng="pool")
                            P.copy(CB[ps_, qq, 3, cs_], cim[ps_, q, :], R=(R_p,), W=(R_CB,), eng="pool")
                    for qq in range(4):
                        for ri in range(2):
                            P.tr(banks[7][:, ri * 128:(ri + 1) * 128], Z[:, qq, ri, :], ident_f, R=(R_Z, R_c), W=(PR[7][ri],))
                        for ri in range(2):
                            P.copy(CB[:, qq, ri, :], banks[7][:, ri * 128:(ri + 1) * 128], R=(PR[7][ri],), W=(R_CB,), eng="act")
                    P.dma(s5c[j, c].rearrange("p (q k n) -> p q k n", q=4, k=4), CB, R=(R_CB,), W=(R_s5c,))
            P.barrier()
            AR.release(UMARK)

        pctr = [0]

        def next_bank(avoid=()):
            while True:
                b = pctr[0] % 8
                pctr[0] += 1
                if b not in avoid:
                    return b

        def dense(wv, wr, col0, nk, rhs_fn, rhs_res, b):
            for k in range(nk):
                P.mm(banks[b][:, :], wv[:, k, col0:col0 + 128], rhs_fn(k), start=(k == 0), stop=(k == nk - 1),
                     R=(wr,) + tuple(rhs_res(k)), W=PR[b])

        def xb_fn(k):
            return xb[:, k, :]

        def xb_res(k):
            return (R_xb[k],)

        def layer_norm(l, which):
            gi, bi = (0, 1) if which == 1 else (2, 3)
            m0 = AR.mark()
            sq = [AR.alloc([128, T], F32) for _ in range(2)]
            R_sq = [Res("lnsq0"), Res("lnsq1")]
            mean = AR.alloc([128, T], F32); rstd = AR.alloc([128, T], F32); tmp = AR.alloc([128, T], F32)
            R_m, R_r, R_t = Res("lnmean"), Res("lnrstd"), Res("lntmp")
            bs = next_bank()
            bq = next_bank()
            for c in range(NCH):
                P.mm(banks[bs][:, :], ones_f, xT[:, c, :], start=(c == 0), stop=(c == NCH - 1), R=(R_c, R_xT[c]), W=PR[bs])
            for c in range(NCH):
                P.act(sq[c % 2], xT[:, c, :], AF.Square, R=(R_xT[c],), W=(R_sq[c % 2],))
                P.mm(banks[bq][:, :], ones_f, sq[c % 2], start=(c == 0), stop=(c == NCH - 1), R=(R_c, R_sq[c % 2]), W=PR[bq])
            P.act(mean, banks[bs][:, :], AF.Copy, scale=1.0 / D, R=PR[bs], W=(R_m,))
            P.tt(tmp, mean, mean, ALU.mult, R=(R_m,), W=(R_t,))
            P.stt(rstd, banks[bq][:, :], 1.0 / D, tmp, ALU.mult, ALU.subtract, R=PR[bq] + [R_t], W=(R_r,))
            P.act(rstd, rstd, AF.Sqrt, bias=ccol[:, 2:3], R=(R_r, R_c), W=(R_r,))
            P.recip(rstd, rstd, R=(R_r,), W=(R_r,))
            for c in range(NCH):
                P.tt(tmp, xT[:, c, :], mean, ALU.subtract, R=(R_xT[c], R_m), W=(R_t,))
                P.tt(tmp, tmp, rstd, ALU.mult, R=(R_t, R_r), W=(R_t,))
                P.act(xT[:, c, :], tmp, AF.Identity, bias=lnp[:, c, bi * 4 + l:bi * 4 + l + 1], scale=lnp[:, c, gi * 4 + l:gi * 4 + l + 1],
                      R=(R_t, R_c), W=(R_xT[c],))
                P.copy(xb[:, c, :], xT[:, c, :], R=(R_xT[c],), W=(R_xb[c],), eng="pool")
            P.barrier()
            AR.release(m0)

        def residual_evac(b, c):
            P.stt(xT[:, c, :], xT[:, c, :], DN_ALPHA, banks[b][:, :], ALU.mult, ALU.add, R=[R_xT[c]] + PR[b], W=(R_xT[c],))

        def cross_attention(l, xq_b, R_xq, mix_b, R_mix):
            sc = 128.0 ** -0.5
            Pf = [AR.alloc([128, MEM], F32) for _ in range(2)]
            Pn = [AR.alloc([128, MEM], BF16) for _ in range(2)]
            PT = [AR.alloc([128, 2, 128], BF16) for _ in range(2)]
            st = [AR.alloc([128, 4], F32) for _ in range(2)]
            R_Pf = [Res("Pf0"), Res("Pf1")]; R_Pn = [Res("Pn0"), Res("Pn1")]; R_PT = [Res("PT0"), Res("PT1")]
            R_st = [Res("st0"), Res("st1")]
            it = 0
            def xa(h, tb, s, bo):
                bsc = next_bank(avoid=(bo,))
                tsl = slice(tb * 128, (tb + 1) * 128)
                P.mm(banks[bsc][:, 0:MEM], xq_b[:, h, tsl], KT[:, l, h, :], R=(R_xq, R_KT), W=(PR[bsc][0], PR[bsc][1]))
                yield
                P.reduce(st[s][:, 0:1], banks[bsc][:, 0:MEM], ALU.max, R=(PR[bsc][0], PR[bsc][1]), W=(R_st[s],))
                P.ts(st[s][:, 1:2], st[s][:, 0:1], -sc, None, ALU.mult, R=(R_st[s],), W=(R_st[s],))
                yield
                P.act(Pf[s], banks[bsc][:, 0:MEM], AF.Exp, bias=st[s][:, 1:2], scale=sc,
                      R=(PR[bsc][0], PR[bsc][1], R_st[s]), W=(R_Pf[s],))
                yield
                P.reduce(st[s][:, 2:3], Pf[s], ALU.add, R=(R_Pf[s],), W=(R_st[s],))
                P.recip(st[s][:, 3:4], st[s][:, 2:3], R=(R_st[s],), W=(R_st[s],))
                yield
                P.act(Pn[s], Pf[s], AF.Copy, scale=st[s][:, 3:4], R=(R_Pf[s], R_st[s]), W=(R_Pn[s],))
                yield
                bt = next_bank(avoid=(bo,))
                btb = bfview(bt)
                for mb in range(2):
                    P.tr(btb[:, mb * 128:(mb + 1) * 128], Pn[s][:, mb * 128:(mb + 1) * 128], ident_b,
                         R=(R_Pn[s], R_c), W=(PR[bt][0],))
                yield
                P.copy(PT[s].rearrange("p a b -> p (a b)"), btb[:, 0:256], R=(PR[bt][0],), W=(R_PT[s],), eng="act")
                yield
                for mb in range(2):
                    P.mm(banks[bo][:, tsl], VM[:, l, mb, h * 128:(h + 1) * 128], PT[s][:, mb, :],
                         start=(mb == 0), stop=(mb == 1), R=(R_VM, R_PT[s]), W=(PR[bo][tb],))

            for h in range(4):
                bo = next_bank()
                for tp in range(0, NB, 2):
                    alive = [xa(h, tp + i_, i_, bo) for i_ in range(2)]
                    while alive:
                        for g_ in list(alive):
                            try:
                                next(g_)
                            except StopIteration:
                                alive.remove(g_)
                P.copy(mix_b[:, 8 + h, :], banks[bo][:, :], R=PR[bo], W=(R_mix[8 + h],), eng="act")

        def out_proj_ln_mlp(l, mix_b, R_mix):
            for grp in range(2):
                wv, wr = load_w(dr["w_o"][l], 12, grp * 512, 512)
                for oc in range(4):
                    b = next_bank()
                    dense(wv, wr, oc * 128, 12, lambda k: mix_b[:, k, :], lambda k: (R_mix[k],), b)
                    residual_evac(b, grp * 4 + oc)
            P.barrier()
            AR.release(TMARK)
            layer_norm(l, 1)
            hb = AR.alloc([128, 32, T], BF16)
            R_h = [Res(f"h{i}") for i in range(32)]
            rl = [AR.alloc([128, T], F32) for _ in range(2)]
            R_rl = [Res("rl0"), Res("rl1")]
            for grp in range(4):
                wv, wr = load_w(dr["mlp_w1"][l], 8, grp * 1024, 1024)
                for oc in range(8):
                    b = next_bank()
                    g = grp * 8 + oc
                    dense(wv, wr, oc * 128, 8, xb_fn, xb_res, b)
                    P.act(rl[g % 2], banks[b][:, :], AF.Relu, R=PR[b], W=(R_rl[g % 2],))
                    P.tt(hb[:, g, :], rl[g % 2], rl[g % 2], ALU.mult, R=(R_rl[g % 2],), W=(R_h[g],), eng="dve")
            for grp in range(4):
                wv, wr = load_w(dr["mlp_w2"][l], 32, grp * 256, 256)
                for oc in range(2):
                    b = next_bank()
                    dense(wv, wr, oc * 128, 32, lambda k: hb[:, k, :], lambda k: (R_h[k],), b)
                    residual_evac(b, grp * 2 + oc)
            P.barrier()
            AR.release(TMARK)
            layer_norm(l, 2)

        def s5_layer(l, j, mix_b, R_mix, xq_b, R_xq):
            W2d = dr["s5_w_in"][j]
            zg_f = AR.alloc([128, NCH, T], F32); zg_b = AR.alloc([128, NCH, T], BF16)
            R_zf = [Res(f"zgf{c}") for c in range(NCH)]; R_zb = [Res(f"zgb{c}") for c in range(NCH)]
            u_f = [AR.alloc([128, T], F32) for _ in range(2)]; u_b = [AR.alloc([128, T], BF16) for _ in range(2)]
            R_u = [Res("u0"), Res("u1")]
            cbuf = [AR.alloc([128, 4, 4, 128], BF16) for _ in range(2)]; R_cb = [Res("cb0"), Res("cb1")]
            tabs = [AR.alloc([128, 2, 4, 128], F32) for _ in range(2)]
            Ctab = [t_[:, 0, :, :] for t_ in tabs]; Stab = [t_[:, 1, :, :] for t_ in tabs]
            nSl = [AR.alloc([128, 4], F32) for _ in range(2)]
            R_tab = [Res("tab0"), Res("tab1")]
            NS = 2
            ta = [AR.alloc([128, T], F32) for _ in range(NS)]; tbb = [AR.alloc([128, T], F32) for _ in range(NS)]
            Wr = [AR.alloc([128, T], F32) for _ in range(NS)]; Wi = [AR.alloc([128, T], F32) for _ in range(NS)]
            gr = [AR.alloc([128, T], F32) for _ in range(NS)]; gi = [AR.alloc([128, T], F32) for _ in range(NS)]
            hrb = [AR.alloc([128, T], BF16) for _ in range(NS)]; hib = [AR.alloc([128, T], BF16) for _ in range(NS)]
            cr = [AR.alloc([128, 4, 4], F32) for _ in range(NS)]
            R_ta = [Res(f"ta{i}") for i in range(NS)]; R_tb = [Res(f"tb{i}") for i in range(NS)]
            R_W = [Res(f"W{i}") for i in range(NS)]; R_g = [Res(f"g{i}") for i in range(NS)]
            R_hb = [Res(f"hb{i}") for i in range(NS)]; R_cr = [Res(f"cr{i}") for i in range(NS)]
            yd = AR.alloc([128, T], F32); x2 = AR.alloc([128, T], F32); sg = AR.alloc([128, T], F32)
            R_yd, R_x2, R_sg = Res("yd"), Res("x2"), Res("sg")
            wv_u, wr_u = load_w(W2d, 8, 0, 1024)
            pc = 0
            for c in range(NCH):
                s = c % 2
                bu = 2 + s
                bY = s
                dense(wv_u, wr_u, c * 128, 8, xb_fn, xb_res, bu)
                P.copy(u_f[s], banks[bu][:, :], R=PR[bu], W=(R_u[s],), eng="act")
                P.copy(u_b[s], u_f[s], R=(R_u[s],), W=(R_u[s],), eng="pool")
                P.dma(cbuf[s], s5c[j, c].rearrange("p (q k n) -> p q k n", q=4, k=4), R=(R_s5c,), W=(R_cb[s],))
                P.dma(tabs[s], s5t[j, c].rearrange("p (k q n) -> p k q n", k=2, q=4), R=(R_s5c,), W=(R_tab[s],))
                P.ts(nSl[s], Stab[s][:, :, 127], -1.0, None, ALU.mult, R=(R_tab[s],), W=(R_tab[s],))
                def s5pair(qq, z):
                    q = 4 * c + qq
                    bA, bB = (4, 5) if z == 0 else (6, 7)
                    P.mm(banks[bA][:, :], cbuf[s][:, qq, 0, :], u_b[s], R=(R_cb[s], R_u[s]), W=PR[bA])
                    P.mm(banks[bB][:, :], cbuf[s][:, qq, 1, :], u_b[s], R=(R_cb[s], R_u[s]), W=PR[bB])
                    yield
                    Cb = Ctab[s][:, qq, :].unsqueeze(1).broadcast_to([128, 4, 128])
                    Sb_ = Stab[s][:, qq, :].unsqueeze(1).broadcast_to([128, 4, 128])
                    pA, pB = v3(banks[bA][:, :], 4), v3(banks[bB][:, :], 4)
                    P.tt(v3(ta[z], 4), pA, Cb, ALU.mult, R=PR[bA] + [R_tab[s]], W=(R_ta[z],))
                    P.tt(v3(tbb[z], 4), pB, Sb_, ALU.mult, R=PR[bB] + [R_tab[s]], W=(R_tb[z],))
                    P.tt(Wr[z], ta[z], tbb[z], ALU.add, R=(R_ta[z], R_tb[z]), W=(R_W[z],), eng="pool")
                    yield
                    P.tt(v3(ta[z], 4), pB, Cb, ALU.mult, R=PR[bB] + [R_tab[s]], W=(R_ta[z],))
                    P.tt(v3(tbb[z], 4), pA, Sb_, ALU.mult, R=PR[bA] + [R_tab[s]], W=(R_tb[z],))
                    P.tt(Wi[z], ta[z], tbb[z], ALU.subtract, R=(R_ta[z], R_tb[z]), W=(R_W[z],), eng="pool")
                    yield
                    rdec = rmag[:, j, q:q + 1].broadcast_to([128, 128])
                    c_l = Ctab[s][:, qq, 127:128]; s_l = Stab[s][:, qq, 127:128]; ns_l = nSl[s][:, qq:qq + 1]
                    for blk in range(NB):
                        bs_ = slice(blk * 128, (blk + 1) * 128)
                        if blk == 0:
                            ir, ii, Rin = hst[:, j, q, 0:1], hst[:, j, q, 1:2], R_hst[j][q]
                        else:
                            ir, ii, Rin = cr[z][:, blk - 1, 0:1], cr[z][:, blk - 1, 1:2], R_cr[z]
                        P.scan(gr[z][:, bs_], rdec, Wr[z][:, bs_], ir, ALU.mult, ALU.add, R=(R_s5p, R_W[z], Rin), W=(R_g[z],))
                        P.scan(gi[z][:, bs_], rdec, Wi[z][:, bs_], ii, ALU.mult, ALU.add, R=(R_s5p, R_W[z], Rin), W=(R_g[z],))
                        yield
                        gr_l = gr[z][:, blk * 128 + 127:blk * 128 + 128]
                        gi_l = gi[z][:, blk * 128 + 127:blk * 128 + 128]
                        if blk < NB - 1:
                            dr_, di_, Rout = cr[z][:, blk, 0:1], cr[z][:, blk, 1:2], R_cr[z]
                        else:
                            dr_, di_, Rout = hst[:, j, q, 0:1], hst[:, j, q, 1:2], R_hst[j][q]
                        P.ts(cr[z][:, blk, 2:3], gr_l, c_l, None, ALU.mult, R=(R_g[z], R_tab[s]), W=(R_cr[z],))
                        P.ts(cr[z][:, blk, 3:4], gi_l, c_l, None, ALU.mult, R=(R_g[z], R_tab[s]), W=(R_cr[z],))
                        P.stt(dr_, gi_l, ns_l, cr[z][:, blk, 2:3], ALU.mult, ALU.add, R=(R_g[z], R_tab[s], R_cr[z]), W=(Rout,))
                        P.stt(di_, gr_l, s_l, cr[z][:, blk, 3:4], ALU.mult, ALU.add, R=(R_g[z], R_tab[s], R_cr[z]), W=(Rout,))
                        yield
                    P.tt(v3(ta[z], 4), v3(gr[z], 4), Cb, ALU.mult, R=(R_g[z], R_tab[s]), W=(R_ta[z],), eng="pool")
                    P.tt(v3(tbb[z], 4), v3(gi[z], 4), Sb_, ALU.mult, R=(R_g[z], R_tab[s]), W=(R_tb[z],), eng="pool")
                    P.tt(hrb[z], ta[z], tbb[z], ALU.subtract, R=(R_ta[z], R_tb[z]), W=(R_hb[z],), eng="pool")
                    yield
                    P.tt(v3(ta[z], 4), v3(gi[z], 4), Cb, ALU.mult, R=(R_g[z], R_tab[s]), W=(R_ta[z],), eng="pool")
                    P.tt(v3(tbb[z], 4), v3(gr[z], 4), Sb_, ALU.mult, R=(R_g[z], R_tab[s]), W=(R_tb[z],), eng="pool")
                    P.tt(hib[z], ta[z], tbb[z], ALU.add, R=(R_ta[z], R_tb[z]), W=(R_hb[z],), eng="pool")
                for qp in range(0, 4, 2):
                    alive = [s5pair(qp + i_, i_) for i_ in range(2)]
                    while alive:
                        for g_ in list(alive):
                            try:
                                next(g_)
                            except StopIteration:
                                alive.remove(g_)
                    for qq in (qp, qp + 1):
                        z = qq - qp
                        P.mm(banks[bY][:, :], cbuf[s][:, qq, 2, :], hrb[z], start=(qq == 0), stop=False, R=(R_cb[s], R_hb[z]), W=PR[bY])
                        P.mm(banks[bY][:, :], cbuf[s][:, qq, 3, :], hib[z], start=False, stop=(qq == 3), R=(R_cb[s], R_hb[z]), W=PR[bY])

                P.stt(yd, u_f[s], s5dv[:, c, j:j + 1], banks[bY][:, :], ALU.mult, ALU.add, R=[R_u[s], R_c] + PR[bY], W=(R_yd,))
                P.act(x2, yd, AF.Square, R=(R_yd,), W=(R_x2,))
                P.ts(x2, x2, 0.044715, 1.0, ALU.mult, ALU.add, R=(R_x2,), W=(R_x2,))
                P.tt(x2, x2, yd, ALU.mult, R=(R_x2, R_yd), W=(R_x2,))
                P.act(sg, x2, AF.Sigmoid, scale=2.0 * math.sqrt(2.0 / math.pi), R=(R_x2,), W=(R_sg,))
                P.tt(zg_f[:, c, :], yd, sg, ALU.mult, R=(R_yd, R_sg), W=(R_zf[c],))
                P.copy(zg_b[:, c, :], zg_f[:, c, :], R=(R_zf[c],), W=(R_zb[c],), eng="pool")
            wv, wr = load_w(W2d, 8, 1024, 512)
            for h in range(4):
                b = 2 + (h % 2)
                dense(wv, wr, h * 128, 8, xb_fn, xb_res, b)
                P.copy(xq_b[:, h, :], banks[b][:, :], R=PR[b], W=(R_xq,), eng="act")
            wv, wr = load_w(dr["s5_w_glu"][j], 8, 0, 1024)
            for oc in range(NCH):
                b = 4 + (oc % 4)
                dense(wv, wr, oc * 128, 8, lambda k: zg_b[:, k, :], lambda k: (R_zb[k],), b)
                P.act(sg, banks[b][:, :], AF.Sigmoid, bias=s5dv[:, oc, 2 + j:3 + j], R=PR[b] + [R_c], W=(R_sg,))
                P.tt(mix_b[:, oc, :], zg_f[:, oc, :], sg, ALU.mult, R=(R_zf[oc], R_sg), W=(R_mix[oc],))

        def gdn_layer(l, j, mix_b, R_mix, xq_b, R_xq):
            W2d = dr["gdn_w_in"][j]
            qkv_b = AR.alloc([128, 24, T], BF16); R_qkv = [Res(f"qkv{i}") for i in range(24)]
            zs = AR.alloc([128, 8, T], BF16); R_zs = [Res(f"zs{i}") for i in range(8)]
            oT = AR.alloc([128, 8, T], F32); R_oT = [Res(f"oT{i}") for i in range(8)]
            stage = [AR.alloc([128, T + 4], F32) for _ in range(2)]; R_stg = [Res("stg0"), Res("stg1")]
            acc = [AR.alloc([128, T], F32) for _ in range(2)]; R_acc = [Res("acc0"), Res("acc1")]
            sl = [AR.alloc([128, T], F32) for _ in range(2)]; R_sl = [Res("sl0"), Res("sl1")]
            sq = AR.alloc([128, T], F32); R_sq = Res("gsq")
            rs = AR.alloc([128, T], F32); R_rs = Res("grs")
            for grp in range(3):
                wv, wr = load_w(W2d, 8, grp * 1024, 1024)
                for oc in range(8):
                    g = grp * 8 + oc
                    s = g % 2
                    b = 4 + (g % 4)
                    dense(wv, wr, oc * 128, 8, xb_fn, xb_res, b)
                    P.copy(stage[s][:, 0:3], convst[:, j, g, 0:3], R=(R_cst[j][g],), W=(R_stg[s],), eng="pool")
                    P.copy(stage[s][:, 3:3 + T], banks[b][:, :], R=PR[b], W=(R_stg[s],), eng="act")
                    P.ts(acc[s], stage[s][:, 0:T], convw[:, g, j * 4:j * 4 + 1], None, ALU.mult, R=(R_stg[s], R_c), W=(R_acc[s],))
                    for k in range(1, 4):
                        P.stt(acc[s], stage[s][:, k:k + T], convw[:, g, j * 4 + k:j * 4 + k + 1], acc[s], ALU.mult, ALU.add,
                              R=(R_stg[s], R_c, R_acc[s]), W=(R_acc[s],))
                    P.copy(convst[:, j, g, 0:3], stage[s][:, T:T + 3], R=(R_stg[s],), W=(R_cst[j][g],), eng="pool")
                    if g >= 16:
                        P.act(qkv_b[:, g, :], acc[s], AF.Silu, R=(R_acc[s],), W=(R_qkv[g],))
                    else:
                        P.act(sl[s], acc[s], AF.Silu, R=(R_acc[s],), W=(R_sl[s],))
                        P.act(sq, sl[s], AF.Square, R=(R_sl[s],), W=(R_sq,))
                        bq = g % 2
                        P.mm(banks[bq][:, :], ones_f, sq, R=(R_c, R_sq), W=PR[bq])
                        P.act(rs, banks[bq][:, :], AF.Sqrt, bias=ccol[:, 3:4], R=PR[bq] + [R_c], W=(R_rs,))
                        P.recip(rs, rs, R=(R_rs,), W=(R_rs,))
                        P.stt(qkv_b[:, g, :], sl[s], (128.0 ** -0.5 if g < 8 else 1.0), rs, ALU.mult, ALU.mult,
                              R=(R_sl[s], R_rs), W=(R_qkv[g],))
            wv, wr = load_w(W2d, 8, 3072, 1024)
            for oc in range(8):
                b = 4 + (oc % 4)
                dense(wv, wr, oc * 128, 8, xb_fn, xb_res, b)
                P.act(zs[:, oc, :], banks[b][:, :], AF.Silu, R=PR[b], W=(R_zs[oc],))
            wv, wr = load_w(W2d, 8, 4112, 512)
            for h in range(4):
                b = 4 + (h % 4)
                dense(wv, wr, h * 128, 8, xb_fn, xb_res, b)
                P.copy(xq_b[:, h, :], banks[b][:, :], R=PR[b], W=(R_xq,), eng="act")
            P.dma(wsmall, W2d.rearrange("(k p) n -> p k n", p=128)[:, :, 4096:4112], W=(R_wsmall,), q="pool")
            for tb in range(NB):
                for k in range(8):
                    P.mm(banks[2][:, tb * 16:(tb + 1) * 16], xb[:, k, tb * 128:(tb + 1) * 128], wsmall[:, k, :],
                         start=(k == 0), stop=(k == 7), R=(R_xb[k], R_wsmall), W=(PR[2][0],))
            sm = AR.alloc([128, 12, NB, 8], F32)
            R_sm = Res("gsm")
            bet, lnb, apre, spv, gg, gc, ngc, gcb, be, kd, egl = [sm[:, i, :, :] for i in range(11)]
            ba = banks[2][:, 0:NB * 16].rearrange("p (t k h) -> p t k h", t=NB, k=2)
            RWs = dict(R=(R_sm, R_c), W=(R_sm,))
            P.act(bet, ba[:, :, 0, :], AF.Sigmoid, R=(PR[2][0],), W=(R_sm,))
            P.act(lnb, bet, AF.Ln, **RWs)
            P.tt(apre, ba[:, :, 1, :], gpar[:, 16 + j * 8:24 + j * 8].unsqueeze(1).broadcast_to([128, NB, 8]), ALU.add, R=(PR[2][0], R_c), W=(R_sm,))
            P.act(spv, apre, AF.Exp, **RWs)
            P.act(spv, spv, AF.Ln, bias=ccol[:, 1:2], **RWs)
            P.tt(gg, spv, gpar[:, j * 8:j * 8 + 8].unsqueeze(1).broadcast_to([128, NB, 8]), ALU.mult, **RWs)
            for tb in range(NB):
                P.mm(banks[3][:, tb * 8:(tb + 1) * 8], triu_f, gg[:, tb, :], R=(R_c, R_sm), W=(PR[3][0],))
                P.mm(banks[3][:, 32 + tb * 8:32 + (tb + 1) * 8], trils_f, gg[:, tb, :], R=(R_c, R_sm), W=(PR[3][0],))
                P.mm(banks[3][:, 64 + tb * 8:64 + (tb + 1) * 8], ones_f, gg[:, tb, :], R=(R_c, R_sm), W=(PR[3][0],))
            p3 = lambda o: banks[3][:, o:o + NB * 8].rearrange("p (t h) -> p t h", t=NB)
            P.copy(gc, p3(0), R=(PR[3][0],), W=(R_sm,), eng="act")
            P.ts(ngc, p3(0), -1.0, None, ALU.mult, R=(PR[3][0],), W=(R_sm,))
            P.tt(gcb, gc, lnb, ALU.add, **RWs)
            P.act(be, gcb, AF.Exp, **RWs)
            P.act(kd, p3(32), AF.Exp, R=(PR[3][0],), W=(R_sm,))
            P.act(egl, p3(64), AF.Exp, R=(PR[3][0],), W=(R_sm,))
            NSL = gdn_slots
            def mk(dt):
                return [AR.alloc([128, 128], dt) for _ in range(NSL)]
            Kbe, Kd_, Vb, attnT, qs, TT, nWmT, vn = [mk(BF16) for _ in range(8)]
            ndg, bdg, DA, DB, EG, A0, A1, B0, B1, Q0, Q1 = [mk(F32) for _ in range(11)]
            names = ["Kbe", "Kd", "Vb", "attnT", "qs", "TT", "nWmT", "vn", "ndg", "bdg", "DA", "DB", "EG", "A0", "A1", "B0", "B1", "Q0", "Q1"]
            RT = [{n: Res(f"{n}{z}") for n in names} for z in range(NSL)]
            it = 0
            def g5(tb, h, z):
                tsl = slice(tb * 128, (tb + 1) * 128)
                r = RT[z]
                Y0, Y1 = 2 * z, 2 * z + 1
                qT, kT, vT = qkv_b[:, h, tsl], qkv_b[:, 8 + h, tsl], qkv_b[:, 16 + h, tsl]
                Rq, Rk, Rv = R_qkv[h], R_qkv[8 + h], R_qkv[16 + h]
                col = lambda t_: t_[:, tb, h:h + 1]
                y0b = bfview(Y0)
                reg = lambda b_, q_: banks[b_][:, q_ * 128:(q_ + 1) * 128]
                P.tr(y0b[:, 0:128], kT, ident_b, R=(Rk, R_c), W=(PR[Y0][0],))
                P.tr(y0b[:, 128:256], vT, ident_b, R=(Rv, R_c), W=(PR[Y0][0],))
                P.act(Kbe[z], y0b[:, 0:128], AF.Copy, scale=col(be), R=(PR[Y0][0], R_sm), W=(r["Kbe"],))
                P.act(Kd_[z], y0b[:, 0:128], AF.Copy, scale=col(kd), R=(PR[Y0][0], R_sm), W=(r["Kd"],))
                P.act(Vb[z], y0b[:, 128:256], AF.Copy, scale=col(bet), R=(PR[Y0][0], R_sm), W=(r["Vb"],))
                yield
                P.mm(reg(Y0, 1), kT, kT, R=(Rk,), W=(PR[Y0][1],))
                P.mm(reg(Y0, 2), kT, qT, R=(Rk, Rq), W=(PR[Y0][2],))
                yield
                P.ts(ndg[z], ident_f, col(ngc), None, ALU.mult, R=(R_c, R_sm), W=(r["ndg"],))
                P.ts(bdg[z], ident_f, col(gcb), None, ALU.mult, R=(R_c, R_sm), W=(r["bdg"],))
                P.mm(reg(Y1, 0), ones_f, ndg[z], start=True, stop=False, R=(R_c, r["ndg"]), W=(PR[Y1][0],))
                P.mm(reg(Y1, 0), ident_f, mls_f, start=False, stop=True, R=(R_c,), W=(PR[Y1][0],))
                P.mm(reg(Y1, 1), ones_f, bdg[z], start=True, stop=False, R=(R_c, r["bdg"]), W=(PR[Y1][1],))
                P.mm(reg(Y1, 1), ident_f, mus_f, start=False, stop=True, R=(R_c,), W=(PR[Y1][1],))
                P.mm(reg(Y1, 2), ones_f, ndg[z], start=True, stop=False, R=(R_c, r["ndg"]), W=(PR[Y1][2],))
                P.mm(reg(Y1, 2), ident_f, mui_f, start=False, stop=True, R=(R_c,), W=(PR[Y1][2],))
                P.mm(reg(Y0, 3), ones_f, ndg[z], R=(R_c, r["ndg"]), W=(PR[Y0][3],))
                yield
                P.act(DA[z], reg(Y1, 0), AF.Exp, bias=col(gcb), R=(PR[Y1][0], R_sm), W=(r["DA"],))
                P.act(DB[z], reg(Y1, 1), AF.Exp, bias=col(ngc), R=(PR[Y1][1], R_sm), W=(r["DB"],))
                P.act(EG[z], reg(Y0, 3), AF.Exp, scale=-1.0, R=(PR[Y0][3],), W=(r["EG"],))
                yield
                P.tt(A0[z], reg(Y0, 1), DA[z], ALU.mult, R=(PR[Y0][1], r["DA"]), W=(r["A0"],))
                P.tt(B0[z], reg(Y0, 1), DB[z], ALU.mult, R=(PR[Y0][1], r["DB"]), W=(r["B0"],))
                P.tt(Q0[z], ident_f, B0[z], ALU.subtract, R=(R_c, r["B0"]), W=(r["Q0"],))
                yield
                P.act(DA[z], reg(Y1, 2), AF.Exp, bias=col(ngc), scale=-1.0, R=(PR[Y1][2], R_sm, r["A0"]), W=(r["DA"],))
                P.tt(DB[z], reg(Y0, 2), DA[z], ALU.mult, R=(PR[Y0][2], r["DA"], r["B0"]), W=(r["DB"],))
                P.copy(attnT[z], DB[z], R=(r["DB"],), W=(r["attnT"],), eng="pool")
                P.tt(qs[z], qT, EG[z], ALU.mult, R=(Rq, r["EG"]), W=(r["qs"],), eng="pool")
                yield
                Ac, Bc, Qc = (A0, A1), (B0, B1), (Q0, Q1)
                An, Bn, Qn = ("A0", "A1"), ("B0", "B1"), ("Q0", "Q1")
                for lev in range(6):
                    ci, ni = lev % 2, (lev + 1) % 2
                    P.mm(reg(Y1, 0), Bc[ci][z], Ac[ci][z], R=(r[Bn[ci]], r[An[ci]]), W=(PR[Y1][0],))
                    if lev < 5:
                        P.mm(reg(Y1, 1), Ac[ci][z], Bc[ci][z], R=(r[Bn[ci]], r[An[ci]]), W=(PR[Y1][1],))
                    P.copy(Ac[ni][z], reg(Y1, 0), R=(PR[Y1][0],), W=(r[An[ni]],), eng="act")
                    if lev < 5:
                        P.copy(Bc[ni][z], reg(Y1, 1), R=(PR[Y1][1],), W=(r[Bn[ni]],), eng="dve")
                    P.mm(reg(Y1, 2), Ac[ni][z], Qc[ci][z], R=(r[An[ni]], r[Qn[ci]]), W=(PR[Y1][2],))
                    P.tt(Qc[ni][z], Qc[ci][z], reg(Y1, 2), ALU.add, R=(r[Qn[ci]], PR[Y1][2]), W=(r[Qn[ni]],))
                    yield
                    if lev == 5:
                        P.copy(TT[z], Qc[ni][z], R=(r[Qn[ni]],), W=(r["TT"],), eng="pool")
                P.mm(reg(Y0, 0), Kbe[z], TT[z], R=(r["Kbe"], r["TT"]), W=(PR[Y0][0],))
                P.act(nWmT[z], reg(Y0, 0), AF.Copy, scale=-1.0, R=(PR[Y0][0],), W=(r["nWmT"],))
                yield
                P.mm(reg(Y0, 1), TT[z], Vb[z], start=True, stop=False, R=(r["TT"], r["Vb"]), W=(PR[Y0][1],))
                P.mm(reg(Y0, 1), nWmT[z], Sbf[:, j, h, :], start=False, stop=True, R=(r["nWmT"], R_Sb[j][h]), W=(PR[Y0][1],))
                P.copy(vn[z], reg(Y0, 1), R=(PR[Y0][1],), W=(r["vn"],), eng="act")
                yield
                P.mm(reg(Y0, 3), Sbf[:, j, h, :], qs[z], start=True, stop=False, R=(R_Sb[j][h], r["qs"]), W=(PR[Y0][3],))
                P.mm(reg(Y0, 3), vn[z], attnT[z], start=False, stop=True, R=(r["vn"], r["attnT"]), W=(PR[Y0][3],))
                P.copy(oT[:, h, tsl], reg(Y0, 3), R=(PR[Y0][3],), W=(R_oT[h],), eng="act")
                yield
                P.mm(reg(Y0, 2), Kd_[z], vn[z], R=(r["Kd"], r["vn"]), W=(PR[Y0][2],))
                P.stt(Sst[:, j, h, :], Sst[:, j, h, :], col(egl), reg(Y0, 2), ALU.mult, ALU.add,
                      R=(R_S[j][h], R_sm, PR[Y0][2]), W=(R_S[j][h],))
                P.copy(Sbf[:, j, h, :], Sst[:, j, h, :], R=(R_S[j][h],), W=(R_Sb[j][h],), eng="pool")
            for tb in range(NB):
                for hp in range(0, 8, NSL):
                    alive = [g5(tb, hp + i_, i_) for i_ in range(NSL)]
                    while alive:
                        for g_ in list(alive):
                            try:
                                next(g_)
                            except StopIteration:
                                alive.remove(g_)
            for h in range(8):
                b = 4 + (h % 4)
                P.act(sq, oT[:, h, :], AF.Square, R=(R_oT[h],), W=(R_sq,))
                P.mm(banks[b][:, :], ones_f, sq, R=(R_c, R_sq), W=PR[b])
                P.act(rs, banks[b][:, :], AF.Sqrt, bias=ccol[:, 4:5], R=PR[b] + [R_c], W=(R_rs,))
                P.recip(rs, rs, R=(R_rs,), W=(R_rs,))
                P.tt(rs, rs, oT[:, h, :], ALU.mult, R=(R_rs, R_oT[h]), W=(R_rs,))
                P.stt(mix_b[:, h, :], rs, normg[:, j:j + 1], zs[:, h, :], ALU.mult, ALU.mult, R=(R_rs, R_c, R_zs[h]), W=(R_mix[h],))

        TMARK = UMARK
        for ti in range(NT):
            t0 = ti * T
            xin = AR.alloc([128, NB, D], F32)
            R_xin = Res("xin")
            P.dma(xin, dr["x"][t0:t0 + T, :].rearrange("(b p) f -> p b f", p=128), W=(R_xin,))
            for c in range(NCH):
                b = next_bank()
                for tb in range(NB):
                    P.tr(banks[b][:, tb * 128:(tb + 1) * 128], xin[:, tb, c * 128:(c + 1) * 128], ident_f,
                         R=(R_xin, R_c), W=(PR[b][tb],))
                P.copy(xT[:, c, :], banks[b][:, :], R=PR[b], W=(R_xT[c],), eng="act")
                P.copy(xb[:, c, :], banks[b][:, :], R=PR[b], W=(R_xb[c],), eng="act")
            P.barrier()
            AR.release(TMARK)

            for l in layers:
                j = l // 2
                mix_b = AR.alloc([128, 12, T], BF16)
                R_mix = [Res(f"mix{i}") for i in range(12)]
                xq_b = AR.alloc([128, 4, T], BF16)
                R_xq = Res("xq")
                if not mixers:
                    for h in range(8):
                        P.memset(mix_b[:, h, :], 0.0, W=(R_mix[h],), eng="pool")
                    if xattn:
                        W2d = dr["gdn_w_in"][j] if l % 2 == 0 else dr["s5_w_in"][j]
                        wv, wr = load_w(W2d, 8, 4112 if l % 2 == 0 else 1024, 512)
                        for h in range(4):
                            b = next_bank()
                            dense(wv, wr, h * 128, 8, xb_fn, xb_res, b)
                            P.copy(xq_b[:, h, :], banks[b][:, :], R=PR[b], W=(R_xq,), eng="act")
                elif l % 2 == 0:
                    gdn_layer(l, j, mix_b, R_mix, xq_b, R_xq)
                else:
                    s5_layer(l, j, mix_b, R_mix, xq_b, R_xq)
                if xattn:
                    cross_attention(l, xq_b, R_xq, mix_b, R_mix)
                else:
                    for h in range(4):
                        P.memset(mix_b[:, 8 + h, :], 0.0, W=(R_mix[8 + h],), eng="pool")
                out_proj_ln_mlp(l, mix_b, R_mix)

            xo = AR.alloc([128, NB, D], F32)
            R_xo = Res("xo")
            for tb in range(NB):
                for half in range(2):
                    b = next_bank()
                    for cc in range(4):
                        c = half * 4 + cc
                        P.tr(banks[b][:, cc * 128:(cc + 1) * 128], xT[:, c, tb * 128:(tb + 1) * 128], ident_f,
                             R=(R_xT[c], R_c), W=(PR[b][cc],))
                    P.copy(xo[:, tb, half * 512:(half + 1) * 512], banks[b][:, :], R=PR[b], W=(R_xo,),
                           eng=("act" if half == 0 else "dve"))
            P.dma(y[t0:t0 + T, :].rearrange("(b p) f -> p b f", p=128), xo, R=(R_xo,), W=())
            P.barrier()
            AR.release(TMARK)

        block = es.enter_context(nc.Block())
        P.emit(block)
        print("arena peak KB", AR.peak / 1024, "ops", {e: len(P.ops[e]) for e in ENGS}, flush=True)
    return nc


_CACHE = {}


def run_cores(inputs, L, nb, ncores=8, **kw):
    key = (L, tuple(sorted(kw.items())))
    nc = build_program(L, **kw)
    cs = host_consts()
    shared = {nm: np.ascontiguousarray(np.asarray(inputs[nm], dtype=np.float32)) for nm, _ in W_NAMES}
    shared.update(cs)
    in_maps = []
    for c in range(ncores):
        b = c % nb
        m = dict(shared)
        m["x"] = np.ascontiguousarray(np.asarray(inputs["x"][b, :L], dtype=np.float32))
        m["mem"] = np.ascontiguousarray(np.asarray(inputs["mem"][b], dtype=np.float32))
        in_maps.append(m)
    res = run_bass_kernel_spmd(nc, in_maps, core_ids=list(range(ncores)))
    return np.stack([res.results[b]["y"] for b in range(nb)], axis=0)


def kernel(**inputs):
    out = run_cores(inputs, 8192, 4, ncores=4)
    return out.astype(np.float32)
```
